# Optimizing a Trainium2 kernel written in Bass

```python
import math
import jax
import jax.numpy as jnp
from jax import lax
import numpy as np

D_MODEL = 1024
BATCH = 8
SEQ = 4096
DEPTH = 1

CTX_LEN = 256
GRID_W = 64

A_HEADS = 4
A_DK = 128
A_DV = 128
A_CHUNK = 64
CONV_K = 5
B_HEADS = 4
B_DK = 128
B_DV = 128
B_CHUNK = 16
N_BRANCH = 2

A_QK = A_HEADS * A_DK
A_V = A_HEADS * A_DV
B_QK = B_HEADS * B_DK
B_V = B_HEADS * B_DV
CONV_DIM = 2 * A_QK + A_V
IN_SPLITS = (A_QK, A_QK, A_V, 2 * A_HEADS, 2 * A_HEADS, A_V,
             B_QK, B_QK, B_QK, B_V, B_V, N_BRANCH * D_MODEL)
D_IN = 2 * A_QK + 2 * A_V + 4 * A_HEADS + 3 * B_QK + 2 * B_V + N_BRANCH * D_MODEL

DEEPNORM_ALPHA = (2 * DEPTH) ** 0.25
DEEPNORM_BETA = (8 * DEPTH) ** -0.25
LN_EPS = 1e-6
RMS_EPS = 1e-6
L2_EPS = 1e-6

kernel_name = 'hybrid_gdn_hgrn2_dit_block'


def layer_norm(t):
    t = t.astype(jnp.float32)
    mu = jnp.mean(t, axis=-1, keepdims=True)
    var = jnp.mean(jnp.square(t - mu), axis=-1, keepdims=True)
    return (t - mu) * lax.rsqrt(var + LN_EPS)


def l2norm(t):
    return t * lax.rsqrt(jnp.sum(t * t, axis=-1, keepdims=True) + L2_EPS)


def rms_norm_gated(o, gain, gate):
    o = o * lax.rsqrt(jnp.mean(o * o, axis=-1, keepdims=True) + RMS_EPS) * gain
    return o.reshape(*o.shape[:-2], -1) * jax.nn.silu(gate)


def split_cols(z):
    out, start = [], 0
    for n in IN_SPLITS:
        out.append(z[..., start:start + n])
        start += n
    return out


def centred_dwconv(u, w):
    u = u.astype(jnp.float32)
    return lax.conv_general_dilated(
        u, w.astype(jnp.float32)[:, None, :], window_strides=(1,),
        padding=((CONV_K // 2, CONV_K // 2),),
        dimension_numbers=('NWC', 'WIO', 'NWC'), feature_group_count=u.shape[-1])


def to_colmajor(t, rows):
    bsz, rest = t.shape[0], t.shape[2:]
    return t.reshape(bsz, rows, GRID_W, *rest).swapaxes(1, 2).reshape(bsz, rows * GRID_W, *rest)


def from_colmajor(t, rows):
    bsz, rest = t.shape[0], t.shape[2:]
    return t.reshape(bsz, GRID_W, rows, *rest).swapaxes(1, 2).reshape(bsz, rows * GRID_W, *rest)


def _to_chunks(t, chunk):
    bsz, length, heads = t.shape[:3]
    t = t.reshape(bsz, length // chunk, chunk, heads, *t.shape[3:])
    return jnp.moveaxis(t, 3, 1)


def _from_chunks(o):
    n, bsz, heads, chunk, dv = o.shape
    return o.transpose(1, 0, 3, 2, 4).reshape(bsz, n * chunk, heads, dv)


def gdn_chunk_scan(q, k, v, g, beta, s0):
    dk = q.shape[-1]
    q, k, v = (_to_chunks(t, A_CHUNK) for t in (q, k, v))
    g, beta = _to_chunks(g, A_CHUNK), _to_chunks(beta, A_CHUNK)
    G = jnp.cumsum(g, axis=-1)
    idx = jnp.arange(A_CHUNK)
    incl = idx[:, None] >= idx[None, :]
    strict = idx[:, None] > idx[None, :]
    diff = G[..., :, None] - G[..., None, :]
    decay = jnp.where(incl, jnp.exp(jnp.where(incl, diff, 0.0)), 0.0)
    kb = k * beta[..., None]
    a_mat = jnp.where(strict, jnp.einsum('bhncd,bhnsd->bhncs', kb, k) * decay, 0.0)
    eye = jnp.eye(A_CHUNK, dtype=a_mat.dtype)
    rhs = jnp.concatenate([kb * jnp.exp(G)[..., None], v * beta[..., None]], axis=-1)
    sol = lax.linalg.triangular_solve(a_mat + eye, rhs, left_side=True, lower=True,
                                      unit_diagonal=True)
    w, u = sol[..., :dk], sol[..., dk:]
    attn = jnp.einsum('bhncd,bhnsd->bhncs', q, k) * decay
    qg = q * jnp.exp(G)[..., None]
    g_last = G[..., -1:]
    kd = k * jnp.exp(g_last - G)[..., None]
    xs = tuple(jnp.moveaxis(t, 2, 0) for t in (w, u, qg, attn, kd, jnp.exp(g_last[..., 0])))

    def step(S, inp):
        w_c, u_c, qg_c, attn_c, kd_c, gl_c = inp
        v_new = u_c - jnp.einsum('bhcd,bhdv->bhcv', w_c, S)
        o_c = jnp.einsum('bhcd,bhdv->bhcv', qg_c, S) + jnp.einsum('bhcs,bhsv->bhcv', attn_c, v_new)
        S = S * gl_c[..., None, None] + jnp.einsum('bhcd,bhcv->bhdv', kd_c, v_new)
        return S, o_c

    s_final, o = lax.scan(step, s0, xs)
    return _from_chunks(o), s_final


def gla_chunk_scan(q, k, v, g, s0):
    q, k, v, g = (_to_chunks(t, B_CHUNK) for t in (q, k, v, g))
    G = jnp.cumsum(g, axis=-2)
    idx = jnp.arange(B_CHUNK)
    incl = idx[:, None] >= idx[None, :]
    qg = q * jnp.exp(G)
    kg = k * jnp.exp(-G)
    attn = jnp.where(incl, jnp.einsum('bhncd,bhnsd->bhncs', qg, kg), 0.0)
    g_last = G[..., -1:, :]
    kd = k * jnp.exp(g_last - G)
    xs = tuple(jnp.moveaxis(t, 2, 0) for t in (qg, attn, v, kd, jnp.exp(g_last[..., 0, :])))

    def step(S, inp):
        qg_c, attn_c, v_c, kd_c, gl_c = inp
        o_c = jnp.einsum('bhcd,bhdv->bhcv', qg_c, S) + jnp.einsum('bhcs,bhsv->bhcv', attn_c, v_c)
        S = S * gl_c[..., :, None] + jnp.einsum('bhcd,bhcv->bhdv', kd_c, v_c)
        return S, o_c

    s_final, o = lax.scan(step, s0, xs)
    return _from_chunks(o), s_final


def prefix_scan(scan_fn, ctx_args, lat_args, s0, reverse):
    flip = (lambda t: jnp.flip(t, axis=1)) if reverse else (lambda t: t)
    o_ctx, s_ctx = scan_fn(*[flip(t) for t in ctx_args], s0)
    o_lat, _ = scan_fn(*[flip(t) for t in lat_args], s_ctx)
    return flip(o_ctx), flip(o_lat)


def project_stream(u, w_in_l, conv_w_l, a_log_l, dt_bias_l, lb_l):
    bsz, length, _ = u.shape
    (a_q, a_k, a_v, a_alpha, a_beta, a_gate,
     b_q, b_f_fwd, b_f_bwd, b_i, b_gate, merge) = split_cols(u @ w_in_l)
    qkv = jax.nn.silu(centred_dwconv(jnp.concatenate([a_q, a_k, a_v], axis=-1), conv_w_l))
    a_q = l2norm(qkv[..., :A_QK].reshape(bsz, length, A_HEADS, A_DK)) * A_DK ** -0.5
    a_k = l2norm(qkv[..., A_QK:2 * A_QK].reshape(bsz, length, A_HEADS, A_DK))
    a_v = qkv[..., 2 * A_QK:].reshape(bsz, length, A_HEADS, A_DV)
    a_g = -jnp.exp(a_log_l) * jax.nn.softplus(a_alpha.reshape(bsz, length, 2, A_HEADS) + dt_bias_l)
    a_b = jax.nn.sigmoid(a_beta.reshape(bsz, length, 2, A_HEADS))
    f_logit = jnp.stack([b_f_fwd, b_f_bwd], axis=2)
    f = lb_l + (1.0 - lb_l) * jax.nn.sigmoid(f_logit)
    b_g = jnp.log(f).reshape(bsz, length, 2, B_HEADS, B_DK)
    b_k = ((1.0 - lb_l) * jax.nn.sigmoid(-f_logit)).reshape(bsz, length, 2, B_HEADS, B_DK)
    b_q = jax.nn.silu(b_q).reshape(bsz, length, B_HEADS, B_DK) * B_DK ** -0.5
    b_i = b_i.reshape(bsz, length, B_HEADS, B_DV)
    return {'a_q': a_q, 'a_k': a_k, 'a_v': a_v, 'a_g': a_g, 'a_b': a_b, 'a_gate': a_gate,
            'b_q': b_q, 'b_k': b_k, 'b_g': b_g, 'b_i': b_i, 'b_gate': b_gate, 'merge': merge}


def merge_branches(o_a, o_b, p, a_norm_l, b_norm_l, w_a_out_l, w_b_out_l, w_out_l):
    y_a = rms_norm_gated(o_a, a_norm_l, p['a_gate']) @ w_a_out_l
    y_b = rms_norm_gated(o_b, b_norm_l, p['b_gate']) @ w_b_out_l
    gates = jax.nn.sigmoid(p['merge'].reshape(*p['merge'].shape[:-1], N_BRANCH, D_MODEL))
    return (gates[..., 0, :] * y_a + gates[..., 1, :] * y_b) @ w_out_l


def setup_inputs(seed: int = 0) -> dict:
    key = jax.random.key(seed)
    ks = jax.random.split(key, 20)
    f32 = jnp.float32

    def nrm(k, shape, scale):
        return jax.random.normal(k, shape, f32) * scale

    x = nrm(ks[0], (BATCH, SEQ, D_MODEL), 1.0)
    c = nrm(ks[1], (BATCH, D_MODEL), 1.0)
    ctx = nrm(ks[2], (BATCH, CTX_LEN, D_MODEL), 1.0)
    c_ctx = nrm(ks[3], (D_MODEL,), 1.0)
    w_mod = nrm(ks[4], (DEPTH, D_MODEL, 3 * D_MODEL), D_MODEL ** -0.5)
    b_mod = nrm(ks[5], (DEPTH, 3 * D_MODEL), 0.02)
    w_in = nrm(ks[6], (DEPTH, D_MODEL, D_IN), D_MODEL ** -0.5)
    conv_w = nrm(ks[7], (DEPTH, CONV_K, CONV_DIM), CONV_K ** -0.5)
    a_log = jnp.log(jax.random.uniform(ks[8], (DEPTH, 2, A_HEADS), f32, 1.0, 16.0))
    dt = jnp.exp(jax.random.uniform(ks[9], (DEPTH, 2, A_HEADS), f32, math.log(1e-3), math.log(1e-1)))
    dt_bias = dt + jnp.log(-jnp.expm1(-dt))
    lb_param = nrm(ks[10], (DEPTH + 1, 2, B_QK), 0.1)
    a_norm_g = 1.0 + nrm(ks[11], (DEPTH, A_DV), 0.02)
    b_norm_g = 1.0 + nrm(ks[12], (DEPTH, B_DV), 0.02)
    w_a_out = nrm(ks[13], (DEPTH, A_V, D_MODEL), DEEPNORM_BETA * A_V ** -0.5)
    w_b_out = nrm(ks[14], (DEPTH, B_V, D_MODEL), DEEPNORM_BETA * B_V ** -0.5)
    w_out = nrm(ks[15], (DEPTH, D_MODEL, D_MODEL), DEEPNORM_BETA * D_MODEL ** -0.5)
    ln_g = 1.0 + nrm(ks[16], (DEPTH, D_MODEL), 0.02)
    ln_b = nrm(ks[17], (DEPTH, D_MODEL), 0.02)
    return {'x': x, 'c': c, 'ctx': ctx, 'c_ctx': c_ctx, 'w_mod': w_mod, 'b_mod': b_mod,
            'w_in': w_in, 'conv_w': conv_w, 'a_log': a_log, 'dt_bias': dt_bias,
            'lb_param': lb_param, 'a_norm_g': a_norm_g, 'b_norm_g': b_norm_g,
            'w_a_out': w_a_out, 'w_b_out': w_b_out, 'w_out': w_out, 'ln_g': ln_g, 'ln_b': ln_b}


def reference(x, c, ctx, c_ctx, w_mod, b_mod, w_in, conv_w, a_log, dt_bias, lb_param,
              a_norm_g, b_norm_g, w_a_out, w_b_out, w_out, ln_g, ln_b):
    f32 = jnp.float32
    bsz = x.shape[0]
    rows = x.shape[1] // GRID_W
    lb_all = jnp.cumsum(jax.nn.softmax(lb_param.astype(f32), axis=0), axis=0)
    h_lat = x.astype(f32)
    h_ctx = ctx.astype(f32)
    s0_a = jnp.zeros((bsz, A_HEADS, A_DK, A_DV), f32)
    s0_b = jnp.zeros((bsz, B_HEADS, B_DK, B_DV), f32)
    cm = lambda t: to_colmajor(t, rows)
    for l in range(DEPTH):
        shift_l, scale_l, gate_l = jnp.split((jax.nn.silu(c) @ w_mod[l] + b_mod[l])[:, None, :], 3, axis=-1)
        shift_c, scale_c, gate_c = jnp.split(jax.nn.silu(c_ctx) @ w_mod[l] + b_mod[l], 3, axis=-1)
        u_lat = layer_norm(h_lat) * (1.0 + scale_l) + shift_l
        u_ctx = layer_norm(h_ctx) * (1.0 + scale_c) + shift_c
        pl = project_stream(u_lat, w_in[l], conv_w[l], a_log[l], dt_bias[l], lb_all[l])
        pc = project_stream(u_ctx, w_in[l], conv_w[l], a_log[l], dt_bias[l], lb_all[l])

        oa_ctx, oa_lat = 0.0, 0.0
        for d in range(2):
            o_c, o_l = prefix_scan(
                gdn_chunk_scan,
                (pc['a_q'], pc['a_k'], pc['a_v'], pc['a_g'][:, :, d], pc['a_b'][:, :, d]),
                (pl['a_q'], pl['a_k'], pl['a_v'], pl['a_g'][:, :, d], pl['a_b'][:, :, d]),
                s0_a, reverse=(d == 1))
            oa_ctx = oa_ctx + o_c
            oa_lat = oa_lat + o_l

        ob_ctx, ob_lat_cm = 0.0, 0.0
        for d in range(2):
            o_c, o_l = prefix_scan(
                gla_chunk_scan,
                (pc['b_q'], pc['b_k'][:, :, d], pc['b_i'], pc['b_g'][:, :, d]),
                (cm(pl['b_q']), cm(pl['b_k'][:, :, d]), cm(pl['b_i']), cm(pl['b_g'][:, :, d])),
                s0_b, reverse=(d == 1))
            ob_ctx = ob_ctx + o_c
            ob_lat_cm = ob_lat_cm + o_l
        ob_lat = from_colmajor(ob_lat_cm, rows)

        sub_lat = merge_branches(oa_lat, ob_lat, pl, a_norm_g[l], b_norm_g[l],
                                 w_a_out[l], w_b_out[l], w_out[l])
        if l < DEPTH - 1:
            sub_ctx = merge_branches(oa_ctx, ob_ctx, pc, a_norm_g[l], b_norm_g[l],
                                     w_a_out[l], w_b_out[l], w_out[l])
            h_ctx = layer_norm(DEEPNORM_ALPHA * h_ctx + gate_c * sub_ctx) * ln_g[l] + ln_b[l]
        h_lat = layer_norm(DEEPNORM_ALPHA * h_lat + gate_l * sub_lat) * ln_g[l] + ln_b[l]
    return h_lat.astype(x.dtype)
```

```python
import numpy as np
import ml_dtypes
from contextlib import ExitStack
import concourse.bass as bass
import concourse.mybir as mybir
from concourse.bass_utils import run_bass_kernel_spmd

F32 = mybir.dt.float32
BF16 = mybir.dt.bfloat16
U8 = mybir.dt.uint8
AF = mybir.ActivationFunctionType
ALU = mybir.AluOpType

NT = 34
NTOK = 4352
QSCALE = 128 ** -0.5
ALPHA = 2.0 ** 0.25
EPS = 1e-6


class Prog:
    ENG = ('pe', 'act', 'dve', 'pool', 'sp')

    def __init__(self):
        self.ops = {e: [] for e in self.ENG}
        self.cnt = {}
        self.know = {e: {} for e in self.ENG}
        self.opclock = {}
        self.lastw = {}
        self.readers = {}
        self.pending = {e: {} for e in self.ENG}
        self.nlanes = {'sp': 8, 'pool': 4, 'act': 4}
        self.rr = {e: 0 for e in self.ENG}

    def lanes(self):
        out = list(self.ENG[:4])
        for q, n in self.nlanes.items():
            out += [f'd{q}{i}' for i in range(n)]
        return out

    def barrier(self):
        snap = dict(self.cnt)
        for e in self.ENG:
            p = self.pending[e]
            for l, n in snap.items():
                if p.get(l, 0) < n:
                    p[l] = n

    def emit(self, eng, fn, reads=(), writes=(), dma=False):
        if dma:
            i = self.rr[eng]
            self.rr[eng] = (i + 1) % self.nlanes[eng]
            lane = f'd{eng}{i}'
        else:
            lane = eng
        psr = [r for r in reads if r.startswith('psb')]
        if psr:
            reads = [r for r in reads if not r.startswith('psb')]
            writes = list(writes) + psr
        deps = dict(self.pending[eng])
        self.pending[eng] = {}

        def add(l, n):
            if deps.get(l, 0) < n:
                deps[l] = n
        for r in reads:
            w = self.lastw.get(r)
            if w:
                add(*w)
        for r in writes:
            w = self.lastw.get(r)
            if w and not (w[0] == lane and not dma):
                add(*w)
            for l, n in self.readers.get(r, {}).items():
                if not (l == lane and not dma):
                    add(l, n)
        if dma and self.cnt.get(lane, 0) > 0:
            add(lane, self.cnt[lane])
        know = self.know[eng]
        waits = [(l, n) for l, n in deps.items() if know.get(l, 0) < n]
        for l, n in deps.items():
            for l2, n2 in self.opclock.get((l, n), {}).items():
                if know.get(l2, 0) < n2:
                    know[l2] = n2
            if know.get(l, 0) < n:
                know[l] = n
        n = self.cnt.get(lane, 0) + 1
        self.cnt[lane] = n
        self.opclock[(lane, n)] = dict(know)
        self.ops[eng].append((waits, fn, lane))
        for r in reads:
            self.readers.setdefault(r, {})[lane] = n
        for r in writes:
            self.lastw[r] = (lane, n)
            self.readers[r] = {}

    def replay(self, name, eng, sems):
        for waits, fn, lane in self.ops[name]:
            for l, n in waits:
                eng.wait_ge(sems[l], n * (16 if l[0] == 'd' and l != 'dve' else 1))
            inst = fn(eng)
            inst.then_inc(sems[lane], 16 if (lane[0] == 'd' and lane != 'dve') else 1)


def seq(*fns):
    def f(e):
        r = None
        for g in fns:
            r = g(e)
        return r
    return f


def build(dbg=None):
    nc = bass.Bass("TRN2", target_bir_lowering=False)
    P = Prog()

    def din(name, shape, dt=F32):
        return nc.dram_tensor(name, list(shape), dt, kind="ExternalInput").ap()

    x = din("x", [4096, 1024])
    ctx = din("ctx", [256, 1024])
    cc = din("cc", [128, 8, 2])
    wmod = din("wmod", [6, 128, 8, 512])
    bmodc = din("bmodc", [128, 16])
    bmodg = din("bmodg", [1, 1024])
    win = din("win", [52, 128, 8, 128])
    wab = din("wab", [128, 8, 16])
    convw = din("convw", [128, 12, 5])
    alog = din("alog", [1, 8])
    dtb = din("dtb", [1, 8])
    lbp = din("lbp", [128, 2, 8])
    ang = din("ang", [128, 1])
    bng = din("bng", [128, 1])
    wao = din("wao", [128, 4, 1024])
    wbo = din("wbo", [128, 4, 1024])
    wo = din("wo", [128, 8, 1024])
    lng = din("lng", [1, 1024])
    lnb = din("lnb", [1, 1024])
    cidf = din("cidf", [128, 128])
    cmask = din("cmask", [128, 8, 128])
    csegm = din("csegm", [128, 512])
    y = nc.dram_tensor("y", [4096, 1024], F32, kind="ExternalOutput").ap()

    def dscr(name, shape, dt):
        kind = {"kind": "ExternalOutput"} if (dbg and name in dbg) else {}
        return nc.dram_tensor(name, list(shape), dt, **kind).ap()

    ZQ = dscr("ZQ", [4, 128, NTOK], BF16)
    ZK = dscr("ZK", [4, 128, NTOK], BF16)
    ZV = dscr("ZV", [4, 128, NTOK], BF16)
    ZAG = dscr("ZAG", [4, 128, 4096], BF16)
    ZBGT = dscr("ZBGT", [4, 128, 4096], BF16)
    ZM = dscr("ZM", [16, 128, 4096], BF16)
    ZBQ = dscr("ZBQ", [4, 128, NTOK], BF16)
    ZBK = dscr("ZBK", [8, 128, NTOK], BF16)
    ZBL = dscr("ZBL", [8, 128, NTOK], F32)
    ZBI = dscr("ZBI", [4, 128, NTOK], BF16)
    OFa = dscr("OFa", [32, 128, 512], F32)
    OFb = dscr("OFb", [32, 128, 512], F32)
    DBG = dscr("DBG", [128, 8192], F32) if dbg else None
    dbg_ota = dscr("dbg_ota", [128, 4, 4096], BF16) if dbg else None
    dbg_otb = dscr("dbg_otb", [128, 4, 4096], BF16) if dbg else None

    es = ExitStack()
    with es:
        ARENA = 204 * 1024
        arena = es.enter_context(nc.sbuf_tensor("arena", [128, ARENA], U8))
        PS = [es.enter_context(nc.psum_tensor(f"ps{i}", [128, 512], F32)) for i in range(8)]
        sems = {l: es.enter_context(nc.semaphore(f"s_{l}")) for l in P.lanes()}
        block = es.enter_context(nc.Block())

        st = {"off": 0, "n": 0}

        def alloc(shape, dt, name=None):
            nb = int(np.prod(shape[1:])) * (4 if dt == F32 else 2)
            off = (st["off"] + 63) // 64 * 64
            assert off + nb <= ARENA, (name, off, nb)
            st["off"] = off + nb
            ap = arena[:, off:off + nb].bitcast(dt)
            if len(shape) == 3:
                ap = ap.rearrange("p (a b) -> p a b", a=shape[1])
            elif len(shape) == 4:
                ap = ap.rearrange("p (a b c) -> p a b c", a=shape[1], b=shape[2])
            st["n"] += 1
            return ap

        class Ring:
            def __init__(self, name, n, shape, dt):
                self.name = name
                self.aps = [alloc(shape, dt, name) for _ in range(n)]
                self.i = 0

            def next(self):
                i = self.i
                self.i = (i + 1) % len(self.aps)
                return self.aps[i], f"{self.name}#{i}"

        psst = {"i": 0, "q": [0] * 8}

        def ps_alloc(nq=1):
            b = psst["i"] % 6
            psst["i"] += 1
            if nq == 4:
                s_ = 0
            else:
                s_ = psst["q"][b]
                psst["q"][b] = (s_ + nq) % 4
                assert s_ + nq <= 4
            ap = PS[b][:, s_ * 128:(s_ + nq) * 128]
            return ap, [f"psb{b}"]

        def bfv(ap):
            return ap.bitcast(BF16)[:, 0:128]

        pe = lambda fn, r, w: P.emit('pe', fn, r, w)
        act = lambda fn, r, w: P.emit('act', fn, r, w)
        dve = lambda fn, r, w: P.emit('dve', fn, r, w)
        pool = lambda fn, r, w: P.emit('pool', fn, r, w)

        def dma(out, in_, r, w, q='sp'):
            P.emit(q, lambda e: e.dma_start(out=out, in_=in_), r, w, dma=True)

        def mm(out, lhsT, rhs, start=True, stop=True):
            return lambda e: e.matmul(out, lhsT=lhsT, rhs=rhs, start=start, stop=stop)

        def tr(out, in_, ident):
            return lambda e: e.transpose(out=out, in_=in_, identity=ident)

        def actf(out, in_, func, bias=None, scale=None, accum_out=None):
            kw = {}
            if bias is not None:
                kw["bias"] = bias
            if scale is not None:
                kw["scale"] = scale
            if accum_out is not None:
                kw["accum_out"] = accum_out
            return lambda e: e.activation(out=out, in_=in_, func=func, **kw)

        def tt(out, in0, in1, op):
            return lambda e: e.tensor_tensor(out=out, in0=in0, in1=in1, op=op)

        def ts(out, in0, s1, s2, op0, op1=None):
            if op1 is None:
                return lambda e: e.tensor_scalar(out=out, in0=in0, scalar1=s1, scalar2=None, op0=op0)
            return lambda e: e.tensor_scalar(out=out, in0=in0, scalar1=s1, scalar2=s2, op0=op0, op1=op1)

        def stt(out, in0, scalar, in1, op0, op1):
            return lambda e: e.scalar_tensor_tensor(out=out, in0=in0, scalar=scalar, in1=in1, op0=op0, op1=op1)

        def cp(out, in_):
            return lambda e: (e.tensor_copy(out=out, in_=in_) if hasattr(e, 'tensor_copy') else e.activation(out=out, in_=in_, func=AF.Copy))

        identf = alloc([128, 128], F32)
        identb = alloc([128, 128], BF16)
        masks = alloc([128, 8, 128], F32)
        onesf = alloc([128, 128], F32)
        segm = alloc([128, 512], F32)
        M_le, M_lt, M_ge, M_gt, BD_le, BD_lt, BD_ge, BD_gt = [masks[:, i, :] for i in range(8)]
        gate_row = alloc([128, 1024], F32)
        GB = alloc([128, NT, 16], F32)
        modc = alloc([128, 16, 2], F32)
        lbc = alloc([128, 8], F32)
        omlb = alloc([128, 8], F32)
        nomlb = alloc([128, 8], F32)
        gains = alloc([128, 2], F32)
        cw = alloc([128, 12, 5], F32)
        rowc = alloc([128, 16], F32)
        persist_off = st["off"]

        dma(identf, cidf, [], ["identf"])
        dma(masks, cmask, [], ["masks"])
        dma(segm, csegm, [], ["segm"])
        dma(cw, convw, [], ["cw"])
        dma(gains[:, 0:1], ang, [], ["gains"])
        dma(gains[:, 1:2], bng, [], ["gains"])
        dma(rowc[:, 0:8], alog.partition_broadcast(128), [], ["rowc"])
        dma(rowc[:, 8:16], dtb.partition_broadcast(128), [], ["rowc"])
        dve(cp(identb, identf), ["identf"], ["identb"])
        pool(lambda e: e.memset(onesf, 1.0), [], ["onesf"])
        act(actf(rowc[:, 0:8], rowc[:, 0:8], AF.Exp), ["rowc"], ["rowc"])
        dve(ts(rowc[:, 0:8], rowc[:, 0:8], -1.0, None, ALU.mult), ["rowc"], ["rowc"])

        ph0 = st["off"]
        cct = alloc([128, 8, 2], F32)
        sil = alloc([128, 8, 2], F32)
        srep = alloc([128, 8, 128], F32)
        bmc = alloc([128, 16], F32)
        lbt = alloc([128, 2, 8], F32)
        wmr = Ring("wm", 2, [128, 8, 512], F32)
        dma(cct, cc, [], ["cct"])
        dma(bmc, bmodc, [], ["bmc"])
        dma(lbt, lbp, [], ["lbt"])
        dma(gate_row, bmodg.partition_broadcast(128), [], ["gate_row"])
        act(actf(sil, cct, AF.Silu), ["cct"], ["sil"])
        dve(cp(srep, sil[:, :, 0:1].to_broadcast([128, 8, 128])), ["sil"], ["srep"])
        dve(tt(lbc, lbt[:, 0, :], lbt[:, 1, :], ALU.subtract), ["lbt"], ["lbc"])
        act(actf(lbc, lbc, AF.Sigmoid), ["lbc"], ["lbc"])
        dve(ts(omlb, lbc, -1.0, 1.0, ALU.mult, ALU.add), ["lbc"], ["omlb"])
        dve(ts(nomlb, lbc, -1.0, None, ALU.add), ["lbc"], ["nomlb"])
        for blk in range(6):
            wt, wk = wmr.next()
            dma(wt, wmod[blk], [], [wk])
            if blk < 4:
                for jj in range(4):
                    j = blk * 4 + jj
                    pt, pk = ps_alloc(1)
                    pe(seq(*[mm(pt[:, 0:2], wt[:, kc, jj * 128:(jj + 1) * 128], sil[:, kc, :],
                                start=(kc == 0), stop=(kc == 7)) for kc in range(8)]),
                       [wk, "sil"], pk)
                    dve(ts(modc[:, j, :], pt[:, 0:2], bmc[:, j:j + 1], 1.0 if j >= 8 else 0.0, ALU.add, ALU.add),
                        pk + ["bmc"], ["modc"])
            else:
                pt, pk = ps_alloc(4)
                pe(seq(*[mm(pt, srep[:, kc, :], wt[:, kc, :], start=(kc == 0), stop=(kc == 7))
                         for kc in range(8)]), [wk, "srep"], pk)
                gs = gate_row[:, (blk - 4) * 512:(blk - 3) * 512]
                dve(tt(gs, gs, pt, ALU.add), pk + ["gate_row"], ["gate_row"])
        P.barrier()
        st["off"] = ph0

        uT = alloc([128, 8, NTOK], BF16)
        ph2 = st["off"]
        xr = Ring("xt", 2, [128, 1024], F32)
        xnr = Ring("xn", 2, [128, 1024], BF16)
        str_ = Ring("st", 2, [128, 2, 6], F32)
        mvr = Ring("mv", 2, [128, 4], F32)
        for T in range(NT):
            src = ctx[T * 128:(T + 1) * 128, :] if T < 2 else x[(T - 2) * 128:(T - 1) * 128, :]
            w = 1 if T < 2 else 0
            xt, xk = xr.next()
            xn, xnk = xnr.next()
            stt_, stk = str_.next()
            mv, mvk = mvr.next()
            dma(xt, src, [], [xk])
            dve(seq(lambda e, a=stt_, b=xt: e.bn_stats(out=a[:, 0, :], in_=b[:, 0:512]),
                    lambda e, a=stt_, b=xt: e.bn_stats(out=a[:, 1, :], in_=b[:, 512:1024])), [xk], [stk])
            dve(lambda e, a=mv, b=stt_: e.bn_aggr(out=a[:, 0:2], in_=b.rearrange("p a b -> p (a b)")), [stk], [mvk])
            act(actf(mv[:, 2:3], mv[:, 1:2], AF.Sqrt, bias=EPS), [mvk], [mvk + "s"])
            dve(lambda e, a=mv: e.reciprocal(out=a[:, 2:3], in_=a[:, 2:3]), [mvk + "s"], [mvk + "s"])
            dve(stt(mv[:, 3:4], mv[:, 0:1], -1.0, mv[:, 2:3], ALU.mult, ALU.mult), [mvk, mvk + "s"], [mvk + "n"])
            act(actf(xn, xt, AF.Identity, bias=mv[:, 3:4], scale=mv[:, 2:3]), [xk, mvk + "s", mvk + "n"], [xnk])
            pt, pk = ps_alloc(4)
            ptb = pt.bitcast(BF16)
            pe(seq(*[tr(ptb[:, kc * 128:(kc + 1) * 128], xn[:, kc * 128:(kc + 1) * 128], identb) for kc in range(8)]),
               [xnk, "identb"], pk)
            for kc in range(8):
                o = uT[:, kc, T * 128:(T + 1) * 128]
                i = ptb[:, kc * 128:(kc + 1) * 128]
                if kc % 2 == 0:
                    act(actf(o, i, AF.Identity, bias=modc[:, kc, w:w + 1], scale=modc[:, 8 + kc, w:w + 1]),
                        pk + ["modc"], [f"uT{T}"])
                else:
                    dve(ts(o, i, modc[:, 8 + kc, w:w + 1], modc[:, kc, w:w + 1], ALU.mult, ALU.add),
                        pk + ["modc"], [f"uT{T}"])
        uT_all = [f"uT{T}" for T in range(NT)]

        wfr = Ring("wf", 3, [128, 8, 128], F32)
        wbr = Ring("wb", 2, [128, 8, 128], BF16)
        ZL = 4360
        bufA = alloc([128, ZL], F32)
        bufB = alloc([128, ZL], F32)
        bufC = alloc([128, ZL], F32)
        stg = Ring("stg", 2, [128, NTOK], BF16)
        wabf = alloc([128, 8, 16], F32)
        wabb = alloc([128, 8, 16], BF16)
        tmpab = alloc([128, NT, 8], F32)
        tmpab2 = alloc([128, NT, 8], F32)
        ssr = Ring("ss", 2, [128, 512], F32)

        dma(wabf, wab, [], ["wabf"])
        pool(cp(wabb, wabf), ["wabf"], ["wabb"])
        for T in range(NT):
            pt, pk = ps_alloc(1)
            pe(seq(*[mm(pt[:, 0:16], uT[:, kc, T * 128:(T + 1) * 128], wabb[:, kc, :], start=(kc == 0), stop=(kc == 7))
                     for kc in range(8)]), [f"uT{T}", "wabb"], pk)
            dve(cp(GB[:, T, :], pt[:, 0:16]), pk, ["GBraw"])
        a_ = tmpab
        b_ = tmpab2
        dve(tt(a_, GB[:, :, 0:8], rowc[:, 8:16].unsqueeze(1).to_broadcast([128, NT, 8]), ALU.add), ["GBraw", "rowc"], ["tmpab"])
        dve(stt(b_, a_, -1.0, a_, ALU.mult, ALU.max), ["tmpab"], ["tmpab2"])
        act(actf(b_, b_, AF.Exp, scale=-1.0), ["tmpab2"], ["tmpab2"])
        act(actf(b_, b_, AF.Ln, bias=1.0), ["tmpab2"], ["tmpab2"])
        dve(stt(a_, a_, 0.0, b_, ALU.max, ALU.add), ["tmpab", "tmpab2"], ["tmpab"])
        dve(tt(GB[:, :, 0:8], a_, rowc[:, 0:8].unsqueeze(1).to_broadcast([128, NT, 8]), ALU.mult), ["tmpab", "rowc", "GBraw"], ["GBg"])
        act(actf(GB[:, :, 8:16], GB[:, :, 8:16], AF.Sigmoid), ["GBraw"], ["GBb"])
        GBk = ["GBg", "GBb"]

        blocks = [(0, 256)] + [(256 + i * 512, 512) for i in range(8)]

        def cm_view(buf, r0, nr=8):
            v = buf[:, 256:256 + 4096].rearrange("p (w r) -> p w r", r=64)[:, :, r0:r0 + nr]
            return v.rearrange("p w r -> p r w")

        def ps_rw(pt):
            return pt.rearrange("p (r w) -> p r w", w=64)

        def project(j, lat_only, epi):
            wf, wfk = wfr.next()
            wb, wbk = wbr.next()
            dma(wf, win[j], [], [wfk])
            pool(cp(wb, wf), [wfk], [wbk])
            for bi, (c0, n) in enumerate(blocks):
                if lat_only and bi == 0:
                    continue
                pt, pk = ps_alloc(4)
                T0 = c0 // 128
                pe(seq(*[mm(pt[:, 0:n], wb[:, kc, :], uT[:, kc, c0:c0 + n], start=(kc == 0), stop=(kc == 7))
                         for kc in range(8)]), [wbk] + [f"uT{T0 + i}" for i in range(n // 128)], pk)
                epi(bi, c0, n, pt, pk)

        def zpos(c0):
            return c0 + 2 if c0 < 256 else c0 + 6

        pool(lambda e: e.memset(bufA, 0.0), [], ["bufA"])

        def conv_chunk(j, kind, h):
            def epi(bi, c0, n, pt, pk):
                z0 = zpos(c0)
                act(cp(bufA[:, z0:z0 + n], pt[:, 0:n]) if False else actf(bufA[:, z0:z0 + n], pt[:, 0:n], AF.Identity), pk, ["bufA"])
            project(j, False, epi)
            L = 4356
            dve(ts(bufB[:, 0:L], bufA[:, 0:L], cw[:, j, 0:1], None, ALU.mult), ["bufA", "cw"], ["bufB"])
            for k in range(1, 5):
                dve(stt(bufB[:, 0:L], bufA[:, k:k + L], cw[:, j, k:k + 1], bufB[:, 0:L], ALU.mult, ALU.add),
                    ["bufA", "bufB", "cw"], ["bufB"])
            sg, sgk = stg.next()
            if kind == 'v':
                act(actf(sg[:, 0:256], bufB[:, 0:256], AF.Silu), ["bufB"], [sgk])
                act(actf(sg[:, 256:NTOK], bufB[:, 260:4356], AF.Silu), ["bufB"], [sgk])
                dma(ZV[h], sg, [sgk], ["ZV"])
                return
            act(actf(bufC[:, 0:L], bufB[:, 0:L], AF.Silu), ["bufB"], ["bufC"])
            pool(tt(bufB[:, 0:L], bufC[:, 0:L], bufC[:, 0:L], ALU.mult), ["bufC", "bufB"], ["bufB"])
            for (c0, n) in blocks:
                a0 = c0 if c0 < 256 else c0 + 4
                pt, pk = ps_alloc(4)
                pe(mm(pt[:, 0:n], onesf, bufB[:, a0:a0 + n]), ["onesf", "bufB"], pk)
                ss, ssk = ssr.next()
                act(actf(ss[:, 0:n], pt[:, 0:n], AF.Sqrt, bias=EPS), pk, [ssk])
                dve(lambda e, a=ss, n=n: e.reciprocal(out=a[:, 0:n], in_=a[:, 0:n]), [ssk], [ssk])
                dve(tt(sg[:, c0:c0 + n], bufC[:, a0:a0 + n], ss[:, 0:n], ALU.mult), [ssk, "bufC"], [sgk])
            dma((ZQ if kind == 'q' else ZK)[h], sg, [sgk], ["ZQ" if kind == 'q' else "ZK"])

        def simple_chunk(j, func, dst, dkey, lat_only, cmo):
            sg, sgk = stg.next()

            def epi(bi, c0, n, pt, pk):
                if lat_only:
                    o = sg[:, c0 - 256:c0 - 256 + n]
                    i = pt[:, 0:n]
                elif bi == 0 or not cmo:
                    o = sg[:, c0:c0 + n]
                    i = pt[:, 0:n]
                else:
                    o = cm_view(sg, (c0 - 256) // 64)
                    i = ps_rw(pt)
                act(actf(o, i, func), pk, [sgk])
            project(j, lat_only, epi)
            if lat_only:
                dma(dst, sg[:, 0:4096], [sgk], [dkey])
            else:
                dma(dst, sg, [sgk], [dkey])

        def f_chunk(j, d, h):
            def epi(bi, c0, n, pt, pk):
                if bi == 0:
                    o = bufA[:, c0:c0 + n]
                    i = pt[:, 0:n]
                else:
                    o = cm_view(bufA, (c0 - 256) // 64)
                    i = ps_rw(pt)
                act(actf(o, i, AF.Sigmoid), pk, ["bufA"])
            project(j, False, epi)
            c = d * 4 + h
            dve(ts(bufB[:, 0:NTOK], bufA[:, 0:NTOK], omlb[:, c:c + 1], lbc[:, c:c + 1], ALU.mult, ALU.add),
                ["bufA", "omlb", "lbc"], ["bufB"])
            act(actf(bufC[:, 0:NTOK], bufB[:, 0:NTOK], AF.Ln), ["bufB"], ["bufC"])
            dma(ZBL[c], bufC[:, 0:NTOK], ["bufC"], ["ZBL"])
            sg, sgk = stg.next()
            pool(ts(sg, bufA[:, 0:NTOK], nomlb[:, c:c + 1], omlb[:, c:c + 1], ALU.mult, ALU.add),
                 ["bufA", "nomlb", "omlb"], [sgk])
            dma(ZBK[c], sg, [sgk], ["ZBK"])

        todo = dbg.get("chunks") if dbg and "chunks" in dbg else range(52)
        for j in todo:
            if j < 4:
                conv_chunk(j, 'q', j)
            elif j < 8:
                conv_chunk(j, 'k', j - 4)
            elif j < 12:
                conv_chunk(j, 'v', j - 8)
            elif j < 16:
                simple_chunk(j, AF.Silu, ZAG[j - 12], "ZAG", True, False)
            elif j < 20:
                simple_chunk(j, AF.Silu, ZBQ[j - 16], "ZBQ", False, True)
            elif j < 24:
                f_chunk(j, 0, j - 20)
            elif j < 28:
                f_chunk(j, 1, j - 24)
            elif j < 32:
                simple_chunk(j, AF.Identity, ZBI[j - 28], "ZBI", False, True)
            elif j < 36:
                simple_chunk(j, AF.Silu, ZBGT[j - 32], "ZBGT", True, False)
            else:
                simple_chunk(j, AF.Sigmoid, ZM[j - 36], "ZM", True, False)
        if dbg and dbg.get("dump_gb"):
            dma(DBG[:, 0:NT * 16], GB.rearrange("p a b -> p (a b)"), GBk, ["DBG"])
            dma(DBG[:, 1024:1024 + 32], modc.rearrange("p a b -> p (a b)"), ["modc"], ["DBG"])
            dma(DBG[:, 2048:3072], gate_row, ["gate_row"], ["DBG"])
        P.barrier()
        st["off"] = persist_off


        OTa = alloc([128, 4, 4096], BF16)
        OTb = alloc([128, 4, 4096], BF16)
        Sst = alloc([128, 8, 128], F32)
        Sb = alloc([128, 8, 128], BF16)
        ph3 = st["off"]
        r_q = Ring("gq", 6, [128, 4, 128], BF16)
        r_k = Ring("gk", 4, [128, 4, 128], BF16)
        r_v = Ring("gv", 4, [128, 4, 128], BF16)
        r_ex = Ring("gex", 6, [128, 16], F32)
        r_nb = Ring("gnb", 6, [128, 4], F32)
        r_gm = Ring("ggm", 4, [128, 8], F32)
        r_f = {n: Ring("g" + n, k, [128, 128], F32) for n, k in
               (("M1", 8), ("E", 8), ("Ei", 8), ("Es", 8), ("B", 8), ("BT", 8), ("X", 16), ("P", 16), ("PT", 16),
                ("Ub", 16), ("o1", 8), ("ot", 8))}
        r_b = {n: Ring("g" + n, k, [128, 128], BF16) for n, k in
               (("at", 16), ("Xb", 8), ("Kg", 8), ("Kd", 16), ("vt", 8), ("WT", 16), ("vn", 8), ("on", 8))}
        r_s = Ring("gss", 8, [128, 4], F32)
        r_ofo = Ring("ofo", 2, [128, 512], F32)
        r_ofi = Ring("ofi", 6, [128, 512], F32)

        def finalize(ot, otk, OTkey, colap, scale, tview=None):
            ss, ssk = r_s.next()
            jk, jkk = r_f["o1"].next()
            act(actf(jk, ot, AF.Square, accum_out=ss[:, 0:1]), [otk], [jkk, ssk])
            dve(ts(ss[:, 1:2], ss[:, 0:1], scale * scale / 128.0, EPS, ALU.mult, ALU.add), [ssk], [ssk + "b"])
            act(actf(ss[:, 1:2], ss[:, 1:2], AF.Sqrt), [ssk + "b"], [ssk + "b"])
            dve(lambda e, a=ss: e.reciprocal(out=a[:, 1:2], in_=a[:, 1:2]), [ssk + "b"], [ssk + "b"])
            on, onk = r_b["on"].next()
            dve(ts(on, ot, ss[:, 1:2], scale, ALU.mult, ALU.mult), [otk, ssk + "b"], [onk])
            pt, pk = ps_alloc(1)
            pe(tr(bfv(pt), on, identb), [onk, "identb"], pk)
            if tview is None:
                act(cp(colap, bfv(pt)), pk, [OTkey])
            else:
                for w2 in range(2):
                    act(cp(colap[:, :, w2], bfv(pt)[:, w2 * 64:(w2 + 1) * 64]), pk, [OTkey])

        pool(lambda e: e.memset(Sst, 0.0), [], ["S%d" % i for i in range(8)])
        pool(lambda e: e.memset(Sb, 0.0), [], ["Sb%d" % i for i in range(8)])

        def gdn_prep(pairs, second):
            prs = []
            for (T, d) in pairs:
                islat = T >= 2
                kT, kk = r_k.next()
                vT, vk = r_v.next()
                tsl = slice(T * 128, (T + 1) * 128)
                dma(kT, ZK[:, :, tsl].rearrange("h p t -> p h t"), ["ZK"], [kk])
                dma(vT, ZV[:, :, tsl].rearrange("h p t -> p h t"), ["ZV"], [vk])
                qT = qk = None
                if islat:
                    qT, qk = r_q.next()
                    dma(qT, ZQ[:, :, tsl].rearrange("h p t -> p h t"), ["ZQ"], [qk])
                ofi = ofik = None
                ofdefer = False
                if islat and second:
                    ofi, ofik = r_ofi.next()
                    ofdefer = f"OF{T - 2}" not in P.lastw
                    if not ofdefer:
                        dma(ofi, OFa[T - 2], [f"OF{T - 2}"], [ofik])
                ML, MR, MS = (BD_le, BD_gt, BD_lt) if d == 0 else (BD_ge, BD_lt, BD_gt)
                g4 = GB[:, T, d * 4:(d + 1) * 4]
                pc, pck = ps_alloc(1)
                gm, gmk = r_gm.next()
                pool(ts(gm[:, 0:4], g4, BD_le[:, 63:64], None, ALU.mult), GBk + ["masks"], [gmk])
                pool(ts(gm[:, 4:8], g4, BD_ge[:, 64:65], None, ALU.mult), GBk + ["masks"], [gmk])
                pe(seq(mm(pc[:, 0:4], ML, g4), mm(pc[:, 4:8], MR, g4),
                       mm(pc[:, 8:12], onesf, gm[:, 0:4]), mm(pc[:, 12:16], onesf, gm[:, 4:8])),
                   ["masks", "onesf", gmk] + GBk, pck)
                ex, exk = r_ex.next()
                act(actf(ex, pc[:, 0:16], AF.Exp), pck, [exk])
                nb, nbk = r_nb.next()
                pool(ts(nb, GB[:, T, 8 + d * 4:8 + (d + 1) * 4], -1.0, None, ALU.mult), GBk, [nbk])
                prs.append(dict(T=T, d=d, islat=islat, qT=qT, qk=qk, kT=kT, kk=kk, vT=vT, vk=vk, ex=ex, exk=exk,
                                nb=nb, nbk=nbk, ML=ML, MR=MR, MS=MS, ofi=ofi, ofik=ofik, ofdefer=ofdefer, heads=[]))
            gstage = dbg.get('gdn_stage', 99) if dbg else 99
            if gstage < 1:
                return prs
            items = []
            for pr in prs:
                for h in range(4):
                    c = pr["d"] * 4 + h
                    hd = dict(h=h, c=c, gcol=GB[:, pr["T"], c:c + 1], bcol=GB[:, pr["T"], 8 + c:9 + c])
                    pr["heads"].append(hd)
                    items.append((pr, hd))
            for pr, hd in items:
                h = hd["h"]
                kT, kk, vT, vk = pr["kT"], pr["kk"], pr["vT"], pr["vk"]
                hd["pKK"], hd["pKKk"] = ps_alloc(1)
                pe(mm(hd["pKK"], kT[:, h, :], kT[:, h, :]), [kk], hd["pKKk"])
                M1, M1k = r_f["M1"].next()
                pool(ts(M1, pr["MR"], hd["gcol"], None, ALU.mult), ["masks"] + GBk, [M1k])
                pD, pDk = ps_alloc(1)
                pe(mm(pD, M1, pr["ML"]), [M1k, "masks"], pDk)
                E, Ek = r_f["E"].next()
                act(actf(E, pD, AF.Exp), pDk, [Ek])
                Es, Esk = r_f["Es"].next()
                pool(tt(Es, E, pr["MS"], ALU.mult), [Ek, "masks"], [Esk])
                B, Bk = r_f["B"].next()
                dve(stt(B, hd["pKK"], hd["bcol"], Es, ALU.mult, ALU.mult), hd["pKKk"] + [Esk] + GBk, [Bk])
                hd["B"], hd["Bk"] = B, Bk
                hd["at"] = hd["atk"] = None
                if pr["islat"]:
                    Ei, Eik = r_f["Ei"].next()
                    pool(tt(Ei, E, pr["ML"], ALU.mult), [Ek, "masks"], [Eik])
                    pQK, pQKk = ps_alloc(1)
                    pe(mm(pQK, kT[:, h, :], pr["qT"][:, h, :]), [kk, pr["qk"]], pQKk)
                    at, atk = r_b["at"].next()
                    dve(tt(at, pQK, Ei, ALU.mult), pQKk + [Eik], [atk])
                    hd["at"], hd["atk"] = at, atk
            if gstage < 2:
                return prs
            for pr, hd in items:
                pBT, pBTk = ps_alloc(1)
                pe(mm(pBT, hd["B"], identf), [hd["Bk"], "identf"], pBTk)
                BT, BTk = r_f["BT"].next()
                act(cp(BT, pBT), pBTk, [BTk])
                X, Xk = r_f["X"].next()
                pool(tt(X, identf, hd["B"], ALU.subtract), [hd["Bk"], "identf"], [Xk])
                hd["P"], hd["Pk"], hd["PT"], hd["PTk"], hd["X"], hd["Xk"] = hd["B"], hd["Bk"], BT, BTk, X, Xk
            if gstage < 3:
                return prs
            for lvl in range(5):
                last = lvl == 4
                for pr, hd in items:
                    hd["p2t"], hd["p2tk"] = ps_alloc(1)
                    pe(mm(hd["p2t"], hd["P"], hd["PT"]), [hd["Pk"], hd["PTk"]], hd["p2tk"])
                    if not last:
                        hd["p2"], hd["p2k"] = ps_alloc(1)
                        pe(mm(hd["p2"], hd["PT"], hd["P"]), [hd["Pk"], hd["PTk"]], hd["p2k"])
                for pr, hd in items:
                    nPT, nPTk = r_f["PT"].next()
                    act(cp(nPT, hd["p2t"]), hd["p2tk"], [nPTk])
                    hd["PT"], hd["PTk"] = nPT, nPTk
                    if not last:
                        nP, nPk = r_f["P"].next()
                        dve(cp(nP, hd["p2"]), hd["p2k"], [nPk])
                        hd["P"], hd["Pk"] = nP, nPk
                for pr, hd in items:
                    hd["px"], hd["pxk"] = ps_alloc(1)
                    pe(mm(hd["px"], hd["PT"], hd["X"]), [hd["PTk"], hd["Xk"]], hd["pxk"])
                for pr, hd in items:
                    if not last:
                        nX, nXk = r_f["X"].next()
                        dve(tt(nX, hd["px"], hd["X"], ALU.add), hd["pxk"] + [hd["Xk"]], [nXk])
                        hd["X"], hd["Xk"] = nX, nXk
                    else:
                        Xb, Xbk = r_b["Xb"].next()
                        dve(tt(Xb, hd["px"], hd["X"], ALU.add), hd["pxk"] + [hd["Xk"]], [Xbk])
                        hd["Xb"], hd["Xbk"] = Xb, Xbk
            if gstage < 4:
                return prs
            for pr, hd in items:
                h = hd["h"]
                ex, exk = pr["ex"], pr["exk"]
                pkt, pktk = ps_alloc(1)
                pe(tr(bfv(pkt), pr["kT"][:, h, :], identb), [pr["kk"], "identb"], pktk)
                pvt, pvtk = ps_alloc(1)
                pe(tr(bfv(pvt), pr["vT"][:, h, :], identb), [pr["vk"], "identb"], pvtk)
                Kg, Kgk = r_b["Kg"].next()
                act(actf(Kg, bfv(pkt), AF.Identity, scale=ex[:, h:h + 1]), pktk + [exk], [Kgk])
                Kd, Kdk = r_b["Kd"].next()
                dve(ts(Kd, bfv(pkt), ex[:, 4 + h:5 + h], None, ALU.mult), pktk + [exk], [Kdk])
                vt, vtk = r_b["vt"].next()
                act(cp(vt, bfv(pvt)), pvtk, [vtk])
                pU, pUk = ps_alloc(1)
                pe(mm(pU, hd["Xb"], vt), [hd["Xbk"], vtk], pUk)
                Ub, Ubk = r_f["Ub"].next()
                act(actf(Ub, pU, AF.Identity, scale=hd["bcol"]), pUk + GBk, [Ubk])
                pW, pWk = ps_alloc(1)
                pe(mm(pW, Kg, hd["Xb"]), [Kgk, hd["Xbk"]], pWk)
                WT, WTk = r_b["WT"].next()
                dve(cp(WT, pW), pWk, [WTk])
                hd.update(WT=WT, WTk=WTk, Ub=Ub, Ubk=Ubk, Kd=Kd, Kdk=Kdk)
            return prs

        def gdn_chain(preps, second):
            items = [(pr, hd) for pr in preps for hd in pr["heads"]]
            for pr in preps:
                if pr["ofdefer"]:
                    dma(pr["ofi"], OFa[pr["T"] - 2], [f"OF{pr['T'] - 2}"], [pr["ofik"]])
            for ii, (pr, hd) in enumerate(items):
                hd["vn"], hd["vnk"] = r_b["vn"].next()
                if pr["islat"]:
                    hd["pO1"], hd["pO1k"] = PS[6 + ii // 4][:, (ii % 4) * 128:(ii % 4 + 1) * 128], [f"psb{6 + ii // 4}"]
            for sub in range(2):
                def rs(pr):
                    j = sub if pr["d"] == 0 else 1 - sub
                    return j, slice(64 * j, 64 * j + 64)
                for pr, hd in items:
                    c, h = hd["c"], hd["h"]
                    j, r = rs(pr)
                    pP, pPk = ps_alloc(1)
                    pe(mm(pP[r, :], hd["WT"][:, r], Sb[:, c, :]), [hd["WTk"], f"Sb{c}"], pPk)
                    dve(stt(hd["vn"][r, :], pP[r, :], pr["nb"][r, h:h + 1], hd["Ub"][r, :], ALU.mult, ALU.add),
                        pPk + [pr["nbk"], hd["Ubk"]], [hd["vnk"]])
                for pr, hd in items:
                    c, h = hd["c"], hd["h"]
                    j, r = rs(pr)
                    if pr["islat"]:
                        pe(mm(hd["pO1"][r, :], pr["qT"][:, h, r], Sb[:, c, :]), [pr["qk"], f"Sb{c}"], hd["pO1k"])
                    pS, pSk = ps_alloc(1)
                    pe(mm(pS, hd["Kd"][r, :], hd["vn"][r, :]), [hd["Kdk"], hd["vnk"]], pSk)
                    hd["pS"], hd["pSk"] = pS, pSk
                for pr, hd in items:
                    c, h = hd["c"], hd["h"]
                    j, r = rs(pr)
                    dve(stt(Sst[:, c, :], Sst[:, c, :], pr["ex"][:, 8 + 4 * j + h:9 + 4 * j + h], hd["pS"], ALU.mult, ALU.add),
                        hd["pSk"] + [pr["exk"], f"S{c}"], [f"S{c}"])
                    act(cp(Sb[:, c, :], Sst[:, c, :]), [f"S{c}"], [f"Sb{c}"])
            for pr, hd in items:
                if pr["islat"]:
                    hd["pO2"], hd["pO2k"] = ps_alloc(1)
                    pe(mm(hd["pO2"], hd["at"], hd["vn"]), [hd["atk"], hd["vnk"]], hd["pO2k"])
            for pr in preps:
                if not pr["islat"]:
                    continue
                lt = pr["T"] - 2
                if not second:
                    ofo, ofok = r_ofo.next()
                for hd in pr["heads"]:
                    c, h = hd["c"], hd["h"]
                    o1, o1k = r_f["o1"].next()
                    act(actf(o1, hd["pO1"], AF.Identity, scale=pr["ex"][:, h:h + 1]), hd["pO1k"] + [pr["exk"]], [o1k])
                    if not second:
                        dve(tt(ofo[:, h * 128:(h + 1) * 128], o1, hd["pO2"], ALU.add), [o1k] + hd["pO2k"], [ofok])
                    else:
                        ot, otk = r_f["ot"].next()
                        dve(tt(ot, o1, hd["pO2"], ALU.add), [o1k] + hd["pO2k"], [otk])
                        pool(tt(ot, ot, pr["ofi"][:, h * 128:(h + 1) * 128], ALU.add), [otk, pr["ofik"]], [otk])
                        finalize(ot, otk, "OTa", OTa[:, h, lt * 128:(lt + 1) * 128], QSCALE)
                if not second:
                    dma(OFa[lt], ofo, [ofok], [f"OF{lt}"])

        bwd_order = [1, 0] + list(range(33, 1, -1))
        nsteps = dbg.get("gdn_steps", NT) if dbg else NT
        if not (dbg and dbg.get("skip_gdn")):
            cur = gdn_prep([(0, 0), (bwd_order[0], 1)], False)
            for i in range(nsteps):
                nxt = None
                if i + 1 < nsteps:
                    nxt = gdn_prep([(i + 1, 0), (bwd_order[i + 1], 1)], (i + 1) >= 18)
                if not (dbg and dbg.get('gdn_stage', 99) < 5):
                    gdn_chain(cur, second=(i >= 18))
                cur = nxt
        if dbg and dbg.get("dump_ota"):
            dma(dbg_ota, OTa, ["OTa"], ["dbg_ota"])
            dma(DBG[:, 4096:4096 + 1024], Sst.rearrange("p a b -> p (a b)"), ["S%d" % i for i in range(8)], ["DBG"])
        P.barrier()
        st["off"] = ph3

        h_q = Ring("hq", 4, [128, 4, 128], BF16)
        h_k = Ring("hk", 4, [128, 4, 128], BF16)
        h_i = Ring("hi", 4, [128, 4, 128], BF16)
        h_g = Ring("hg", 4, [128, 512], F32)
        h_w = {n: Ring("h" + n, 3, [128, 512], F32) for n in ("G", "eq", "ek", "ed", "Gx", "enx")}
        h_qg = Ring("hqg", 4, [128, 4, 128], BF16)
        h_kg = Ring("hkg", 3, [128, 4, 128], BF16)
        h_kd = Ring("hkd", 3, [128, 4, 128], BF16)
        h_gl = Ring("hgl", 4, [128, 16], F32)
        h_at = Ring("hat", 16, [128, 128], BF16)
        h_kt = Ring("hkt", 16, [128, 128], BF16)
        h_vt = Ring("hvt", 16, [128, 128], BF16)
        r_s = Ring("hss", 8, [128, 4], F32)
        r_f = {"o1": Ring("ho1", 4, [128, 128], F32), "ot": Ring("hot", 8, [128, 128], F32)}
        r_b = {"on": Ring("hon", 8, [128, 128], BF16)}
        r_ofo = Ring("hofo", 2, [128, 512], F32)
        r_ofi = Ring("hofi", 6, [128, 512], F32)
        pool(lambda e: e.memset(Sst, 0.0), ["S%d" % i for i in range(8)], ["S%d" % i for i in range(8)])
        pool(lambda e: e.memset(Sb, 0.0), ["Sb%d" % i for i in range(8)], ["Sb%d" % i for i in range(8)])

        def bc8(ap8):
            return ap8.unsqueeze(2).to_broadcast([128, 8, 64])

        def v864(ap):
            return ap.rearrange("p (a b) -> p a b", b=64)

        def hg_prep(pairs, second):
            prs = []
            for (T, d) in pairs:
                islat = T >= 2
                tsl = slice(T * 128, (T + 1) * 128)
                qT, qk = h_q.next()
                kT, kk = h_k.next()
                iT, ik = h_i.next()
                gT, gk = h_g.next()
                dma(qT, ZBQ[:, :, tsl].rearrange("h p t -> p h t"), ["ZBQ"], [qk])
                dma(kT, ZBK[d * 4:(d + 1) * 4, :, tsl].rearrange("h p t -> p h t"), ["ZBK"], [kk])
                dma(iT, ZBI[:, :, tsl].rearrange("h p t -> p h t"), ["ZBI"], [ik])
                dma(gT.rearrange("p (h t) -> p h t", h=4), ZBL[d * 4:(d + 1) * 4, :, tsl].rearrange("h p t -> p h t"), ["ZBL"], [gk])
                ofi = ofik = None
                ofdefer = False
                if islat and second:
                    ofi, ofik = r_ofi.next()
                    ofdefer = f"OFb{T - 2}" not in P.lastw
                    if not ofdefer:
                        dma(ofi, OFb[T - 2], [f"OFb{T - 2}"], [ofik])
                G, Gk = h_w["G"].next()
                dve(lambda e, a=G, b=gT: e.tensor_tensor_scan(out=a, data0=segm, data1=b, initial=0.0, op0=ALU.mult, op1=ALU.add),
                    [gk, "segm"], [Gk])
                gl, glk = h_gl.next()
                Glast = v864(G)[:, :, 63]
                eq, eqk = h_w["eq"].next()
                ek, ekk = h_w["ek"].next()
                ed, edk = h_w["ed"].next()
                act(actf(gl[:, 0:8], Glast, AF.Exp), [Gk], [glk])
                if d == 0:
                    act(actf(eq, G, AF.Exp), [Gk], [eqk])
                    act(actf(ek, G, AF.Exp, scale=-1.0), [Gk], [ekk])
                    dve(tt(v864(ed), v864(ek), bc8(gl[:, 0:8]), ALU.mult), [ekk, glk], [edk])
                else:
                    Gx, Gxk = h_w["Gx"].next()
                    enx, enxk = h_w["enx"].next()
                    act(actf(gl[:, 8:16], Glast, AF.Exp, scale=-1.0), [Gk], [glk])
                    pool(tt(Gx, G, gT, ALU.subtract), [Gk, gk], [Gxk])
                    act(actf(ed, Gx, AF.Exp), [Gxk], [edk])
                    act(actf(enx, Gx, AF.Exp, scale=-1.0), [Gxk], [enxk])
                    dve(tt(v864(eq), v864(enx), bc8(gl[:, 0:8]), ALU.mult), [enxk, glk], [eqk])
                    pool(tt(v864(ek), v864(ed), bc8(gl[:, 8:16]), ALU.mult), [edk, glk], [ekk])
                qg, qgk = h_qg.next()
                kg, kgk = h_kg.next()
                kd, kdk = h_kd.next()
                f2 = lambda a: a.rearrange("p h t -> p (h t)")
                dve(tt(f2(qg), f2(qT), eq, ALU.mult), [qk, eqk], [qgk])
                pool(tt(f2(kg), f2(kT), ek, ALU.mult), [kk, ekk], [kgk])
                pool(tt(f2(kd), f2(kT), ed, ALU.mult), [kk, edk], [kdk])
                MI = BD_le if d == 0 else BD_ge
                heads = []
                for h in range(4):
                    c = d * 4 + h
                    at = atk = None
                    if islat:
                        pA, pAk = ps_alloc(1)
                        pe(mm(pA, kg[:, h, :], qg[:, h, :]), [kgk, qgk], pAk)
                        at, atk = h_at.next()
                        dve(tt(at, pA, MI, ALU.mult), pAk + ["masks"], [atk])
                    pk_, pkk = ps_alloc(1)
                    pe(tr(bfv(pk_), kd[:, h, :], identb), [kdk, "identb"], pkk)
                    kt, ktk = h_kt.next()
                    act(cp(kt, bfv(pk_)), pkk, [ktk])
                    pv_, pvk = ps_alloc(1)
                    pe(tr(bfv(pv_), iT[:, h, :], identb), [ik, "identb"], pvk)
                    vt, vtk = h_vt.next()
                    act(cp(vt, bfv(pv_)), pvk, [vtk])
                    heads.append(dict(h=h, c=c, at=at, atk=atk, kt=kt, ktk=ktk, vt=vt, vtk=vtk))
                prs.append(dict(T=T, d=d, islat=islat, qg=qg, qgk=qgk, gl=gl, glk=glk, ofi=ofi, ofik=ofik,
                                ofdefer=ofdefer, heads=heads))
            return prs

        def otb_view(j):
            def colap(h):
                return OTb[:, h, :].rearrange("p (r w) -> p r w", w=64)[:, :, 2 * j:2 * j + 2]
            return colap

        def hg_chain(preps, second):
            items = [(pr, hd) for pr in preps for hd in pr["heads"]]
            for pr in preps:
                if pr["ofdefer"]:
                    dma(pr["ofi"], OFb[pr["T"] - 2], [f"OFb{pr['T'] - 2}"], [pr["ofik"]])
            for ii, (pr, hd) in enumerate(items):
                if pr["islat"]:
                    hd["pO"], hd["pOk"] = PS[6 + ii // 4][:, (ii % 4) * 128:(ii % 4 + 1) * 128], [f"psb{6 + ii // 4}"]
            for sub in range(2):
                def rs(pr):
                    j = sub if pr["d"] == 0 else 1 - sub
                    return j, slice(64 * j, 64 * j + 64)
                for pr, hd in items:
                    c, h = hd["c"], hd["h"]
                    j, r = rs(pr)
                    if pr["islat"]:
                        pe(seq(mm(hd["pO"][r, :], pr["qg"][:, h, r], Sb[:, c, :], start=True, stop=False),
                               mm(hd["pO"][r, :], hd["at"][r, r], hd["vt"][r, :], start=False, stop=True)),
                           [pr["qgk"], f"Sb{c}", hd["atk"], hd["vtk"]], hd["pOk"])
                    pS, pSk = ps_alloc(1)
                    pe(mm(pS, hd["kt"][r, :], hd["vt"][r, :]), [hd["ktk"], hd["vtk"]], pSk)
                    hd["pS"], hd["pSk"] = pS, pSk
                for pr, hd in items:
                    c, h = hd["c"], hd["h"]
                    j, r = rs(pr)
                    dve(stt(Sst[:, c, :], Sst[:, c, :], pr["gl"][:, h * 2 + j:h * 2 + j + 1], hd["pS"], ALU.mult, ALU.add),
                        hd["pSk"] + [pr["glk"], f"S{c}"], [f"S{c}"])
                    act(cp(Sb[:, c, :], Sst[:, c, :]), [f"S{c}"], [f"Sb{c}"])
            for pr in preps:
                if not pr["islat"]:
                    continue
                j = pr["T"] - 2
                if not second:
                    ofo, ofok = r_ofo.next()
                for hd in pr["heads"]:
                    h = hd["h"]
                    if not second:
                        act(cp(ofo[:, h * 128:(h + 1) * 128], hd["pO"]), hd["pOk"], [ofok])
                    else:
                        ot, otk = r_f["ot"].next()
                        dve(tt(ot, hd["pO"], pr["ofi"][:, h * 128:(h + 1) * 128], ALU.add), hd["pOk"] + [pr["ofik"]], [otk])
                        finalize(ot, otk, "OTb", otb_view(j)(h), QSCALE,
                                 tview=lambda a: a.rearrange("p (a b) -> p a b", a=2).rearrange("p a b -> p b a"))
                if not second:
                    dma(OFb[j], ofo, [ofok], [f"OFb{j}"])

        hsteps = dbg.get("hg_steps", NT) if dbg else NT
        if not (dbg and dbg.get("skip_hg")):
            cur = hg_prep([(0, 0), (bwd_order[0], 1)], False)
            for i in range(hsteps):
                nxt = None
                if i + 1 < hsteps:
                    nxt = hg_prep([(i + 1, 0), (bwd_order[i + 1], 1)], (i + 1) >= 18)
                hg_chain(cur, second=(i >= 18))
                cur = nxt
        if dbg and dbg.get("dump_otb"):
            dma(dbg_otb, OTb, ["OTb"], ["dbg_otb"])
        P.barrier()
        st["off"] = ph3

        waoB = alloc([128, 4, 1024], BF16)
        wboB = alloc([128, 4, 1024], BF16)
        woB = alloc([128, 8, 1024], BF16)
        lng_row = alloc([128, 1024], F32)
        lnb_row = alloc([128, 1024], F32)
        Hr = Ring("oH", 3, [128, 1024], F32)
        xr4 = Ring("ox", 2, [128, 1024], F32)
        agr = Ring("oag", 1, [128, 4, 512], BF16)
        bgr = Ring("obg", 1, [128, 4, 512], BF16)
        mr = Ring("om", 4, [128, 2, 512], BF16)
        mar = Ring("oma", 1, [128, 4, 512], BF16)
        mbr = Ring("omb", 1, [128, 4, 512], BF16)
        t1r = Ring("ot1", 2, [128, 512], F32)
        t2r = Ring("ot2", 2, [128, 512], F32)
        mixr = Ring("omix", 2, [128, 8, 512], BF16)
        st4 = Ring("ost", 2, [128, 2, 6], F32)
        mv4 = Ring("omv", 2, [128, 4], F32)
        dma(lng_row, lng.partition_broadcast(128), [], ["lng_row"])
        dma(lnb_row, lnb.partition_broadcast(128), [], ["lnb_row"])
        for (wsrc, wdst, n, key) in ((wao, waoB, 4, "waoB"), (wbo, wboB, 4, "wboB"), (wo, woB, 8, "woB")):
            for i in range(n):
                hbuf, hk = Hr.next()
                dma(hbuf, wsrc[:, i, :], [], [hk])
                pool(cp(wdst[:, i, :], hbuf), [hk], [key])
        nblk = dbg.get("out_blocks", 8) if dbg else 8
        for bb in range(nblk):
            bsl = slice(bb * 512, (bb + 1) * 512)
            ag, agk = agr.next()
            bg, bgk = bgr.next()
            dma(ag, ZAG[:, :, bsl].rearrange("h p t -> p h t"), ["ZAG"], [agk])
            dma(bg, ZBGT[:, :, bsl].rearrange("h p t -> p h t"), ["ZBGT"], [bgk])
            ma, mak = mar.next()
            mb, mbk = mbr.next()
            dve(stt(ma, OTa[:, :, bsl], gains[:, 0:1], ag, ALU.mult, ALU.mult), ["OTa", "gains", agk], [mak])
            dve(stt(mb, OTb[:, :, bsl], gains[:, 1:2], bg, ALU.mult, ALU.mult), ["OTb", "gains", bgk], [mbk])
            mix, mixk = mixr.next()
            for cc in range(8):
                mt, mtk = mr.next()
                dma(mt[:, 0, :], ZM[cc, :, bsl], ["ZM"], [mtk])
                dma(mt[:, 1, :], ZM[8 + cc, :, bsl], ["ZM"], [mtk])
                pYa, pYak = ps_alloc(4)
                pe(seq(*[mm(pYa, waoB[:, h, cc * 128:(cc + 1) * 128], ma[:, h, :], start=(h == 0), stop=(h == 3))
                         for h in range(4)]), ["waoB", mak], pYak)
                pYb, pYbk = ps_alloc(4)
                pe(seq(*[mm(pYb, wboB[:, h, cc * 128:(cc + 1) * 128], mb[:, h, :], start=(h == 0), stop=(h == 3))
                         for h in range(4)]), ["wboB", mbk], pYbk)
                t1, t1k = t1r.next()
                t2, t2k = t2r.next()
                dve(tt(t1, pYa, mt[:, 0, :], ALU.mult), pYak + [mtk], [t1k])
                dve(tt(t2, pYb, mt[:, 1, :], ALU.mult), pYbk + [mtk], [t2k])
                pool(tt(mix[:, cc, :], t1, t2, ALU.add), [t1k, t2k], [mixk])
            for ti in range(4):
                lt = bb * 4 + ti
                xt, xk = xr4.next()
                dma(xt, x[lt * 128:(lt + 1) * 128, :], [], [xk])
                H, Hk = Hr.next()
                for half in range(2):
                    pSu, pSuk = ps_alloc(4)
                    pe(seq(*[mm(pSu, mix[:, cc, ti * 128:(ti + 1) * 128], woB[:, cc, half * 512:(half + 1) * 512],
                                start=(cc == 0), stop=(cc == 7)) for cc in range(8)]), [mixk, "woB"], pSuk)
                    dve(tt(H[:, half * 512:(half + 1) * 512], pSu, gate_row[:, half * 512:(half + 1) * 512], ALU.mult),
                        pSuk + ["gate_row"], [Hk])
                act(actf(xt, xt, AF.Copy, scale=ALPHA), [xk], [xk])
                pool(tt(H, H, xt, ALU.add), [Hk, xk], [Hk])
                stt_, stk = st4.next()
                mv, mvk = mv4.next()
                dve(seq(lambda e, a=stt_, b=H: e.bn_stats(out=a[:, 0, :], in_=b[:, 0:512]),
                        lambda e, a=stt_, b=H: e.bn_stats(out=a[:, 1, :], in_=b[:, 512:1024])), [Hk], [stk])
                dve(lambda e, a=mv, b=stt_: e.bn_aggr(out=a[:, 0:2], in_=b.rearrange("p a b -> p (a b)")), [stk], [mvk])
                act(actf(mv[:, 2:3], mv[:, 1:2], AF.Sqrt, bias=EPS), [mvk], [mvk + "s"])
                dve(lambda e, a=mv: e.reciprocal(out=a[:, 2:3], in_=a[:, 2:3]), [mvk + "s"], [mvk + "s"])
                dve(stt(mv[:, 3:4], mv[:, 0:1], -1.0, mv[:, 2:3], ALU.mult, ALU.mult), [mvk, mvk + "s"], [mvk + "n"])
                act(actf(H, H, AF.Identity, bias=mv[:, 3:4], scale=mv[:, 2:3]), [Hk, mvk + "s", mvk + "n"], [Hk])
                dve(tt(H, H, lng_row, ALU.mult), [Hk, "lng_row"], [Hk])
                pool(tt(H, H, lnb_row, ALU.add), [Hk, "lnb_row"], [Hk])
                dma(y[lt * 128:(lt + 1) * 128, :], H, [Hk], ["y"])

        final_cnt = dict(P.cnt)

        @block.sync
        def _(e):
            P.replay('sp', e, sems)
            for l, n in final_cnt.items():
                if l[0] == 'd' and l != 'dve':
                    e.wait_ge(sems[l], 16 * n)

        @block.tensor
        def _(e):
            P.replay('pe', e, sems)

        @block.scalar
        def _(e):
            P.replay('act', e, sems)

        @block.vector
        def _(e):
            P.replay('dve', e, sems)

        @block.gpsimd
        def _(e):
            P.replay('pool', e, sems)
    return nc, P


def _consts():
    p = np.arange(128)[:, None]
    f = np.arange(128)[None, :]
    same = (p // 64) == (f // 64)
    m = np.stack([p <= f, p < f, p >= f, p > f, (p <= f) & same, (p < f) & same, (p >= f) & same, (p > f) & same], axis=1).astype(np.float32)
    segm = np.ones((128, 512), np.float32)
    segm[:, ::64] = 0.0
    return np.eye(128, dtype=np.float32), np.ascontiguousarray(m), segm


def make_in_maps(inp):
    f = np.float32
    A = lambda a: np.ascontiguousarray(a, dtype=f)
    w_in = inp['w_in'][0]
    cols = np.r_[0:1536, 1552:6672]
    win = A(w_in[:, cols].reshape(8, 128, 52, 128).transpose(2, 1, 0, 3))
    wab = A(w_in[:, 1536:1552].reshape(8, 128, 16).transpose(1, 0, 2))
    w_mod = inp['w_mod'][0]
    wmod = A(w_mod.reshape(8, 128, 6, 512).transpose(2, 1, 0, 3))
    b_mod = inp['b_mod'][0]
    bmodc = A(b_mod[:2048].reshape(16, 128).T)
    bmodg = A(b_mod[2048:].reshape(1, 1024))
    convw = A(inp['conv_w'][0].reshape(5, 12, 128).transpose(2, 1, 0))
    alog = A(inp['a_log'][0].reshape(1, 8))
    dtb = A(inp['dt_bias'][0].reshape(1, 8))
    lbp = A(inp['lb_param'].reshape(2, 2, 4, 128).transpose(3, 0, 1, 2).reshape(128, 2, 8))
    ang = A(inp['a_norm_g'][0].reshape(128, 1))
    bng = A(inp['b_norm_g'][0].reshape(128, 1))
    wao = A(inp['w_a_out'][0].reshape(4, 128, 1024).transpose(1, 0, 2))
    wbo = A(inp['w_b_out'][0].reshape(4, 128, 1024).transpose(1, 0, 2))
    wo = A(inp['w_out'][0].reshape(8, 128, 1024).transpose(1, 0, 2))
    lng = A(inp['ln_g'][0].reshape(1, 1024))
    lnb = A(inp['ln_b'][0].reshape(1, 1024))
    cidf, cmask, csegm = _consts()
    maps = []
    for b in range(8):
        ccv = np.stack([inp['c'][b].reshape(8, 128).T, inp['c_ctx'].reshape(8, 128).T], axis=2)
        maps.append(dict(x=A(inp['x'][b]), ctx=A(inp['ctx'][b]), cc=A(ccv), wmod=wmod, bmodc=bmodc, bmodg=bmodg,
                         win=win, wab=wab, convw=convw, alog=alog, dtb=dtb, lbp=lbp, ang=ang, bng=bng,
                         wao=wao, wbo=wbo, wo=wo, lng=lng, lnb=lnb, cidf=cidf, cmask=cmask, csegm=csegm))
    return maps


def kernel(**inputs):
    nc, _ = build()
    maps = make_in_maps(inputs)
    res = run_bass_kernel_spmd(nc, maps, core_ids=list(range(8)))
    return np.stack([np.asarray(r["y"], dtype=np.float32) for r in res.results], axis=0)
```

```python
import numpy as np
import ml_dtypes
from contextlib import ExitStack
import concourse.bass as bass
import concourse.mybir as mybir
from concourse.bass_utils import run_bass_kernel_spmd

F32 = mybir.dt.float32
BF16 = mybir.dt.bfloat16
U8 = mybir.dt.uint8
AF = mybir.ActivationFunctionType
ALU = mybir.AluOpType

NT = 34
NTOK = 4352
QSCALE = 128 ** -0.5
ALPHA = 2.0 ** 0.25
EPS = 1e-6


class Prog:
    ENG = ('pe', 'act', 'dve', 'pool', 'sp')

    def __init__(self):
        self.ops = {e: [] for e in self.ENG}
        self.cnt = {}
        self.know = {e: {} for e in self.ENG}
        self.opclock = {}
        self.lastw = {}
        self.readers = {}
        self.pending = {e: {} for e in self.ENG}
        self.nlanes = {'sp': 8, 'pool': 6, 'act': 4}
        self.rr = {e: 0 for e in self.ENG}

    def lanes(self):
        out = list(self.ENG[:4])
        for q, n in self.nlanes.items():
            out += [f'd{q}{i}' for i in range(n)]
        return out

    def barrier(self):
        snap = dict(self.cnt)
        for e in self.ENG:
            p = self.pending[e]
            for l, n in snap.items():
                if p.get(l, 0) < n:
                    p[l] = n

    def emit(self, eng, fn, reads=(), writes=(), dma=False):
        if dma:
            i = self.rr[eng]
            self.rr[eng] = (i + 1) % self.nlanes[eng]
            lane = f'd{eng}{i}'
        else:
            lane = eng
        psr = [r for r in reads if r.startswith('psb')]
        if psr:
            reads = [r for r in reads if not r.startswith('psb')]
            writes = list(writes) + psr
        deps = dict(self.pending[eng])
        self.pending[eng] = {}

        def add(l, n):
            if deps.get(l, 0) < n:
                deps[l] = n
        for r in reads:
            w = self.lastw.get(r)
            if w:
                add(*w)
        for r in writes:
            w = self.lastw.get(r)
            if w and not (w[0] == lane and not dma):
                add(*w)
            for l, n in self.readers.get(r, {}).items():
                if not (l == lane and not dma):
                    add(l, n)
        if dma and self.cnt.get(lane, 0) > 0:
            add(lane, self.cnt[lane])
        know = self.know[eng]
        waits = [(l, n) for l, n in deps.items() if know.get(l, 0) < n]
        for l, n in deps.items():
            for l2, n2 in self.opclock.get((l, n), {}).items():
                if know.get(l2, 0) < n2:
                    know[l2] = n2
            if know.get(l, 0) < n:
                know[l] = n
        n = self.cnt.get(lane, 0) + 1
        self.cnt[lane] = n
        self.opclock[(lane, n)] = dict(know)
        self.ops[eng].append((waits, fn, lane))
        for r in reads:
            self.readers.setdefault(r, {})[lane] = n
        for r in writes:
            self.lastw[r] = (lane, n)
            self.readers[r] = {}

    def replay(self, name, eng, sems):
        for waits, fn, lane in self.ops[name]:
            for l, n in waits:
                eng.wait_ge(sems[l], n * (16 if l[0] == 'd' and l != 'dve' else 1))
            inst = fn(eng)
            inst.then_inc(sems[lane], 16 if (lane[0] == 'd' and lane != 'dve') else 1)


def seq(*fns):
    def f(e):
        r = None
        for g in fns:
            r = g(e)
        return r
    return f


def build(dbg=None):
    nc = bass.Bass("TRN2", target_bir_lowering=False)
    P = Prog()

    def din(name, shape, dt=F32):
        return nc.dram_tensor(name, list(shape), dt, kind="ExternalInput").ap()

    x = din("x", [4096, 1024])
    ctx = din("ctx", [256, 1024])
    cc = din("cc", [128, 8, 2])
    wmod = din("wmod", [6, 128, 8, 512])
    bmodc = din("bmodc", [128, 16])
    bmodg = din("bmodg", [1, 1024])
    win = din("win", [52, 128, 8, 128])
    wab = din("wab", [128, 8, 16])
    convw = din("convw", [128, 12, 5])
    alog = din("alog", [1, 8])
    dtb = din("dtb", [1, 8])
    lbp = din("lbp", [128, 2, 8])
    ang = din("ang", [128, 1])
    bng = din("bng", [128, 1])
    wao = din("wao", [128, 4, 1024])
    wbo = din("wbo", [128, 4, 1024])
    wo = din("wo", [128, 8, 1024])
    lng = din("lng", [1, 1024])
    lnb = din("lnb", [1, 1024])
    cidf = din("cidf", [128, 128])
    cmask = din("cmask", [128, 8, 128])
    csegm = din("csegm", [128, 512])
    y = nc.dram_tensor("y", [4096, 1024], F32, kind="ExternalOutput").ap()

    def dscr(name, shape, dt):
        kind = {"kind": "ExternalOutput"} if (dbg and name in dbg) else {}
        return nc.dram_tensor(name, list(shape), dt, **kind).ap()

    ZQ = dscr("ZQ", [4, 128, NTOK], BF16)
    ZK = dscr("ZK", [4, 128, NTOK], BF16)
    ZV = dscr("ZV", [4, 128, NTOK], BF16)
    ZAG = dscr("ZAG", [4, 128, 4096], BF16)
    ZBGT = dscr("ZBGT", [4, 128, 4096], BF16)
    ZM = dscr("ZM", [16, 128, 4096], BF16)
    ZBQ = dscr("ZBQ", [4, 128, NTOK], BF16)
    ZBK = dscr("ZBK", [8, 128, NTOK], BF16)
    ZBL = dscr("ZBL", [8, 128, NTOK], F32)
    ZBI = dscr("ZBI", [4, 128, NTOK], BF16)
    OFa = dscr("OFa", [32, 128, 512], F32)
    OFb = dscr("OFb", [32, 128, 512], F32)
    DBG = dscr("DBG", [128, 8192], F32) if dbg else None
    dbg_ota = dscr("dbg_ota", [128, 4, 4096], BF16) if dbg else None
    dbg_otb = dscr("dbg_otb", [128, 4, 4096], BF16) if dbg else None

    es = ExitStack()
    with es:
        ARENA = 204 * 1024
        arena = es.enter_context(nc.sbuf_tensor("arena", [128, ARENA], U8))
        PS = [es.enter_context(nc.psum_tensor(f"ps{i}", [128, 512], F32)) for i in range(8)]
        sems = {l: es.enter_context(nc.semaphore(f"s_{l}")) for l in P.lanes()}
        block = es.enter_context(nc.Block())

        st = {"off": 0, "n": 0}

        def alloc(shape, dt, name=None):
            nb = int(np.prod(shape[1:])) * (4 if dt == F32 else 2)
            off = (st["off"] + 63) // 64 * 64
            assert off + nb <= ARENA, (name, off, nb)
            st["off"] = off + nb
            ap = arena[:, off:off + nb].bitcast(dt)
            if len(shape) == 3:
                ap = ap.rearrange("p (a b) -> p a b", a=shape[1])
            elif len(shape) == 4:
                ap = ap.rearrange("p (a b c) -> p a b c", a=shape[1], b=shape[2])
            st["n"] += 1
            return ap

        class Ring:
            def __init__(self, name, n, shape, dt):
                self.name = name
                self.aps = [alloc(shape, dt, name) for _ in range(n)]
                self.i = 0

            def next(self):
                i = self.i
                self.i = (i + 1) % len(self.aps)
                return self.aps[i], f"{self.name}#{i}"

        psst = {"i": 0, "q": [0] * 8}

        def ps_alloc(nq=1):
            b = psst["i"] % 6
            psst["i"] += 1
            if nq == 4:
                s_ = 0
            else:
                s_ = psst["q"][b]
                psst["q"][b] = (s_ + nq) % 4
                assert s_ + nq <= 4
            ap = PS[b][:, s_ * 128:(s_ + nq) * 128]
            return ap, [f"psb{b}"]

        def bfv(ap):
            return ap.bitcast(BF16)[:, 0:128]

        STQ = 'pool'
        pe = lambda fn, r, w: P.emit('pe', fn, r, w)
        act = lambda fn, r, w: P.emit('act', fn, r, w)
        dve = lambda fn, r, w: P.emit('dve', fn, r, w)
        pool = lambda fn, r, w: P.emit('pool', fn, r, w)

        def dma(out, in_, r, w, q='sp'):
            P.emit(q, lambda e: e.dma_start(out=out, in_=in_), r, w, dma=True)

        def dstore(out, in_, r, w):
            dma(out, in_, r, w, q=STQ)

        def mm(out, lhsT, rhs, start=True, stop=True):
            return lambda e: e.matmul(out, lhsT=lhsT, rhs=rhs, start=start, stop=stop)

        def tr(out, in_, ident):
            return lambda e: e.transpose(out=out, in_=in_, identity=ident)

        def actf(out, in_, func, bias=None, scale=None, accum_out=None):
            kw = {}
            if bias is not None:
                kw["bias"] = bias
            if scale is not None:
                kw["scale"] = scale
            if accum_out is not None:
                kw["accum_out"] = accum_out
            return lambda e: e.activation(out=out, in_=in_, func=func, **kw)

        def tt(out, in0, in1, op):
            return lambda e: e.tensor_tensor(out=out, in0=in0, in1=in1, op=op)

        def ts(out, in0, s1, s2, op0, op1=None):
            if op1 is None:
                return lambda e: e.tensor_scalar(out=out, in0=in0, scalar1=s1, scalar2=None, op0=op0)
            return lambda e: e.tensor_scalar(out=out, in0=in0, scalar1=s1, scalar2=s2, op0=op0, op1=op1)

        def stt(out, in0, scalar, in1, op0, op1):
            return lambda e: e.scalar_tensor_tensor(out=out, in0=in0, scalar=scalar, in1=in1, op0=op0, op1=op1)

        def cp(out, in_):
            return lambda e: (e.tensor_copy(out=out, in_=in_) if hasattr(e, 'tensor_copy') else e.activation(out=out, in_=in_, func=AF.Copy))

        identf = alloc([128, 128], F32)
        identb = alloc([128, 128], BF16)
        masks = alloc([128, 8, 128], F32)
        onesf = alloc([128, 128], F32)
        segm = alloc([128, 512], F32)
        M_le, M_lt, M_ge, M_gt, BD_le, BD_lt, BD_ge, BD_gt = [masks[:, i, :] for i in range(8)]
        gate_row = alloc([128, 1024], F32)
        GB = alloc([128, NT, 16], F32)
        modc = alloc([128, 16, 2], F32)
        lbc = alloc([128, 8], F32)
        omlb = alloc([128, 8], F32)
        nomlb = alloc([128, 8], F32)
        gains = alloc([128, 2], F32)
        cw = alloc([128, 12, 5], F32)
        rowc = alloc([128, 16], F32)
        persist_off = st["off"]

        dma(identf, cidf, [], ["identf"])
        dma(masks, cmask, [], ["masks"])
        dma(segm, csegm, [], ["segm"])
        dma(cw, convw, [], ["cw"])
        dma(gains[:, 0:1], ang, [], ["gains"])
        dma(gains[:, 1:2], bng, [], ["gains"])
        dma(rowc[:, 0:8], alog.partition_broadcast(128), [], ["rowc"])
        dma(rowc[:, 8:16], dtb.partition_broadcast(128), [], ["rowc"])
        dve(cp(identb, identf), ["identf"], ["identb"])
        pool(lambda e: e.memset(onesf, 1.0), [], ["onesf"])
        act(actf(rowc[:, 0:8], rowc[:, 0:8], AF.Exp), ["rowc"], ["rowc"])
        dve(ts(rowc[:, 0:8], rowc[:, 0:8], -1.0, None, ALU.mult), ["rowc"], ["rowc"])

        ph0 = st["off"]
        cct = alloc([128, 8, 2], F32)
        sil = alloc([128, 8, 2], F32)
        srep = alloc([128, 8, 128], F32)
        bmc = alloc([128, 16], F32)
        lbt = alloc([128, 2, 8], F32)
        wmr = Ring("wm", 2, [128, 8, 512], F32)
        dma(cct, cc, [], ["cct"])
        dma(bmc, bmodc, [], ["bmc"])
        dma(lbt, lbp, [], ["lbt"])
        dma(gate_row, bmodg.partition_broadcast(128), [], ["gate_row"])
        act(actf(sil, cct, AF.Silu), ["cct"], ["sil"])
        dve(cp(srep, sil[:, :, 0:1].to_broadcast([128, 8, 128])), ["sil"], ["srep"])
        dve(tt(lbc, lbt[:, 0, :], lbt[:, 1, :], ALU.subtract), ["lbt"], ["lbc"])
        act(actf(lbc, lbc, AF.Sigmoid), ["lbc"], ["lbc"])
        dve(ts(omlb, lbc, -1.0, 1.0, ALU.mult, ALU.add), ["lbc"], ["omlb"])
        dve(ts(nomlb, lbc, -1.0, None, ALU.add), ["lbc"], ["nomlb"])
        for blk in range(6):
            wt, wk = wmr.next()
            dma(wt, wmod[blk], [], [wk])
            if blk < 4:
                for jj in range(4):
                    j = blk * 4 + jj
                    pt, pk = ps_alloc(1)
                    pe(seq(*[mm(pt[:, 0:2], wt[:, kc, jj * 128:(jj + 1) * 128], sil[:, kc, :],
                                start=(kc == 0), stop=(kc == 7)) for kc in range(8)]),
                       [wk, "sil"], pk)
                    dve(ts(modc[:, j, :], pt[:, 0:2], bmc[:, j:j + 1], 1.0 if j >= 8 else 0.0, ALU.add, ALU.add),
                        pk + ["bmc"], ["modc"])
            else:
                pt, pk = ps_alloc(4)
                pe(seq(*[mm(pt, srep[:, kc, :], wt[:, kc, :], start=(kc == 0), stop=(kc == 7))
                         for kc in range(8)]), [wk, "srep"], pk)
                gs = gate_row[:, (blk - 4) * 512:(blk - 3) * 512]
                dve(tt(gs, gs, pt, ALU.add), pk + ["gate_row"], ["gate_row"])
        P.barrier()
        st["off"] = ph0

        uT = alloc([128, 8, NTOK], BF16)
        ph2 = st["off"]
        xr = Ring("xt", 2, [128, 1024], F32)
        xnr = Ring("xn", 2, [128, 1024], BF16)
        str_ = Ring("st", 2, [128, 2, 6], F32)
        mvr = Ring("mv", 2, [128, 4], F32)
        for T in range(NT):
            src = ctx[T * 128:(T + 1) * 128, :] if T < 2 else x[(T - 2) * 128:(T - 1) * 128, :]
            w = 1 if T < 2 else 0
            xt, xk = xr.next()
            xn, xnk = xnr.next()
            stt_, stk = str_.next()
            mv, mvk = mvr.next()
            dma(xt, src, [], [xk])
            dve(seq(lambda e, a=stt_, b=xt: e.bn_stats(out=a[:, 0, :], in_=b[:, 0:512]),
                    lambda e, a=stt_, b=xt: e.bn_stats(out=a[:, 1, :], in_=b[:, 512:1024])), [xk], [stk])
            dve(lambda e, a=mv, b=stt_: e.bn_aggr(out=a[:, 0:2], in_=b.rearrange("p a b -> p (a b)")), [stk], [mvk])
            act(actf(mv[:, 2:3], mv[:, 1:2], AF.Sqrt, bias=EPS), [mvk], [mvk + "s"])
            dve(lambda e, a=mv: e.reciprocal(out=a[:, 2:3], in_=a[:, 2:3]), [mvk + "s"], [mvk + "s"])
            dve(stt(mv[:, 3:4], mv[:, 0:1], -1.0, mv[:, 2:3], ALU.mult, ALU.mult), [mvk, mvk + "s"], [mvk + "n"])
            act(actf(xn, xt, AF.Identity, bias=mv[:, 3:4], scale=mv[:, 2:3]), [xk, mvk + "s", mvk + "n"], [xnk])
            pt, pk = ps_alloc(4)
            ptb = pt.bitcast(BF16)
            pe(seq(*[tr(ptb[:, kc * 128:(kc + 1) * 128], xn[:, kc * 128:(kc + 1) * 128], identb) for kc in range(8)]),
               [xnk, "identb"], pk)
            for kc in range(8):
                o = uT[:, kc, T * 128:(T + 1) * 128]
                i = ptb[:, kc * 128:(kc + 1) * 128]
                if kc % 2 == 0:
                    act(actf(o, i, AF.Identity, bias=modc[:, kc, w:w + 1], scale=modc[:, 8 + kc, w:w + 1]),
                        pk + ["modc"], [f"uT{T}"])
                else:
                    dve(ts(o, i, modc[:, 8 + kc, w:w + 1], modc[:, kc, w:w + 1], ALU.mult, ALU.add),
                        pk + ["modc"], [f"uT{T}"])
        uT_all = [f"uT{T}" for T in range(NT)]

        wfr = Ring("wf", 2, [128, 8, 128], F32)
        wbr = Ring("wb", 4, [128, 8, 128], BF16)
        ZL = 4360
        bufA = alloc([128, ZL], F32)
        bufB = alloc([128, ZL], F32)
        bufC = alloc([128, ZL], F32)
        stg = Ring("stg", 3, [128, NTOK], BF16)
        wabf = alloc([128, 8, 16], F32)
        wabb = alloc([128, 8, 16], BF16)
        tmpab = alloc([128, NT, 8], F32)
        tmpab2 = alloc([128, NT, 8], F32)
        ssr = Ring("ss", 2, [128, 512], F32)

        dma(wabf, wab, [], ["wabf"])
        pool(cp(wabb, wabf), ["wabf"], ["wabb"])
        for T in range(NT):
            pt, pk = ps_alloc(1)
            pe(seq(*[mm(pt[:, 0:16], uT[:, kc, T * 128:(T + 1) * 128], wabb[:, kc, :], start=(kc == 0), stop=(kc == 7))
                     for kc in range(8)]), [f"uT{T}", "wabb"], pk)
            dve(cp(GB[:, T, :], pt[:, 0:16]), pk, ["GBraw"])
        a_ = tmpab
        b_ = tmpab2
        dve(tt(a_, GB[:, :, 0:8], rowc[:, 8:16].unsqueeze(1).to_broadcast([128, NT, 8]), ALU.add), ["GBraw", "rowc"], ["tmpab"])
        dve(stt(b_, a_, -1.0, a_, ALU.mult, ALU.max), ["tmpab"], ["tmpab2"])
        act(actf(b_, b_, AF.Exp, scale=-1.0), ["tmpab2"], ["tmpab2"])
        act(actf(b_, b_, AF.Ln, bias=1.0), ["tmpab2"], ["tmpab2"])
        dve(stt(a_, a_, 0.0, b_, ALU.max, ALU.add), ["tmpab", "tmpab2"], ["tmpab"])
        dve(tt(GB[:, :, 0:8], a_, rowc[:, 0:8].unsqueeze(1).to_broadcast([128, NT, 8]), ALU.mult), ["tmpab", "rowc", "GBraw"], ["GBg"])
        act(actf(GB[:, :, 8:16], GB[:, :, 8:16], AF.Sigmoid), ["GBraw"], ["GBb"])
        GBk = ["GBg", "GBb"]

        blocks = [(0, 256)] + [(256 + i * 512, 512) for i in range(8)]

        def cm_view(buf, r0, nr=8):
            v = buf[:, 256:256 + 4096].rearrange("p (w r) -> p w r", r=64)[:, :, r0:r0 + nr]
            return v.rearrange("p w r -> p r w")

        def ps_rw(pt):
            return pt.rearrange("p (r w) -> p r w", w=64)

        wpre = {}

        def prefetch_w(j):
            if j in wpre:
                return
            wf, wfk = wfr.next()
            wb, wbk = wbr.next()
            dma(wf, win[j], [], [wfk])
            pool(cp(wb, wf), [wfk], [wbk])
            wpre[j] = (wb, wbk)

        def project(j, lat_only, epi):
            prefetch_w(j)
            wb, wbk = wpre[j]
            for bi, (c0, n) in enumerate(blocks):
                if lat_only and bi == 0:
                    continue
                pt, pk = ps_alloc(4)
                T0 = c0 // 128
                pe(seq(*[mm(pt[:, 0:n], wb[:, kc, :], uT[:, kc, c0:c0 + n], start=(kc == 0), stop=(kc == 7))
                         for kc in range(8)]), [wbk] + [f"uT{T0 + i}" for i in range(n // 128)], pk)
                epi(bi, c0, n, pt, pk)

        def zpos(c0):
            return c0 + 2 if c0 < 256 else c0 + 6

        pool(lambda e: e.memset(bufA, 0.0), [], ["bufA"])

        def conv_chunk(j, kind, h):
            def epi(bi, c0, n, pt, pk):
                z0 = zpos(c0)
                act(cp(bufA[:, z0:z0 + n], pt[:, 0:n]) if False else actf(bufA[:, z0:z0 + n], pt[:, 0:n], AF.Identity), pk, ["bufA"])
            project(j, False, epi)
            L = 4356
            dve(ts(bufB[:, 0:L], bufA[:, 0:L], cw[:, j, 0:1], None, ALU.mult), ["bufA", "cw"], ["bufB"])
            for k in range(1, 5):
                dve(stt(bufB[:, 0:L], bufA[:, k:k + L], cw[:, j, k:k + 1], bufB[:, 0:L], ALU.mult, ALU.add),
                    ["bufA", "bufB", "cw"], ["bufB"])
            return lambda: conv_part2(j, kind, h)

        def conv_part2(j, kind, h):
            L = 4356
            sg, sgk = stg.next()
            if kind == 'v':
                act(actf(sg[:, 0:256], bufB[:, 0:256], AF.Silu), ["bufB"], [sgk])
                act(actf(sg[:, 256:NTOK], bufB[:, 260:4356], AF.Silu), ["bufB"], [sgk])
                dstore(ZV[h], sg, [sgk], ["ZV"])
                return
            act(actf(bufC[:, 0:L], bufB[:, 0:L], AF.Silu), ["bufB"], ["bufC"])
            pool(tt(bufB[:, 0:L], bufC[:, 0:L], bufC[:, 0:L], ALU.mult), ["bufC", "bufB"], ["bufB"])
            for (c0, n) in blocks:
                a0 = c0 if c0 < 256 else c0 + 4
                pt, pk = ps_alloc(4)
                pe(mm(pt[:, 0:n], onesf, bufB[:, a0:a0 + n]), ["onesf", "bufB"], pk)
                ss, ssk = ssr.next()
                act(actf(ss[:, 0:n], pt[:, 0:n], AF.Sqrt, bias=EPS), pk, [ssk])
                dve(lambda e, a=ss, n=n: e.reciprocal(out=a[:, 0:n], in_=a[:, 0:n]), [ssk], [ssk])
                dve(tt(sg[:, c0:c0 + n], bufC[:, a0:a0 + n], ss[:, 0:n], ALU.mult), [ssk, "bufC"], [sgk])
            dstore((ZQ if kind == 'q' else ZK)[h], sg, [sgk], ["ZQ" if kind == 'q' else "ZK"])

        def simple_chunk(j, func, dst, dkey, lat_only, cmo):
            sg, sgk = stg.next()

            def epi(bi, c0, n, pt, pk):
                if lat_only:
                    o = sg[:, c0 - 256:c0 - 256 + n]
                    i = pt[:, 0:n]
                elif bi == 0 or not cmo:
                    o = sg[:, c0:c0 + n]
                    i = pt[:, 0:n]
                else:
                    o = cm_view(sg, (c0 - 256) // 64)
                    i = ps_rw(pt)
                act(actf(o, i, func), pk, [sgk])
            project(j, lat_only, epi)
            if lat_only:
                dstore(dst, sg[:, 0:4096], [sgk], [dkey])
            else:
                dstore(dst, sg, [sgk], [dkey])

        def f_chunk(j, d, h):
            def epi(bi, c0, n, pt, pk):
                if bi == 0:
                    o = bufA[:, c0:c0 + n]
                    i = pt[:, 0:n]
                else:
                    o = cm_view(bufA, (c0 - 256) // 64)
                    i = ps_rw(pt)
                act(actf(o, i, AF.Sigmoid), pk, ["bufA"])
            project(j, False, epi)
            return lambda: f_part2(j, d, h)

        def f_part2(j, d, h):
            c = d * 4 + h
            dve(ts(bufB[:, 0:NTOK], bufA[:, 0:NTOK], omlb[:, c:c + 1], lbc[:, c:c + 1], ALU.mult, ALU.add),
                ["bufA", "omlb", "lbc"], ["bufB"])
            act(actf(bufC[:, 0:NTOK], bufB[:, 0:NTOK], AF.Ln), ["bufB"], ["bufC"])
            dstore(ZBL[c], bufC[:, 0:NTOK], ["bufC"], ["ZBL"])
            sg, sgk = stg.next()
            pool(ts(sg, bufA[:, 0:NTOK], nomlb[:, c:c + 1], omlb[:, c:c + 1], ALU.mult, ALU.add),
                 ["bufA", "nomlb", "omlb"], [sgk])
            dstore(ZBK[c], sg, [sgk], ["ZBK"])

        def do_chunk(j):
            if j < 4:
                return conv_chunk(j, 'q', j)
            elif j < 8:
                return conv_chunk(j, 'k', j - 4)
            elif j < 12:
                return conv_chunk(j, 'v', j - 8)
            elif j < 16:
                return simple_chunk(j, AF.Silu, ZAG[j - 12], "ZAG", True, False)
            elif j < 20:
                return simple_chunk(j, AF.Silu, ZBQ[j - 16], "ZBQ", False, True)
            elif j < 24:
                return f_chunk(j, 0, j - 20)
            elif j < 28:
                return f_chunk(j, 1, j - 24)
            elif j < 32:
                return simple_chunk(j, AF.Identity, ZBI[j - 28], "ZBI", False, True)
            elif j < 36:
                return simple_chunk(j, AF.Silu, ZBGT[j - 32], "ZBGT", True, False)
            else:
                return simple_chunk(j, AF.Sigmoid, ZM[j - 36], "ZM", True, False)

        if dbg and "chunks" in dbg:
            for j in dbg["chunks"]:
                p2_ = do_chunk(j)
                if p2_:
                    p2_()
        else:
            heavy = list(range(0, 12)) + list(range(20, 28))
            simple = list(range(12, 20)) + list(range(28, 52))
            order = []
            si = 0
            for j in heavy:
                order.append(("a", j))
                for _ in range(2 if j < 12 else 1):
                    if si < len(simple):
                        order.append(("s", simple[si]))
                        si += 1
                order.append(("b", j))
            while si < len(simple):
                order.append(("s", simple[si]))
                si += 1
            projs = [j for (k, j) in order if k != "b"]
            pending2 = {}
            pi = 0
            prefetch_w(projs[0])
            prefetch_w(projs[1])
            for (k, j) in order:
                if k == "b":
                    pending2.pop(j)()
                    continue
                if pi + 2 < len(projs):
                    prefetch_w(projs[pi + 2])
                pi += 1
                r_ = do_chunk(j)
                if k == "a":
                    pending2[j] = r_
        if dbg and dbg.get("dump_gb"):
            dma(DBG[:, 0:NT * 16], GB.rearrange("p a b -> p (a b)"), GBk, ["DBG"])
            dma(DBG[:, 1024:1024 + 32], modc.rearrange("p a b -> p (a b)"), ["modc"], ["DBG"])
            dma(DBG[:, 2048:3072], gate_row, ["gate_row"], ["DBG"])
        P.barrier()
        st["off"] = persist_off


        OTa = alloc([128, 4, 4096], BF16)
        OTb = alloc([128, 4, 4096], BF16)
        Sst = alloc([128, 8, 128], F32)
        Sb = alloc([128, 8, 128], BF16)
        ph3 = st["off"]
        r_q = Ring("gq", 4, [128, 4, 128], BF16)
        r_k = Ring("gk", 4, [128, 4, 128], BF16)
        r_v = Ring("gv", 4, [128, 4, 128], BF16)
        r_ex = Ring("gex", 6, [128, 16], F32)
        r_nb = Ring("gnb", 6, [128, 4], F32)
        r_gm = Ring("ggm", 4, [128, 8], F32)
        r_f = {n: Ring("g" + n, k, [128, 4, 128], F32) for n, k in
               (("M1", 2), ("E", 2), ("Es", 2), ("B", 2), ("BT", 2), ("X", 4), ("P", 4), ("PT", 4),
                ("Ub", 4), ("tmp", 2), ("ot", 2))}
        r_f["o1"] = Ring("go1", 4, [128, 128], F32)
        r_f["Ei"] = r_f["M1"]
        r_b = {n: Ring("g" + n, k, [128, 4, 128], BF16) for n, k in
               (("at", 4), ("Xb", 2), ("Kg", 2), ("Kd", 4), ("vt", 2), ("WT", 4), ("vn", 2))}
        r_b["on"] = Ring("gon", 8, [128, 128], BF16)
        r_s = Ring("gss", 8, [128, 4], F32)
        r_ofo = Ring("ofo", 2, [128, 512], F32)
        r_ofi = Ring("ofi", 4, [128, 512], F32)

        def finalize(ot, otk, OTkey, colap, scale, tview=None):
            ss, ssk = r_s.next()
            jk, jkk = r_f["o1"].next()
            act(actf(jk, ot, AF.Square, accum_out=ss[:, 0:1]), [otk], [jkk, ssk])
            dve(ts(ss[:, 1:2], ss[:, 0:1], scale * scale / 128.0, EPS, ALU.mult, ALU.add), [ssk], [ssk + "b"])
            act(actf(ss[:, 1:2], ss[:, 1:2], AF.Sqrt), [ssk + "b"], [ssk + "b"])
            dve(lambda e, a=ss: e.reciprocal(out=a[:, 1:2], in_=a[:, 1:2]), [ssk + "b"], [ssk + "b"])
            on, onk = r_b["on"].next()
            dve(ts(on, ot, ss[:, 1:2], scale, ALU.mult, ALU.mult), [otk, ssk + "b"], [onk])
            pt, pk = ps_alloc(1)
            pe(tr(bfv(pt), on, identb), [onk, "identb"], pk)
            if tview is None:
                act(cp(colap, bfv(pt)), pk, [OTkey])
            else:
                for w2 in range(2):
                    act(cp(colap[:, :, w2], bfv(pt)[:, w2 * 64:(w2 + 1) * 64]), pk, [OTkey])

        pool(lambda e: e.memset(Sst, 0.0), [], ["S0", "S1"])
        pool(lambda e: e.memset(Sb, 0.0), [], ["Sb0", "Sb1"])

        def bc4(col4):
            return col4.unsqueeze(2).to_broadcast([128, 4, 128])

        def mb4(m):
            return m.unsqueeze(1).to_broadcast([128, 4, 128])

        def v4(ap):
            return ap.rearrange("p (h v) -> p h v", h=4)

        def gdn_prep(pairs, second):
            prs = []
            for (T, d) in pairs:
                islat = T >= 2
                kT, kk = r_k.next()
                vT, vk = r_v.next()
                tsl = slice(T * 128, (T + 1) * 128)
                dma(kT, ZK[:, :, tsl].rearrange("h p t -> p h t"), ["ZK"], [kk])
                dma(vT, ZV[:, :, tsl].rearrange("h p t -> p h t"), ["ZV"], [vk])
                qT = qk = None
                if islat:
                    qT, qk = r_q.next()
                    dma(qT, ZQ[:, :, tsl].rearrange("h p t -> p h t"), ["ZQ"], [qk])
                ofi = ofik = None
                ofdefer = False
                if islat and second:
                    ofi, ofik = r_ofi.next()
                    ofdefer = f"OF{T - 2}" not in P.lastw
                    if not ofdefer:
                        dma(ofi, OFa[T - 2], [f"OF{T - 2}"], [ofik])
                ML, MR, MS = (BD_le, BD_gt, BD_lt) if d == 0 else (BD_ge, BD_lt, BD_gt)
                g4 = GB[:, T, d * 4:(d + 1) * 4]
                b4 = GB[:, T, 8 + d * 4:8 + (d + 1) * 4]
                pc, pck = ps_alloc(1)
                gm, gmk = r_gm.next()
                pool(ts(gm[:, 0:4], g4, BD_le[:, 63:64], None, ALU.mult), GBk + ["masks"], [gmk])
                pool(ts(gm[:, 4:8], g4, BD_ge[:, 64:65], None, ALU.mult), GBk + ["masks"], [gmk])
                pe(seq(mm(pc[:, 0:4], ML, g4), mm(pc[:, 4:8], MR, g4),
                       mm(pc[:, 8:12], onesf, gm[:, 0:4]), mm(pc[:, 12:16], onesf, gm[:, 4:8])),
                   ["masks", "onesf", gmk] + GBk, pck)
                ex, exk = r_ex.next()
                act(actf(ex, pc[:, 0:16], AF.Exp), pck, [exk])
                nb, nbk = r_nb.next()
                pool(ts(nb, b4, -1.0, None, ALU.mult), GBk, [nbk])
                prs.append(dict(T=T, d=d, islat=islat, qT=qT, qk=qk, kT=kT, kk=kk, vT=vT, vk=vk, ex=ex, exk=exk,
                                nb=nb, nbk=nbk, ML=ML, MR=MR, MS=MS, ofi=ofi, ofik=ofik, ofdefer=ofdefer,
                                g4=g4, b4=b4))
            gstage = dbg.get('gdn_stage', 99) if dbg else 99
            if gstage < 1:
                return prs
            for pr in prs:
                kT, kk = pr["kT"], pr["kk"]
                pKK, pKKk = ps_alloc(4)
                pe(seq(*[mm(pKK[:, h * 128:(h + 1) * 128], kT[:, h, :], kT[:, h, :]) for h in range(4)]), [kk], pKKk)
                M1, M1k = r_f["M1"].next()
                pool(tt(M1, mb4(pr["MR"]), bc4(pr["g4"]), ALU.mult), ["masks"] + GBk, [M1k])
                pD, pDk = ps_alloc(4)
                pe(seq(*[mm(pD[:, h * 128:(h + 1) * 128], M1[:, h, :], pr["ML"]) for h in range(4)]), [M1k, "masks"], pDk)
                E, Ek = r_f["E"].next()
                act(actf(E, v4(pD), AF.Exp), pDk, [Ek])
                Es, Esk = r_f["Es"].next()
                pool(tt(Es, E, mb4(pr["MS"]), ALU.mult), [Ek, "masks"], [Esk])
                pool(tt(Es, Es, bc4(pr["b4"]), ALU.mult), [Esk] + GBk, [Esk])
                B, Bk = r_f["B"].next()
                dve(tt(B, v4(pKK), Es, ALU.mult), pKKk + [Esk], [Bk])
                pr["B"], pr["Bk"] = B, Bk
                pr["at"] = pr["atk"] = None
                if pr["islat"]:
                    Ei, Eik = r_f["Ei"].next()
                    pool(tt(Ei, E, mb4(pr["ML"]), ALU.mult), [Ek, "masks"], [Eik])
                    pQK, pQKk = ps_alloc(4)
                    pe(seq(*[mm(pQK[:, h * 128:(h + 1) * 128], kT[:, h, :], pr["qT"][:, h, :]) for h in range(4)]),
                       [kk, pr["qk"]], pQKk)
                    at, atk = r_b["at"].next()
                    dve(tt(at, v4(pQK), Ei, ALU.mult), pQKk + [Eik], [atk])
                    pr["at"], pr["atk"] = at, atk
            if gstage < 2:
                return prs
            for pr in prs:
                B, Bk = pr["B"], pr["Bk"]
                pBT, pBTk = ps_alloc(4)
                pe(seq(*[mm(pBT[:, h * 128:(h + 1) * 128], B[:, h, :], identf) for h in range(4)]), [Bk, "identf"], pBTk)
                BT, BTk = r_f["BT"].next()
                act(cp(BT, v4(pBT)), pBTk, [BTk])
                X, Xk = r_f["X"].next()
                pool(tt(X, mb4(identf), B, ALU.subtract), [Bk, "identf"], [Xk])
                pr.update(P=B, Pk=Bk, PT=BT, PTk=BTk, X=X, Xk=Xk)
            if gstage < 3:
                return prs
            for lvl in range(5):
                last = lvl == 4
                for pr in prs:
                    Pm, Pk, PTm, PTk = pr["P"], pr["Pk"], pr["PT"], pr["PTk"]
                    pr["p2t"], pr["p2tk"] = ps_alloc(4)
                    pe(seq(*[mm(pr["p2t"][:, h * 128:(h + 1) * 128], Pm[:, h, :], PTm[:, h, :]) for h in range(4)]),
                       [Pk, PTk], pr["p2tk"])
                    if not last:
                        pr["p2"], pr["p2k"] = ps_alloc(4)
                        pe(seq(*[mm(pr["p2"][:, h * 128:(h + 1) * 128], PTm[:, h, :], Pm[:, h, :]) for h in range(4)]),
                           [Pk, PTk], pr["p2k"])
                for pr in prs:
                    nPT, nPTk = r_f["PT"].next()
                    act(cp(nPT, v4(pr["p2t"])), pr["p2tk"], [nPTk])
                    pr["PT"], pr["PTk"] = nPT, nPTk
                    if not last:
                        nP, nPk = r_f["P"].next()
                        dve(cp(nP, v4(pr["p2"])), pr["p2k"], [nPk])
                        pr["P"], pr["Pk"] = nP, nPk
                for pr in prs:
                    pr["px"], pr["pxk"] = ps_alloc(4)
                    pe(seq(*[mm(pr["px"][:, h * 128:(h + 1) * 128], pr["PT"][:, h, :], pr["X"][:, h, :]) for h in range(4)]),
                       [pr["PTk"], pr["Xk"]], pr["pxk"])
                for pr in prs:
                    if not last:
                        nX, nXk = r_f["X"].next()
                        dve(tt(nX, v4(pr["px"]), pr["X"], ALU.add), pr["pxk"] + [pr["Xk"]], [nXk])
                        pr["X"], pr["Xk"] = nX, nXk
                    else:
                        Xb, Xbk = r_b["Xb"].next()
                        dve(tt(Xb, v4(pr["px"]), pr["X"], ALU.add), pr["pxk"] + [pr["Xk"]], [Xbk])
                        pr["Xb"], pr["Xbk"] = Xb, Xbk
            if gstage < 4:
                return prs
            for pr in prs:
                ex, exk = pr["ex"], pr["exk"]
                pkt, pktk = ps_alloc(4)
                pktb = pkt.bitcast(BF16)
                pe(seq(*[tr(pktb[:, h * 128:(h + 1) * 128], pr["kT"][:, h, :], identb) for h in range(4)]),
                   [pr["kk"], "identb"], pktk)
                pvt, pvtk = ps_alloc(4)
                pvtb = pvt.bitcast(BF16)
                pe(seq(*[tr(pvtb[:, h * 128:(h + 1) * 128], pr["vT"][:, h, :], identb) for h in range(4)]),
                   [pr["vk"], "identb"], pvtk)
                Kg, Kgk = r_b["Kg"].next()
                dve(tt(Kg, v4(pktb[:, 0:512]), bc4(ex[:, 0:4]), ALU.mult), pktk + [exk], [Kgk])
                Kd, Kdk = r_b["Kd"].next()
                dve(tt(Kd, v4(pktb[:, 0:512]), bc4(ex[:, 4:8]), ALU.mult), pktk + [exk], [Kdk])
                vt, vtk = r_b["vt"].next()
                act(cp(vt, v4(pvtb[:, 0:512])), pvtk, [vtk])
                pU, pUk = ps_alloc(4)
                pe(seq(*[mm(pU[:, h * 128:(h + 1) * 128], pr["Xb"][:, h, :], vt[:, h, :]) for h in range(4)]),
                   [pr["Xbk"], vtk], pUk)
                Ub, Ubk = r_f["Ub"].next()
                dve(tt(Ub, v4(pU), bc4(pr["b4"]), ALU.mult), pUk + GBk, [Ubk])
                pW, pWk = ps_alloc(4)
                pe(seq(*[mm(pW[:, h * 128:(h + 1) * 128], Kg[:, h, :], pr["Xb"][:, h, :]) for h in range(4)]),
                   [Kgk, pr["Xbk"]], pWk)
                WT, WTk = r_b["WT"].next()
                act(cp(WT, v4(pW)), pWk, [WTk])
                pr.update(WT=WT, WTk=WTk, Ub=Ub, Ubk=Ubk, Kd=Kd, Kdk=Kdk)
            return prs

        def gdn_chain(preps, second):
            for pr in preps:
                if pr["ofdefer"]:
                    dma(pr["ofi"], OFa[pr["T"] - 2], [f"OF{pr['T'] - 2}"], [pr["ofik"]])
            for pr in preps:
                pr["vn"], pr["vnk"] = r_b["vn"].next()
                if pr["islat"]:
                    pr["pO1"], pr["pO1k"] = PS[6 + pr["d"]], [f"psb{6 + pr['d']}"]
            for sub in range(2):
                def rs(pr):
                    j = sub if pr["d"] == 0 else 1 - sub
                    return j, slice(64 * j, 64 * j + 64)
                for pr in preps:
                    d = pr["d"]
                    j, r = rs(pr)
                    pP, pPk = ps_alloc(4)
                    pe(seq(*[mm(pP[r, h * 128:(h + 1) * 128], pr["WT"][:, h, r], Sb[:, d * 4 + h, :]) for h in range(4)]),
                       [pr["WTk"], f"Sb{d}"], pPk)
                    tmp, tmpk = r_f["tmp"].next()
                    dve(tt(tmp[r, :, :], v4(pP)[r, :, :], pr["nb"][r, :].unsqueeze(2).to_broadcast([64, 4, 128]), ALU.mult),
                        pPk + [pr["nbk"]], [tmpk])
                    pool(tt(pr["vn"][r, :, :], tmp[r, :, :], pr["Ub"][r, :, :], ALU.add), [tmpk, pr["Ubk"]], [pr["vnk"]])
                for pr in preps:
                    d = pr["d"]
                    j, r = rs(pr)
                    if pr["islat"]:
                        pe(seq(*[mm(pr["pO1"][r, h * 128:(h + 1) * 128], pr["qT"][:, h, r], Sb[:, d * 4 + h, :]) for h in range(4)]),
                           [pr["qk"], f"Sb{d}"], pr["pO1k"])
                    pS, pSk = ps_alloc(4)
                    pe(seq(*[mm(pS[:, h * 128:(h + 1) * 128], pr["Kd"][r, h, :], pr["vn"][r, h, :]) for h in range(4)]),
                       [pr["Kdk"], pr["vnk"]], pSk)
                    pr["pS"], pr["pSk"] = pS, pSk
                    S4 = Sst[:, d * 4:(d + 1) * 4, :]
                    pool(tt(S4, S4, bc4(pr["ex"][:, 8 + 4 * j:12 + 4 * j]), ALU.mult), [pr["exk"], f"S{d}"], [f"S{d}"])
                for pr in preps:
                    d = pr["d"]
                    S4 = Sst[:, d * 4:(d + 1) * 4, :]
                    dve(tt(S4, S4, v4(pr["pS"]), ALU.add), pr["pSk"] + [f"S{d}"], [f"S{d}"])
                    act(cp(Sb[:, d * 4:(d + 1) * 4, :], S4), [f"S{d}"], [f"Sb{d}"])
            for pr in preps:
                if not pr["islat"]:
                    continue
                lt = pr["T"] - 2
                pO2, pO2k = ps_alloc(4)
                pe(seq(*[mm(pO2[:, h * 128:(h + 1) * 128], pr["at"][:, h, :], pr["vn"][:, h, :]) for h in range(4)]),
                   [pr["atk"], pr["vnk"]], pO2k)
                tmp, tmpk = r_f["tmp"].next()
                dve(tt(tmp, v4(pr["pO1"]), bc4(pr["ex"][:, 0:4]), ALU.mult), pr["pO1k"] + [pr["exk"]], [tmpk])
                if not second:
                    ofo, ofok = r_ofo.next()
                    dve(tt(ofo, pO2, tmp.rearrange("p h v -> p (h v)"), ALU.add), pO2k + [tmpk], [ofok])
                    dstore(OFa[lt], ofo, [ofok], [f"OF{lt}"])
                else:
                    ot, otk = r_f["ot"].next()
                    dve(tt(ot, v4(pO2), tmp, ALU.add), pO2k + [tmpk], [otk])
                    pool(tt(ot, ot, v4(pr["ofi"]), ALU.add), [otk, pr["ofik"]], [otk])
                    for h in range(4):
                        finalize(ot[:, h, :], otk, "OTa", OTa[:, h, lt * 128:(lt + 1) * 128], QSCALE)

        bwd_order = [1, 0] + list(range(33, 1, -1))
        nsteps = dbg.get("gdn_steps", NT) if dbg else NT
        if not (dbg and dbg.get("skip_gdn")):
            cur = gdn_prep([(0, 0), (bwd_order[0], 1)], False)
            for i in range(nsteps):
                nxt = None
                if i + 1 < nsteps:
                    nxt = gdn_prep([(i + 1, 0), (bwd_order[i + 1], 1)], (i + 1) >= 18)
                if not (dbg and dbg.get('gdn_stage', 99) < 5):
                    gdn_chain(cur, second=(i >= 18))
                cur = nxt
        if dbg and dbg.get("dump_ota"):
            dma(dbg_ota, OTa, ["OTa"], ["dbg_ota"])
            dma(DBG[:, 4096:4096 + 1024], Sst.rearrange("p a b -> p (a b)"), ["S0", "S1"], ["DBG"])
        P.barrier()
        st["off"] = ph3

        h_q = Ring("hq", 4, [128, 4, 128], BF16)
        h_k = Ring("hk", 4, [128, 4, 128], BF16)
        h_i = Ring("hi", 4, [128, 4, 128], BF16)
        h_g = Ring("hg", 4, [128, 512], F32)
        h_w = {n: Ring("h" + n, 3, [128, 512], F32) for n in ("G", "eq", "ek", "ed", "Gx", "enx")}
        h_qg = Ring("hqg", 4, [128, 4, 128], BF16)
        h_kg = Ring("hkg", 3, [128, 4, 128], BF16)
        h_kd = Ring("hkd", 3, [128, 4, 128], BF16)
        h_gl = Ring("hgl", 4, [128, 16], F32)
        h_at = Ring("hat", 16, [128, 128], BF16)
        h_kt = Ring("hkt", 16, [128, 128], BF16)
        h_vt = Ring("hvt", 16, [128, 128], BF16)
        r_s = Ring("hss", 8, [128, 4], F32)
        r_f = {"o1": Ring("ho1", 4, [128, 128], F32), "ot": Ring("hot", 8, [128, 128], F32)}
        r_b = {"on": Ring("hon", 8, [128, 128], BF16)}
        r_ofo = Ring("hofo", 2, [128, 512], F32)
        r_ofi = Ring("hofi", 6, [128, 512], F32)
        pool(lambda e: e.memset(Sst, 0.0), ["S0", "S1"], ["S%d" % i for i in range(8)])
        pool(lambda e: e.memset(Sb, 0.0), ["Sb0", "Sb1"], ["Sb%d" % i for i in range(8)])

        def bc8(ap8):
            return ap8.unsqueeze(2).to_broadcast([128, 8, 64])

        def v864(ap):
            return ap.rearrange("p (a b) -> p a b", b=64)

        def hg_prep(pairs, second):
            prs = []
            for (T, d) in pairs:
                islat = T >= 2
                tsl = slice(T * 128, (T + 1) * 128)
                qT, qk = h_q.next()
                kT, kk = h_k.next()
                iT, ik = h_i.next()
                gT, gk = h_g.next()
                dma(qT, ZBQ[:, :, tsl].rearrange("h p t -> p h t"), ["ZBQ"], [qk])
                dma(kT, ZBK[d * 4:(d + 1) * 4, :, tsl].rearrange("h p t -> p h t"), ["ZBK"], [kk])
                dma(iT, ZBI[:, :, tsl].rearrange("h p t -> p h t"), ["ZBI"], [ik])
                dma(gT.rearrange("p (h t) -> p h t", h=4), ZBL[d * 4:(d + 1) * 4, :, tsl].rearrange("h p t -> p h t"), ["ZBL"], [gk])
                ofi = ofik = None
                ofdefer = False
                if islat and second:
                    ofi, ofik = r_ofi.next()
                    ofdefer = f"OFb{T - 2}" not in P.lastw
                    if not ofdefer:
                        dma(ofi, OFb[T - 2], [f"OFb{T - 2}"], [ofik])
                G, Gk = h_w["G"].next()
                dve(lambda e, a=G, b=gT: e.tensor_tensor_scan(out=a, data0=segm, data1=b, initial=0.0, op0=ALU.mult, op1=ALU.add),
                    [gk, "segm"], [Gk])
                gl, glk = h_gl.next()
                Glast = v864(G)[:, :, 63]
                eq, eqk = h_w["eq"].next()
                ek, ekk = h_w["ek"].next()
                ed, edk = h_w["ed"].next()
                act(actf(gl[:, 0:8], Glast, AF.Exp), [Gk], [glk])
                if d == 0:
                    act(actf(eq, G, AF.Exp), [Gk], [eqk])
                    act(actf(ek, G, AF.Exp, scale=-1.0), [Gk], [ekk])
                    dve(tt(v864(ed), v864(ek), bc8(gl[:, 0:8]), ALU.mult), [ekk, glk], [edk])
                else:
                    Gx, Gxk = h_w["Gx"].next()
                    enx, enxk = h_w["enx"].next()
                    act(actf(gl[:, 8:16], Glast, AF.Exp, scale=-1.0), [Gk], [glk])
                    pool(tt(Gx, G, gT, ALU.subtract), [Gk, gk], [Gxk])
                    act(actf(ed, Gx, AF.Exp), [Gxk], [edk])
                    act(actf(enx, Gx, AF.Exp, scale=-1.0), [Gxk], [enxk])
                    dve(tt(v864(eq), v864(enx), bc8(gl[:, 0:8]), ALU.mult), [enxk, glk], [eqk])
                    pool(tt(v864(ek), v864(ed), bc8(gl[:, 8:16]), ALU.mult), [edk, glk], [ekk])
                qg, qgk = h_qg.next()
                kg, kgk = h_kg.next()
                kd, kdk = h_kd.next()
                f2 = lambda a: a.rearrange("p h t -> p (h t)")
                dve(tt(f2(qg), f2(qT), eq, ALU.mult), [qk, eqk], [qgk])
                pool(tt(f2(kg), f2(kT), ek, ALU.mult), [kk, ekk], [kgk])
                pool(tt(f2(kd), f2(kT), ed, ALU.mult), [kk, edk], [kdk])
                MI = BD_le if d == 0 else BD_ge
                heads = []
                for h in range(4):
                    c = d * 4 + h
                    at = atk = None
                    if islat:
                        pA, pAk = ps_alloc(1)
                        pe(mm(pA, kg[:, h, :], qg[:, h, :]), [kgk, qgk], pAk)
                        at, atk = h_at.next()
                        dve(tt(at, pA, MI, ALU.mult), pAk + ["masks"], [atk])
                    pk_, pkk = ps_alloc(1)
                    pe(tr(bfv(pk_), kd[:, h, :], identb), [kdk, "identb"], pkk)
                    kt, ktk = h_kt.next()
                    act(cp(kt, bfv(pk_)), pkk, [ktk])
                    pv_, pvk = ps_alloc(1)
                    pe(tr(bfv(pv_), iT[:, h, :], identb), [ik, "identb"], pvk)
                    vt, vtk = h_vt.next()
                    act(cp(vt, bfv(pv_)), pvk, [vtk])
                    heads.append(dict(h=h, c=c, at=at, atk=atk, kt=kt, ktk=ktk, vt=vt, vtk=vtk))
                prs.append(dict(T=T, d=d, islat=islat, qg=qg, qgk=qgk, gl=gl, glk=glk, ofi=ofi, ofik=ofik,
                                ofdefer=ofdefer, heads=heads))
            return prs

        def otb_view(j):
            def colap(h):
                return OTb[:, h, :].rearrange("p (r w) -> p r w", w=64)[:, :, 2 * j:2 * j + 2]
            return colap

        def hg_chain(preps, second):
            items = [(pr, hd) for pr in preps for hd in pr["heads"]]
            for pr in preps:
                if pr["ofdefer"]:
                    dma(pr["ofi"], OFb[pr["T"] - 2], [f"OFb{pr['T'] - 2}"], [pr["ofik"]])
            for ii, (pr, hd) in enumerate(items):
                if pr["islat"]:
                    hd["pO"], hd["pOk"] = PS[6 + ii // 4][:, (ii % 4) * 128:(ii % 4 + 1) * 128], [f"psb{6 + ii // 4}"]
            for sub in range(2):
                def rs(pr):
                    j = sub if pr["d"] == 0 else 1 - sub
                    return j, slice(64 * j, 64 * j + 64)
                for pr, hd in items:
                    c, h = hd["c"], hd["h"]
                    j, r = rs(pr)
                    if pr["islat"]:
                        pe(seq(mm(hd["pO"][r, :], pr["qg"][:, h, r], Sb[:, c, :], start=True, stop=False),
                               mm(hd["pO"][r, :], hd["at"][r, r], hd["vt"][r, :], start=False, stop=True)),
                           [pr["qgk"], f"Sb{c}", hd["atk"], hd["vtk"]], hd["pOk"])
                    pS, pSk = ps_alloc(1)
                    pe(mm(pS, hd["kt"][r, :], hd["vt"][r, :]), [hd["ktk"], hd["vtk"]], pSk)
                    hd["pS"], hd["pSk"] = pS, pSk
                for pr, hd in items:
                    c, h = hd["c"], hd["h"]
                    j, r = rs(pr)
                    dve(stt(Sst[:, c, :], Sst[:, c, :], pr["gl"][:, h * 2 + j:h * 2 + j + 1], hd["pS"], ALU.mult, ALU.add),
                        hd["pSk"] + [pr["glk"], f"S{c}"], [f"S{c}"])
                    act(cp(Sb[:, c, :], Sst[:, c, :]), [f"S{c}"], [f"Sb{c}"])
            for pr in preps:
                if not pr["islat"]:
                    continue
                j = pr["T"] - 2
                if not second:
                    ofo, ofok = r_ofo.next()
                for hd in pr["heads"]:
                    h = hd["h"]
                    if not second:
                        act(cp(ofo[:, h * 128:(h + 1) * 128], hd["pO"]), hd["pOk"], [ofok])
                    else:
                        ot, otk = r_f["ot"].next()
                        dve(tt(ot, hd["pO"], pr["ofi"][:, h * 128:(h + 1) * 128], ALU.add), hd["pOk"] + [pr["ofik"]], [otk])
                        finalize(ot, otk, "OTb", otb_view(j)(h), QSCALE,
                                 tview=lambda a: a.rearrange("p (a b) -> p a b", a=2).rearrange("p a b -> p b a"))
                if not second:
                    dstore(OFb[j], ofo, [ofok], [f"OFb{j}"])

        hsteps = dbg.get("hg_steps", NT) if dbg else NT
        if not (dbg and dbg.get("skip_hg")):
            cur = hg_prep([(0, 0), (bwd_order[0], 1)], False)
            for i in range(hsteps):
                nxt = None
                if i + 1 < hsteps:
                    nxt = hg_prep([(i + 1, 0), (bwd_order[i + 1], 1)], (i + 1) >= 18)
                hg_chain(cur, second=(i >= 18))
                cur = nxt
        if dbg and dbg.get("dump_otb"):
            dma(dbg_otb, OTb, ["OTb"], ["dbg_otb"])
        P.barrier()
        st["off"] = ph3

        waoB = alloc([128, 4, 1024], BF16)
        wboB = alloc([128, 4, 1024], BF16)
        woB = alloc([128, 8, 1024], BF16)
        lng_row = alloc([128, 1024], F32)
        lnb_row = alloc([128, 1024], F32)
        Hr = Ring("oH", 3, [128, 1024], F32)
        xr4 = Ring("ox", 2, [128, 1024], F32)
        agr = Ring("oag", 1, [128, 4, 512], BF16)
        bgr = Ring("obg", 1, [128, 4, 512], BF16)
        mr = Ring("om", 4, [128, 2, 512], BF16)
        mar = Ring("oma", 1, [128, 4, 512], BF16)
        mbr = Ring("omb", 1, [128, 4, 512], BF16)
        t1r = Ring("ot1", 2, [128, 512], F32)
        t2r = Ring("ot2", 2, [128, 512], F32)
        mixr = Ring("omix", 2, [128, 8, 512], BF16)
        st4 = Ring("ost", 2, [128, 2, 6], F32)
        mv4 = Ring("omv", 2, [128, 4], F32)
        dma(lng_row, lng.partition_broadcast(128), [], ["lng_row"])
        dma(lnb_row, lnb.partition_broadcast(128), [], ["lnb_row"])
        for (wsrc, wdst, n, key) in ((wao, waoB, 4, "waoB"), (wbo, wboB, 4, "wboB"), (wo, woB, 8, "woB")):
            for i in range(n):
                hbuf, hk = Hr.next()
                dma(hbuf, wsrc[:, i, :], [], [hk])
                pool(cp(wdst[:, i, :], hbuf), [hk], [key])
        nblk = dbg.get("out_blocks", 8) if dbg else 8
        for bb in range(nblk):
            bsl = slice(bb * 512, (bb + 1) * 512)
            ag, agk = agr.next()
            bg, bgk = bgr.next()
            dma(ag, ZAG[:, :, bsl].rearrange("h p t -> p h t"), ["ZAG"], [agk])
            dma(bg, ZBGT[:, :, bsl].rearrange("h p t -> p h t"), ["ZBGT"], [bgk])
            ma, mak = mar.next()
            mb, mbk = mbr.next()
            dve(stt(ma, OTa[:, :, bsl], gains[:, 0:1], ag, ALU.mult, ALU.mult), ["OTa", "gains", agk], [mak])
            dve(stt(mb, OTb[:, :, bsl], gains[:, 1:2], bg, ALU.mult, ALU.mult), ["OTb", "gains", bgk], [mbk])
            mix, mixk = mixr.next()
            for cc in range(8):
                mt, mtk = mr.next()
                dma(mt[:, 0, :], ZM[cc, :, bsl], ["ZM"], [mtk])
                dma(mt[:, 1, :], ZM[8 + cc, :, bsl], ["ZM"], [mtk])
                pYa, pYak = ps_alloc(4)
                pe(seq(*[mm(pYa, waoB[:, h, cc * 128:(cc + 1) * 128], ma[:, h, :], start=(h == 0), stop=(h == 3))
                         for h in range(4)]), ["waoB", mak], pYak)
                pYb, pYbk = ps_alloc(4)
                pe(seq(*[mm(pYb, wboB[:, h, cc * 128:(cc + 1) * 128], mb[:, h, :], start=(h == 0), stop=(h == 3))
                         for h in range(4)]), ["wboB", mbk], pYbk)
                t1, t1k = t1r.next()
                t2, t2k = t2r.next()
                dve(tt(t1, pYa, mt[:, 0, :], ALU.mult), pYak + [mtk], [t1k])
                dve(tt(t2, pYb, mt[:, 1, :], ALU.mult), pYbk + [mtk], [t2k])
                pool(tt(mix[:, cc, :], t1, t2, ALU.add), [t1k, t2k], [mixk])
            for ti in range(4):
                lt = bb * 4 + ti
                xt, xk = xr4.next()
                dma(xt, x[lt * 128:(lt + 1) * 128, :], [], [xk])
                H, Hk = Hr.next()
                for half in range(2):
                    pSu, pSuk = ps_alloc(4)
                    pe(seq(*[mm(pSu, mix[:, cc, ti * 128:(ti + 1) * 128], woB[:, cc, half * 512:(half + 1) * 512],
                                start=(cc == 0), stop=(cc == 7)) for cc in range(8)]), [mixk, "woB"], pSuk)
                    dve(tt(H[:, half * 512:(half + 1) * 512], pSu, gate_row[:, half * 512:(half + 1) * 512], ALU.mult),
                        pSuk + ["gate_row"], [Hk])
                act(actf(xt, xt, AF.Copy, scale=ALPHA), [xk], [xk])
                pool(tt(H, H, xt, ALU.add), [Hk, xk], [Hk])
                stt_, stk = st4.next()
                mv, mvk = mv4.next()
                dve(seq(lambda e, a=stt_, b=H: e.bn_stats(out=a[:, 0, :], in_=b[:, 0:512]),
                        lambda e, a=stt_, b=H: e.bn_stats(out=a[:, 1, :], in_=b[:, 512:1024])), [Hk], [stk])
                dve(lambda e, a=mv, b=stt_: e.bn_aggr(out=a[:, 0:2], in_=b.rearrange("p a b -> p (a b)")), [stk], [mvk])
                act(actf(mv[:, 2:3], mv[:, 1:2], AF.Sqrt, bias=EPS), [mvk], [mvk + "s"])
                dve(lambda e, a=mv: e.reciprocal(out=a[:, 2:3], in_=a[:, 2:3]), [mvk + "s"], [mvk + "s"])
                dve(stt(mv[:, 3:4], mv[:, 0:1], -1.0, mv[:, 2:3], ALU.mult, ALU.mult), [mvk, mvk + "s"], [mvk + "n"])
                act(actf(H, H, AF.Identity, bias=mv[:, 3:4], scale=mv[:, 2:3]), [Hk, mvk + "s", mvk + "n"], [Hk])
                dve(tt(H, H, lng_row, ALU.mult), [Hk, "lng_row"], [Hk])
                pool(tt(H, H, lnb_row, ALU.add), [Hk, "lnb_row"], [Hk])
                dstore(y[lt * 128:(lt + 1) * 128, :], H, [Hk], ["y"])

        final_cnt = dict(P.cnt)

        @block.sync
        def _(e):
            P.replay('sp', e, sems)
            for l, n in final_cnt.items():
                if l[0] == 'd' and l != 'dve':
                    e.wait_ge(sems[l], 16 * n)

        @block.tensor
        def _(e):
            P.replay('pe', e, sems)

        @block.scalar
        def _(e):
            P.replay('act', e, sems)

        @block.vector
        def _(e):
            P.replay('dve', e, sems)

        @block.gpsimd
        def _(e):
            P.replay('pool', e, sems)
    return nc, P


def _consts():
    p = np.arange(128)[:, None]
    f = np.arange(128)[None, :]
    same = (p // 64) == (f // 64)
    m = np.stack([p <= f, p < f, p >= f, p > f, (p <= f) & same, (p < f) & same, (p >= f) & same, (p > f) & same], axis=1).astype(np.float32)
    segm = np.ones((128, 512), np.float32)
    segm[:, ::64] = 0.0
    return np.eye(128, dtype=np.float32), np.ascontiguousarray(m), segm


def make_in_maps(inp):
    f = np.float32
    A = lambda a: np.ascontiguousarray(a, dtype=f)
    w_in = inp['w_in'][0]
    cols = np.r_[0:1536, 1552:6672]
    win = A(w_in[:, cols].reshape(8, 128, 52, 128).transpose(2, 1, 0, 3))
    wab = A(w_in[:, 1536:1552].reshape(8, 128, 16).transpose(1, 0, 2))
    w_mod = inp['w_mod'][0]
    wmod = A(w_mod.reshape(8, 128, 6, 512).transpose(2, 1, 0, 3))
    b_mod = inp['b_mod'][0]
    bmodc = A(b_mod[:2048].reshape(16, 128).T)
    bmodg = A(b_mod[2048:].reshape(1, 1024))
    convw = A(inp['conv_w'][0].reshape(5, 12, 128).transpose(2, 1, 0))
    alog = A(inp['a_log'][0].reshape(1, 8))
    dtb = A(inp['dt_bias'][0].reshape(1, 8))
    lbp = A(inp['lb_param'].reshape(2, 2, 4, 128).transpose(3, 0, 1, 2).reshape(128, 2, 8))
    ang = A(inp['a_norm_g'][0].reshape(128, 1))
    bng = A(inp['b_norm_g'][0].reshape(128, 1))
    wao = A(inp['w_a_out'][0].reshape(4, 128, 1024).transpose(1, 0, 2))
    wbo = A(inp['w_b_out'][0].reshape(4, 128, 1024).transpose(1, 0, 2))
    wo = A(inp['w_out'][0].reshape(8, 128, 1024).transpose(1, 0, 2))
    lng = A(inp['ln_g'][0].reshape(1, 1024))
    lnb = A(inp['ln_b'][0].reshape(1, 1024))
    cidf, cmask, csegm = _consts()
    maps = []
    for b in range(8):
        ccv = np.stack([inp['c'][b].reshape(8, 128).T, inp['c_ctx'].reshape(8, 128).T], axis=2)
        maps.append(dict(x=A(inp['x'][b]), ctx=A(inp['ctx'][b]), cc=A(ccv), wmod=wmod, bmodc=bmodc, bmodg=bmodg,
                         win=win, wab=wab, convw=convw, alog=alog, dtb=dtb, lbp=lbp, ang=ang, bng=bng,
                         wao=wao, wbo=wbo, wo=wo, lng=lng, lnb=lnb, cidf=cidf, cmask=cmask, csegm=csegm))
    return maps


def kernel(**inputs):
    nc, _ = build()
    maps = make_in_maps(inputs)
    res = run_bass_kernel_spmd(nc, maps, core_ids=list(range(8)))
    return np.stack([np.asarray(r["y"], dtype=np.float32) for r in res.results], axis=0)
```

```python
import numpy as np
import ml_dtypes
from contextlib import ExitStack
import concourse.bass as bass
import concourse.mybir as mybir
from concourse.bass_utils import run_bass_kernel_spmd

F32 = mybir.dt.float32
BF16 = mybir.dt.bfloat16
U8 = mybir.dt.uint8
AF = mybir.ActivationFunctionType
ALU = mybir.AluOpType

NT = 34
NTOK = 4352
QSCALE = 128 ** -0.5
ALPHA = 2.0 ** 0.25
EPS = 1e-6


class Prog:
    ENG = ('pe', 'act', 'dve', 'pool', 'sp')

    def __init__(self):
        self.ops = {e: [] for e in self.ENG}
        self.cnt = {}
        self.know = {e: {} for e in self.ENG}
        self.opclock = {}
        self.lastw = {}
        self.readers = {}
        self.pending = {e: {} for e in self.ENG}
        self.nlanes = {'sp': 8, 'pool': 6, 'act': 4}
        self.rr = {e: 0 for e in self.ENG}

    def lanes(self):
        out = list(self.ENG[:4])
        for q, n in self.nlanes.items():
            out += [f'd{q}{i}' for i in range(n)]
        return out

    def barrier(self):
        snap = dict(self.cnt)
        for e in self.ENG:
            p = self.pending[e]
            for l, n in snap.items():
                if p.get(l, 0) < n:
                    p[l] = n

    def emit(self, eng, fn, reads=(), writes=(), dma=False):
        if dma:
            i = self.rr[eng]
            self.rr[eng] = (i + 1) % self.nlanes[eng]
            lane = f'd{eng}{i}'
        else:
            lane = eng
        psr = [r for r in reads if r.startswith('psb')]
        if psr:
            reads = [r for r in reads if not r.startswith('psb')]
            writes = list(writes) + psr
        deps = dict(self.pending[eng])
        self.pending[eng] = {}

        def add(l, n):
            if deps.get(l, 0) < n:
                deps[l] = n
        for r in reads:
            w = self.lastw.get(r)
            if w:
                add(*w)
        for r in writes:
            w = self.lastw.get(r)
            if w and not (w[0] == lane and not dma):
                add(*w)
            for l, n in self.readers.get(r, {}).items():
                if not (l == lane and not dma):
                    add(l, n)
        if dma and self.cnt.get(lane, 0) > 0:
            add(lane, self.cnt[lane])
        know = self.know[eng]
        waits = [(l, n) for l, n in deps.items() if know.get(l, 0) < n]
        for l, n in deps.items():
            for l2, n2 in self.opclock.get((l, n), {}).items():
                if know.get(l2, 0) < n2:
                    know[l2] = n2
            if know.get(l, 0) < n:
                know[l] = n
        n = self.cnt.get(lane, 0) + 1
        self.cnt[lane] = n
        self.opclock[(lane, n)] = dict(know)
        self.ops[eng].append((waits, fn, lane))
        for r in reads:
            self.readers.setdefault(r, {})[lane] = n
        for r in writes:
            self.lastw[r] = (lane, n)
            self.readers[r] = {}

    def replay(self, name, eng, sems):
        for waits, fn, lane in self.ops[name]:
            for l, n in waits:
                eng.wait_ge(sems[l], n * (16 if l[0] == 'd' and l != 'dve' else 1))
            inst = fn(eng)
            inst.then_inc(sems[lane], 16 if (lane[0] == 'd' and lane != 'dve') else 1)


def seq(*fns):
    def f(e):
        r = None
        for g in fns:
            r = g(e)
        return r
    return f


def build(dbg=None):
    nc = bass.Bass("TRN2", target_bir_lowering=False)
    P = Prog()

    def din(name, shape, dt=F32):
        return nc.dram_tensor(name, list(shape), dt, kind="ExternalInput").ap()

    x = din("x", [4096, 1024])
    ctx = din("ctx", [256, 1024])
    cc = din("cc", [128, 8, 2])
    wmod = din("wmod", [6, 128, 8, 512])
    bmodc = din("bmodc", [128, 16])
    bmodg = din("bmodg", [1, 1024])
    win = din("win", [52, 128, 8, 128])
    wab = din("wab", [128, 8, 16])
    convw = din("convw", [128, 12, 5])
    alog = din("alog", [1, 8])
    dtb = din("dtb", [1, 8])
    lbp = din("lbp", [128, 2, 8])
    ang = din("ang", [128, 1])
    bng = din("bng", [128, 1])
    wao = din("wao", [128, 4, 1024])
    wbo = din("wbo", [128, 4, 1024])
    wo = din("wo", [128, 8, 1024])
    lng = din("lng", [1, 1024])
    lnb = din("lnb", [1, 1024])
    cidf = din("cidf", [128, 128])
    cmask = din("cmask", [128, 8, 128])
    csegm = din("csegm", [128, 512])
    y = nc.dram_tensor("y", [4096, 1024], F32, kind="ExternalOutput").ap()

    def dscr(name, shape, dt):
        kind = {"kind": "ExternalOutput"} if (dbg and name in dbg) else {}
        return nc.dram_tensor(name, list(shape), dt, **kind).ap()

    ZQ = dscr("ZQ", [4, 128, NTOK], BF16)
    ZK = dscr("ZK", [4, 128, NTOK], BF16)
    ZV = dscr("ZV", [4, 128, NTOK], BF16)
    ZAG = dscr("ZAG", [4, 128, 4096], BF16)
    ZBGT = dscr("ZBGT", [4, 128, 4096], BF16)
    ZM = dscr("ZM", [16, 128, 4096], BF16)
    ZBQ = dscr("ZBQ", [4, 128, NTOK], BF16)
    ZBK = dscr("ZBK", [8, 128, NTOK], BF16)
    ZBL = dscr("ZBL", [8, 128, NTOK], F32)
    ZBI = dscr("ZBI", [4, 128, NTOK], BF16)
    OFa = dscr("OFa", [32, 128, 512], F32)
    OFb = dscr("OFb", [32, 128, 512], F32)
    DBG = dscr("DBG", [128, 8192], F32) if dbg else None
    dbg_ota = dscr("dbg_ota", [128, 4, 4096], BF16) if dbg else None
    dbg_otb = dscr("dbg_otb", [128, 4, 4096], BF16) if dbg else None

    es = ExitStack()
    with es:
        ARENA = 204 * 1024
        arena = es.enter_context(nc.sbuf_tensor("arena", [128, ARENA], U8))
        PS = [es.enter_context(nc.psum_tensor(f"ps{i}", [128, 512], F32)) for i in range(8)]
        sems = {l: es.enter_context(nc.semaphore(f"s_{l}")) for l in P.lanes()}
        block = es.enter_context(nc.Block())

        st = {"off": 0, "n": 0}

        def alloc(shape, dt, name=None):
            nb = int(np.prod(shape[1:])) * (4 if dt == F32 else 2)
            off = (st["off"] + 63) // 64 * 64
            assert off + nb <= ARENA, (name, off, nb)
            st["off"] = off + nb
            ap = arena[:, off:off + nb].bitcast(dt)
            if len(shape) == 3:
                ap = ap.rearrange("p (a b) -> p a b", a=shape[1])
            elif len(shape) == 4:
                ap = ap.rearrange("p (a b c) -> p a b c", a=shape[1], b=shape[2])
            st["n"] += 1
            return ap

        class Ring:
            def __init__(self, name, n, shape, dt):
                self.name = name
                self.aps = [alloc(shape, dt, name) for _ in range(n)]
                self.i = 0

            def next(self):
                i = self.i
                self.i = (i + 1) % len(self.aps)
                return self.aps[i], f"{self.name}#{i}"

        psst = {"i": 0, "q": [0] * 8, "ip": 0, "ic": 0, "grp": None}

        def ps_alloc(nq=1):
            grp = psst["grp"]
            if grp == 'p':
                b = psst["ip"] % 4
                psst["ip"] += 1
            elif grp == 'c':
                b = 4 + psst["ic"] % 2
                psst["ic"] += 1
            else:
                b = psst["i"] % 6
                psst["i"] += 1
            if nq == 4:
                s_ = 0
            else:
                s_ = psst["q"][b]
                psst["q"][b] = (s_ + nq) % 4
                assert s_ + nq <= 4
            ap = PS[b][:, s_ * 128:(s_ + nq) * 128]
            return ap, [f"psb{b}"]

        def bfv(ap):
            return ap.bitcast(BF16)[:, 0:128]

        STQ = 'pool'
        pe = lambda fn, r, w: P.emit('pe', fn, r, w)
        act = lambda fn, r, w: P.emit('act', fn, r, w)
        dve = lambda fn, r, w: P.emit('dve', fn, r, w)
        pool = lambda fn, r, w: P.emit('pool', fn, r, w)

        def dma(out, in_, r, w, q='sp'):
            P.emit(q, lambda e: e.dma_start(out=out, in_=in_), r, w, dma=True)

        def dstore(out, in_, r, w):
            dma(out, in_, r, w, q=STQ)

        def mm(out, lhsT, rhs, start=True, stop=True):
            return lambda e: e.matmul(out, lhsT=lhsT, rhs=rhs, start=start, stop=stop)

        def tr(out, in_, ident):
            return lambda e: e.transpose(out=out, in_=in_, identity=ident)

        def actf(out, in_, func, bias=None, scale=None, accum_out=None):
            kw = {}
            if bias is not None:
                kw["bias"] = bias
            if scale is not None:
                kw["scale"] = scale
            if accum_out is not None:
                kw["accum_out"] = accum_out
            return lambda e: e.activation(out=out, in_=in_, func=func, **kw)

        def tt(out, in0, in1, op):
            return lambda e: e.tensor_tensor(out=out, in0=in0, in1=in1, op=op)

        def ts(out, in0, s1, s2, op0, op1=None):
            if op1 is None:
                return lambda e: e.tensor_scalar(out=out, in0=in0, scalar1=s1, scalar2=None, op0=op0)
            return lambda e: e.tensor_scalar(out=out, in0=in0, scalar1=s1, scalar2=s2, op0=op0, op1=op1)

        def stt(out, in0, scalar, in1, op0, op1):
            return lambda e: e.scalar_tensor_tensor(out=out, in0=in0, scalar=scalar, in1=in1, op0=op0, op1=op1)

        def cp(out, in_):
            return lambda e: (e.tensor_copy(out=out, in_=in_) if hasattr(e, 'tensor_copy') else e.activation(out=out, in_=in_, func=AF.Copy))

        identf = alloc([128, 128], F32)
        identb = alloc([128, 128], BF16)
        masks = alloc([128, 8, 128], F32)
        onesf = alloc([128, 128], F32)
        segm = alloc([128, 512], F32)
        M_le, M_lt, M_ge, M_gt, BD_le, BD_lt, BD_ge, BD_gt = [masks[:, i, :] for i in range(8)]
        gate_row = alloc([128, 1024], F32)
        GB = alloc([128, NT, 16], F32)
        modc = alloc([128, 16, 2], F32)
        lbc = alloc([128, 8], F32)
        omlb = alloc([128, 8], F32)
        nomlb = alloc([128, 8], F32)
        gains = alloc([128, 2], F32)
        cw = alloc([128, 12, 5], F32)
        rowc = alloc([128, 16], F32)
        persist_off = st["off"]

        dma(identf, cidf, [], ["identf"])
        dma(masks, cmask, [], ["masks"])
        dma(segm, csegm, [], ["segm"])
        dma(cw, convw, [], ["cw"])
        dma(gains[:, 0:1], ang, [], ["gains"])
        dma(gains[:, 1:2], bng, [], ["gains"])
        dma(rowc[:, 0:8], alog.partition_broadcast(128), [], ["rowc"])
        dma(rowc[:, 8:16], dtb.partition_broadcast(128), [], ["rowc"])
        dve(cp(identb, identf), ["identf"], ["identb"])
        pool(lambda e: e.memset(onesf, 1.0), [], ["onesf"])
        act(actf(rowc[:, 0:8], rowc[:, 0:8], AF.Exp), ["rowc"], ["rowc"])
        dve(ts(rowc[:, 0:8], rowc[:, 0:8], -1.0, None, ALU.mult), ["rowc"], ["rowc"])

        ph0 = st["off"]
        cct = alloc([128, 8, 2], F32)
        sil = alloc([128, 8, 2], F32)
        srep = alloc([128, 8, 128], F32)
        bmc = alloc([128, 16], F32)
        lbt = alloc([128, 2, 8], F32)
        wmr = Ring("wm", 2, [128, 8, 512], F32)
        dma(cct, cc, [], ["cct"])
        dma(bmc, bmodc, [], ["bmc"])
        dma(lbt, lbp, [], ["lbt"])
        dma(gate_row, bmodg.partition_broadcast(128), [], ["gate_row"])
        act(actf(sil, cct, AF.Silu), ["cct"], ["sil"])
        dve(cp(srep, sil[:, :, 0:1].to_broadcast([128, 8, 128])), ["sil"], ["srep"])
        dve(tt(lbc, lbt[:, 0, :], lbt[:, 1, :], ALU.subtract), ["lbt"], ["lbc"])
        act(actf(lbc, lbc, AF.Sigmoid), ["lbc"], ["lbc"])
        dve(ts(omlb, lbc, -1.0, 1.0, ALU.mult, ALU.add), ["lbc"], ["omlb"])
        dve(ts(nomlb, lbc, -1.0, None, ALU.add), ["lbc"], ["nomlb"])
        for blk in range(6):
            wt, wk = wmr.next()
            dma(wt, wmod[blk], [], [wk])
            if blk < 4:
                for jj in range(4):
                    j = blk * 4 + jj
                    pt, pk = ps_alloc(1)
                    pe(seq(*[mm(pt[:, 0:2], wt[:, kc, jj * 128:(jj + 1) * 128], sil[:, kc, :],
                                start=(kc == 0), stop=(kc == 7)) for kc in range(8)]),
                       [wk, "sil"], pk)
                    dve(ts(modc[:, j, :], pt[:, 0:2], bmc[:, j:j + 1], 1.0 if j >= 8 else 0.0, ALU.add, ALU.add),
                        pk + ["bmc"], ["modc"])
            else:
                pt, pk = ps_alloc(4)
                pe(seq(*[mm(pt, srep[:, kc, :], wt[:, kc, :], start=(kc == 0), stop=(kc == 7))
                         for kc in range(8)]), [wk, "srep"], pk)
                gs = gate_row[:, (blk - 4) * 512:(blk - 3) * 512]
                dve(tt(gs, gs, pt, ALU.add), pk + ["gate_row"], ["gate_row"])
        P.barrier()
        st["off"] = ph0

        uT = alloc([128, 8, NTOK], BF16)
        ph2 = st["off"]
        xr = Ring("xt", 2, [128, 1024], F32)
        xnr = Ring("xn", 2, [128, 1024], BF16)
        str_ = Ring("st", 2, [128, 2, 6], F32)
        mvr = Ring("mv", 2, [128, 4], F32)
        for T in range(NT):
            src = ctx[T * 128:(T + 1) * 128, :] if T < 2 else x[(T - 2) * 128:(T - 1) * 128, :]
            w = 1 if T < 2 else 0
            xt, xk = xr.next()
            xn, xnk = xnr.next()
            stt_, stk = str_.next()
            mv, mvk = mvr.next()
            dma(xt, src, [], [xk])
            dve(seq(lambda e, a=stt_, b=xt: e.bn_stats(out=a[:, 0, :], in_=b[:, 0:512]),
                    lambda e, a=stt_, b=xt: e.bn_stats(out=a[:, 1, :], in_=b[:, 512:1024])), [xk], [stk])
            dve(lambda e, a=mv, b=stt_: e.bn_aggr(out=a[:, 0:2], in_=b.rearrange("p a b -> p (a b)")), [stk], [mvk])
            act(actf(mv[:, 2:3], mv[:, 1:2], AF.Sqrt, bias=EPS), [mvk], [mvk + "s"])
            dve(lambda e, a=mv: e.reciprocal(out=a[:, 2:3], in_=a[:, 2:3]), [mvk + "s"], [mvk + "s"])
            dve(stt(mv[:, 3:4], mv[:, 0:1], -1.0, mv[:, 2:3], ALU.mult, ALU.mult), [mvk, mvk + "s"], [mvk + "n"])
            act(actf(xn, xt, AF.Identity, bias=mv[:, 3:4], scale=mv[:, 2:3]), [xk, mvk + "s", mvk + "n"], [xnk])
            pt, pk = ps_alloc(4)
            ptb = pt.bitcast(BF16)
            pe(seq(*[tr(ptb[:, kc * 128:(kc + 1) * 128], xn[:, kc * 128:(kc + 1) * 128], identb) for kc in range(8)]),
               [xnk, "identb"], pk)
            for kc in range(8):
                o = uT[:, kc, T * 128:(T + 1) * 128]
                i = ptb[:, kc * 128:(kc + 1) * 128]
                if kc % 2 == 0:
                    act(actf(o, i, AF.Identity, bias=modc[:, kc, w:w + 1], scale=modc[:, 8 + kc, w:w + 1]),
                        pk + ["modc"], [f"uT{T}"])
                else:
                    dve(ts(o, i, modc[:, 8 + kc, w:w + 1], modc[:, kc, w:w + 1], ALU.mult, ALU.add),
                        pk + ["modc"], [f"uT{T}"])
        uT_all = [f"uT{T}" for T in range(NT)]

        wfr = Ring("wf", 2, [128, 8, 128], F32)
        wbr = Ring("wb", 4, [128, 8, 128], BF16)
        ZL = 4360
        bufA = alloc([128, ZL], F32)
        bufB = alloc([128, ZL], F32)
        bufC = alloc([128, ZL], F32)
        stg = Ring("stg", 3, [128, NTOK], BF16)
        wabf = alloc([128, 8, 16], F32)
        wabb = alloc([128, 8, 16], BF16)
        tmpab = alloc([128, NT, 8], F32)
        tmpab2 = alloc([128, NT, 8], F32)
        ssr = Ring("ss", 2, [128, 512], F32)

        dma(wabf, wab, [], ["wabf"])
        pool(cp(wabb, wabf), ["wabf"], ["wabb"])
        for T in range(NT):
            pt, pk = ps_alloc(1)
            pe(seq(*[mm(pt[:, 0:16], uT[:, kc, T * 128:(T + 1) * 128], wabb[:, kc, :], start=(kc == 0), stop=(kc == 7))
                     for kc in range(8)]), [f"uT{T}", "wabb"], pk)
            dve(cp(GB[:, T, :], pt[:, 0:16]), pk, ["GBraw"])
        a_ = tmpab
        b_ = tmpab2
        dve(tt(a_, GB[:, :, 0:8], rowc[:, 8:16].unsqueeze(1).to_broadcast([128, NT, 8]), ALU.add), ["GBraw", "rowc"], ["tmpab"])
        dve(stt(b_, a_, -1.0, a_, ALU.mult, ALU.max), ["tmpab"], ["tmpab2"])
        act(actf(b_, b_, AF.Exp, scale=-1.0), ["tmpab2"], ["tmpab2"])
        act(actf(b_, b_, AF.Ln, bias=1.0), ["tmpab2"], ["tmpab2"])
        dve(stt(a_, a_, 0.0, b_, ALU.max, ALU.add), ["tmpab", "tmpab2"], ["tmpab"])
        dve(tt(GB[:, :, 0:8], a_, rowc[:, 0:8].unsqueeze(1).to_broadcast([128, NT, 8]), ALU.mult), ["tmpab", "rowc", "GBraw"], ["GBg"])
        act(actf(GB[:, :, 8:16], GB[:, :, 8:16], AF.Sigmoid), ["GBraw"], ["GBb"])
        GBk = ["GBg", "GBb"]

        blocks = [(0, 256)] + [(256 + i * 512, 512) for i in range(8)]

        def cm_view(buf, r0, nr=8):
            v = buf[:, 256:256 + 4096].rearrange("p (w r) -> p w r", r=64)[:, :, r0:r0 + nr]
            return v.rearrange("p w r -> p r w")

        def ps_rw(pt):
            return pt.rearrange("p (r w) -> p r w", w=64)

        wpre = {}

        def prefetch_w(j):
            if j in wpre:
                return
            wf, wfk = wfr.next()
            wb, wbk = wbr.next()
            dma(wf, win[j], [], [wfk])
            pool(cp(wb, wf), [wfk], [wbk])
            wpre[j] = (wb, wbk)

        def project(j, lat_only, epi):
            prefetch_w(j)
            wb, wbk = wpre[j]
            for bi, (c0, n) in enumerate(blocks):
                if lat_only and bi == 0:
                    continue
                pt, pk = ps_alloc(4)
                T0 = c0 // 128
                pe(seq(*[mm(pt[:, 0:n], wb[:, kc, :], uT[:, kc, c0:c0 + n], start=(kc == 0), stop=(kc == 7))
                         for kc in range(8)]), [wbk] + [f"uT{T0 + i}" for i in range(n // 128)], pk)
                epi(bi, c0, n, pt, pk)

        def zpos(c0):
            return c0 + 2 if c0 < 256 else c0 + 6

        pool(lambda e: e.memset(bufA, 0.0), [], ["bufA"])

        def conv_chunk(j, kind, h):
            def epi(bi, c0, n, pt, pk):
                z0 = zpos(c0)
                act(cp(bufA[:, z0:z0 + n], pt[:, 0:n]) if False else actf(bufA[:, z0:z0 + n], pt[:, 0:n], AF.Identity), pk, ["bufA"])
            project(j, False, epi)
            L = 4356
            dve(ts(bufB[:, 0:L], bufA[:, 0:L], cw[:, j, 0:1], None, ALU.mult), ["bufA", "cw"], ["bufB"])
            for k in range(1, 5):
                dve(stt(bufB[:, 0:L], bufA[:, k:k + L], cw[:, j, k:k + 1], bufB[:, 0:L], ALU.mult, ALU.add),
                    ["bufA", "bufB", "cw"], ["bufB"])
            return lambda: conv_part2(j, kind, h)

        def conv_part2(j, kind, h):
            L = 4356
            sg, sgk = stg.next()
            if kind == 'v':
                act(actf(sg[:, 0:256], bufB[:, 0:256], AF.Silu), ["bufB"], [sgk])
                act(actf(sg[:, 256:NTOK], bufB[:, 260:4356], AF.Silu), ["bufB"], [sgk])
                dstore(ZV[h], sg, [sgk], ["ZV"])
                return
            act(actf(bufC[:, 0:L], bufB[:, 0:L], AF.Silu), ["bufB"], ["bufC"])
            pool(tt(bufB[:, 0:L], bufC[:, 0:L], bufC[:, 0:L], ALU.mult), ["bufC", "bufB"], ["bufB"])
            for (c0, n) in blocks:
                a0 = c0 if c0 < 256 else c0 + 4
                pt, pk = ps_alloc(4)
                pe(mm(pt[:, 0:n], onesf, bufB[:, a0:a0 + n]), ["onesf", "bufB"], pk)
                ss, ssk = ssr.next()
                act(actf(ss[:, 0:n], pt[:, 0:n], AF.Sqrt, bias=EPS), pk, [ssk])
                dve(lambda e, a=ss, n=n: e.reciprocal(out=a[:, 0:n], in_=a[:, 0:n]), [ssk], [ssk])
                dve(tt(sg[:, c0:c0 + n], bufC[:, a0:a0 + n], ss[:, 0:n], ALU.mult), [ssk, "bufC"], [sgk])
            dstore((ZQ if kind == 'q' else ZK)[h], sg, [sgk], ["ZQ" if kind == 'q' else "ZK"])

        def simple_chunk(j, func, dst, dkey, lat_only, cmo):
            sg, sgk = stg.next()

            def epi(bi, c0, n, pt, pk):
                if lat_only:
                    o = sg[:, c0 - 256:c0 - 256 + n]
                    i = pt[:, 0:n]
                elif bi == 0 or not cmo:
                    o = sg[:, c0:c0 + n]
                    i = pt[:, 0:n]
                else:
                    o = cm_view(sg, (c0 - 256) // 64)
                    i = ps_rw(pt)
                act(actf(o, i, func), pk, [sgk])
            project(j, lat_only, epi)
            if lat_only:
                dstore(dst, sg[:, 0:4096], [sgk], [dkey])
            else:
                dstore(dst, sg, [sgk], [dkey])

        def f_chunk(j, d, h):
            def epi(bi, c0, n, pt, pk):
                if bi == 0:
                    o = bufA[:, c0:c0 + n]
                    i = pt[:, 0:n]
                else:
                    o = cm_view(bufA, (c0 - 256) // 64)
                    i = ps_rw(pt)
                act(actf(o, i, AF.Sigmoid), pk, ["bufA"])
            project(j, False, epi)
            return lambda: f_part2(j, d, h)

        def f_part2(j, d, h):
            c = d * 4 + h
            dve(ts(bufB[:, 0:NTOK], bufA[:, 0:NTOK], omlb[:, c:c + 1], lbc[:, c:c + 1], ALU.mult, ALU.add),
                ["bufA", "omlb", "lbc"], ["bufB"])
            act(actf(bufC[:, 0:NTOK], bufB[:, 0:NTOK], AF.Ln), ["bufB"], ["bufC"])
            dstore(ZBL[c], bufC[:, 0:NTOK], ["bufC"], ["ZBL"])
            sg, sgk = stg.next()
            pool(ts(sg, bufA[:, 0:NTOK], nomlb[:, c:c + 1], omlb[:, c:c + 1], ALU.mult, ALU.add),
                 ["bufA", "nomlb", "omlb"], [sgk])
            dstore(ZBK[c], sg, [sgk], ["ZBK"])

        def do_chunk(j):
            if j < 4:
                return conv_chunk(j, 'q', j)
            elif j < 8:
                return conv_chunk(j, 'k', j - 4)
            elif j < 12:
                return conv_chunk(j, 'v', j - 8)
            elif j < 16:
                return simple_chunk(j, AF.Silu, ZAG[j - 12], "ZAG", True, False)
            elif j < 20:
                return simple_chunk(j, AF.Silu, ZBQ[j - 16], "ZBQ", False, True)
            elif j < 24:
                return f_chunk(j, 0, j - 20)
            elif j < 28:
                return f_chunk(j, 1, j - 24)
            elif j < 32:
                return simple_chunk(j, AF.Identity, ZBI[j - 28], "ZBI", False, True)
            elif j < 36:
                return simple_chunk(j, AF.Silu, ZBGT[j - 32], "ZBGT", True, False)
            else:
                return simple_chunk(j, AF.Sigmoid, ZM[j - 36], "ZM", True, False)

        if dbg and "chunks" in dbg:
            for j in dbg["chunks"]:
                p2_ = do_chunk(j)
                if p2_:
                    p2_()
        else:
            heavy = list(range(0, 12)) + list(range(20, 28))
            simple = list(range(12, 20)) + list(range(28, 52))
            order = []
            si = 0
            for j in heavy:
                order.append(("a", j))
                for _ in range(2 if j < 12 else 1):
                    if si < len(simple):
                        order.append(("s", simple[si]))
                        si += 1
                order.append(("b", j))
            while si < len(simple):
                order.append(("s", simple[si]))
                si += 1
            projs = [j for (k, j) in order if k != "b"]
            pending2 = {}
            pi = 0
            prefetch_w(projs[0])
            prefetch_w(projs[1])
            for (k, j) in order:
                if k == "b":
                    pending2.pop(j)()
                    continue
                if pi + 2 < len(projs):
                    prefetch_w(projs[pi + 2])
                pi += 1
                r_ = do_chunk(j)
                if k == "a":
                    pending2[j] = r_
        if dbg and dbg.get("dump_gb"):
            dma(DBG[:, 0:NT * 16], GB.rearrange("p a b -> p (a b)"), GBk, ["DBG"])
            dma(DBG[:, 1024:1024 + 32], modc.rearrange("p a b -> p (a b)"), ["modc"], ["DBG"])
            dma(DBG[:, 2048:3072], gate_row, ["gate_row"], ["DBG"])
        P.barrier()
        st["off"] = persist_off


        OTa = alloc([128, 4, 4096], BF16)
        OTb = alloc([128, 4, 4096], BF16)
        Sst = alloc([128, 8, 128], F32)
        Sb = alloc([128, 8, 128], BF16)
        ph3 = st["off"]
        r_q = Ring("gq", 4, [128, 4, 128], BF16)
        r_k = Ring("gk", 4, [128, 4, 128], BF16)
        r_v = Ring("gv", 4, [128, 4, 128], BF16)
        r_ex = Ring("gex", 6, [128, 16], F32)
        r_nb = Ring("gnb", 6, [128, 4], F32)
        r_gm = Ring("ggm", 4, [128, 8], F32)
        r_f = {n: Ring("g" + n, k, [128, 4, 128], F32) for n, k in
               (("M1", 2), ("E", 2), ("Es", 2), ("B", 2), ("BT", 2), ("X", 4), ("P", 4), ("PT", 4),
                ("Ub", 4), ("tmp", 2), ("ot", 2))}
        r_f["Ei"] = r_f["M1"]
        r_b = {n: Ring("g" + n, k, [128, 4, 128], BF16) for n, k in
               (("at", 4), ("Xb", 2), ("Kg", 2), ("Kd", 4), ("vt", 2), ("WT", 4), ("vn", 2))}
        r_s = Ring("gss", 8, [128, 4], F32)
        r_s4 = Ring("gs4", 4, [128, 8], F32)
        r_b["on4"] = Ring("gon4", 2, [128, 4, 128], BF16)
        r_ofo = Ring("ofo", 2, [128, 512], F32)
        r_ofi = Ring("ofi", 4, [128, 512], F32)

        def finalize4(ot, otk, OTkey, dest, scale, hgj=None):
            ss, ssk = r_s4.next()
            jk, jkk = r_f["tmp"].next()
            for h in range(4):
                act(actf(jk[:, h, :], ot[:, h, :], AF.Square, accum_out=ss[:, h:h + 1]), [otk], [jkk, ssk])
            dve(ts(ss[:, 4:8], ss[:, 0:4], 1.0 / 128.0, EPS / (scale * scale), ALU.mult, ALU.add), [ssk], [ssk + "b"])
            act(actf(ss[:, 4:8], ss[:, 4:8], AF.Sqrt), [ssk + "b"], [ssk + "b"])
            dve(lambda e, a=ss: e.reciprocal(out=a[:, 4:8], in_=a[:, 4:8]), [ssk + "b"], [ssk + "b"])
            on, onk = r_b["on4"].next()
            dve(tt(on, ot, bc4(ss[:, 4:8]), ALU.mult), [otk, ssk + "b"], [onk])
            pt, pk = ps_alloc(4)
            ptb = pt.bitcast(BF16)
            pe(seq(*[tr(ptb[:, h * 128:(h + 1) * 128], on[:, h, :], identb) for h in range(4)]), [onk, "identb"], pk)
            if hgj is None:
                act(cp(dest, v4(ptb[:, 0:512])), pk, [OTkey])
            else:
                srcv = ptb[:, 0:512].rearrange("p (h a b) -> p h a b", h=4, a=2)
                dstv = OTb.rearrange("p h (r w) -> p h r w", w=64)
                for w2 in range(2):
                    act(cp(dstv[:, :, :, 2 * hgj + w2], srcv[:, :, w2, :]), pk, [OTkey])

        def finalize(ot, otk, OTkey, colap, scale, tview=None):
            ss, ssk = r_s.next()
            jk, jkk = r_f["o1"].next()
            act(actf(jk, ot, AF.Square, accum_out=ss[:, 0:1]), [otk], [jkk, ssk])
            dve(ts(ss[:, 1:2], ss[:, 0:1], scale * scale / 128.0, EPS, ALU.mult, ALU.add), [ssk], [ssk + "b"])
            act(actf(ss[:, 1:2], ss[:, 1:2], AF.Sqrt), [ssk + "b"], [ssk + "b"])
            dve(lambda e, a=ss: e.reciprocal(out=a[:, 1:2], in_=a[:, 1:2]), [ssk + "b"], [ssk + "b"])
            on, onk = r_b["on"].next()
            dve(ts(on, ot, ss[:, 1:2], scale, ALU.mult, ALU.mult), [otk, ssk + "b"], [onk])
            pt, pk = ps_alloc(1)
            pe(tr(bfv(pt), on, identb), [onk, "identb"], pk)
            if tview is None:
                act(cp(colap, bfv(pt)), pk, [OTkey])
            else:
                for w2 in range(2):
                    act(cp(colap[:, :, w2], bfv(pt)[:, w2 * 64:(w2 + 1) * 64]), pk, [OTkey])

        pool(lambda e: e.memset(Sst, 0.0), [], ["S0", "S1"])
        pool(lambda e: e.memset(Sb, 0.0), [], ["Sb0", "Sb1"])

        def bc4(col4):
            return col4.unsqueeze(2).to_broadcast([128, 4, 128])

        def mb4(m):
            return m.unsqueeze(1).to_broadcast([128, 4, 128])

        def v4(ap):
            return ap.rearrange("p (h v) -> p h v", h=4)

        def gdn_prep(pairs, second, prs):
            for (T, d) in pairs:
                islat = T >= 2
                kT, kk = r_k.next()
                vT, vk = r_v.next()
                tsl = slice(T * 128, (T + 1) * 128)
                dma(kT, ZK[:, :, tsl].rearrange("h p t -> p h t"), ["ZK"], [kk])
                dma(vT, ZV[:, :, tsl].rearrange("h p t -> p h t"), ["ZV"], [vk])
                qT = qk = None
                if islat:
                    qT, qk = r_q.next()
                    dma(qT, ZQ[:, :, tsl].rearrange("h p t -> p h t"), ["ZQ"], [qk])
                ofi = ofik = None
                ofdefer = False
                if islat and second:
                    ofi, ofik = r_ofi.next()
                    ofdefer = f"OF{T - 2}" not in P.lastw
                    if not ofdefer:
                        dma(ofi, OFa[T - 2], [f"OF{T - 2}"], [ofik])
                ML, MR, MS = (BD_le, BD_gt, BD_lt) if d == 0 else (BD_ge, BD_lt, BD_gt)
                g4 = GB[:, T, d * 4:(d + 1) * 4]
                b4 = GB[:, T, 8 + d * 4:8 + (d + 1) * 4]
                pc, pck = ps_alloc(1)
                gm, gmk = r_gm.next()
                pool(ts(gm[:, 0:4], g4, BD_le[:, 63:64], None, ALU.mult), GBk + ["masks"], [gmk])
                pool(ts(gm[:, 4:8], g4, BD_ge[:, 64:65], None, ALU.mult), GBk + ["masks"], [gmk])
                pe(seq(mm(pc[:, 0:4], ML, g4), mm(pc[:, 4:8], MR, g4),
                       mm(pc[:, 8:12], onesf, gm[:, 0:4]), mm(pc[:, 12:16], onesf, gm[:, 4:8])),
                   ["masks", "onesf", gmk] + GBk, pck)
                ex, exk = r_ex.next()
                act(actf(ex, pc[:, 0:16], AF.Exp), pck, [exk])
                nb, nbk = r_nb.next()
                pool(ts(nb, b4, -1.0, None, ALU.mult), GBk, [nbk])
                prs.append(dict(T=T, d=d, islat=islat, qT=qT, qk=qk, kT=kT, kk=kk, vT=vT, vk=vk, ex=ex, exk=exk,
                                nb=nb, nbk=nbk, ML=ML, MR=MR, MS=MS, ofi=ofi, ofik=ofik, ofdefer=ofdefer,
                                g4=g4, b4=b4))
            gstage = dbg.get('gdn_stage', 99) if dbg else 99
            yield
            if gstage < 1:
                return
            for pr in prs:
                kT, kk = pr["kT"], pr["kk"]
                pKK, pKKk = ps_alloc(4)
                pe(seq(*[mm(pKK[:, h * 128:(h + 1) * 128], kT[:, h, :], kT[:, h, :]) for h in range(4)]), [kk], pKKk)
                M1, M1k = r_f["M1"].next()
                pool(tt(M1, mb4(pr["MR"]), bc4(pr["g4"]), ALU.mult), ["masks"] + GBk, [M1k])
                pD, pDk = ps_alloc(4)
                pe(seq(*[mm(pD[:, h * 128:(h + 1) * 128], M1[:, h, :], pr["ML"]) for h in range(4)]), [M1k, "masks"], pDk)
                E, Ek = r_f["E"].next()
                act(actf(E, v4(pD), AF.Exp), pDk, [Ek])
                Es, Esk = r_f["Es"].next()
                pool(tt(Es, E, mb4(pr["MS"]), ALU.mult), [Ek, "masks"], [Esk])
                pool(tt(Es, Es, bc4(pr["b4"]), ALU.mult), [Esk] + GBk, [Esk])
                B, Bk = r_f["B"].next()
                dve(tt(B, v4(pKK), Es, ALU.mult), pKKk + [Esk], [Bk])
                pr["B"], pr["Bk"] = B, Bk
                pr["at"] = pr["atk"] = None
                if pr["islat"]:
                    Ei, Eik = r_f["Ei"].next()
                    pool(tt(Ei, E, mb4(pr["ML"]), ALU.mult), [Ek, "masks"], [Eik])
                    pQK, pQKk = ps_alloc(4)
                    pe(seq(*[mm(pQK[:, h * 128:(h + 1) * 128], kT[:, h, :], pr["qT"][:, h, :]) for h in range(4)]),
                       [kk, pr["qk"]], pQKk)
                    at, atk = r_b["at"].next()
                    dve(tt(at, v4(pQK), Ei, ALU.mult), pQKk + [Eik], [atk])
                    pr["at"], pr["atk"] = at, atk
            yield
            if gstage < 2:
                return
            for pr in prs:
                B, Bk = pr["B"], pr["Bk"]
                pBT, pBTk = ps_alloc(4)
                pe(seq(*[mm(pBT[:, h * 128:(h + 1) * 128], B[:, h, :], identf) for h in range(4)]), [Bk, "identf"], pBTk)
                BT, BTk = r_f["BT"].next()
                act(cp(BT, v4(pBT)), pBTk, [BTk])
                X, Xk = r_f["X"].next()
                pool(tt(X, mb4(identf), B, ALU.subtract), [Bk, "identf"], [Xk])
                pr.update(P=B, Pk=Bk, PT=BT, PTk=BTk, X=X, Xk=Xk)
            yield
            if gstage < 3:
                return
            NL = dbg.get('gdn_levels', 5) if dbg else 5
            for lvl in range(NL):
                last = lvl == NL - 1
                for pr in prs:
                    Pm, Pk, PTm, PTk = pr["P"], pr["Pk"], pr["PT"], pr["PTk"]
                    pr["p2t"], pr["p2tk"] = ps_alloc(4)
                    pe(seq(*[mm(pr["p2t"][:, h * 128:(h + 1) * 128], Pm[:, h, :], PTm[:, h, :]) for h in range(4)]),
                       [Pk, PTk], pr["p2tk"])
                    if not last:
                        pr["p2"], pr["p2k"] = ps_alloc(4)
                        pe(seq(*[mm(pr["p2"][:, h * 128:(h + 1) * 128], PTm[:, h, :], Pm[:, h, :]) for h in range(4)]),
                           [Pk, PTk], pr["p2k"])
                yield
                for pr in prs:
                    nPT, nPTk = r_f["PT"].next()
                    act(cp(nPT, v4(pr["p2t"])), pr["p2tk"], [nPTk])
                    pr["PT"], pr["PTk"] = nPT, nPTk
                    if not last:
                        nP, nPk = r_f["P"].next()
                        dve(cp(nP, v4(pr["p2"])), pr["p2k"], [nPk])
                        pr["P"], pr["Pk"] = nP, nPk
                for pr in prs:
                    pr["px"], pr["pxk"] = ps_alloc(4)
                    pe(seq(*[mm(pr["px"][:, h * 128:(h + 1) * 128], pr["PT"][:, h, :], pr["X"][:, h, :]) for h in range(4)]),
                       [pr["PTk"], pr["Xk"]], pr["pxk"])
                yield
                for pr in prs:
                    if not last:
                        nX, nXk = r_f["X"].next()
                        dve(tt(nX, v4(pr["px"]), pr["X"], ALU.add), pr["pxk"] + [pr["Xk"]], [nXk])
                        pr["X"], pr["Xk"] = nX, nXk
                    else:
                        Xb, Xbk = r_b["Xb"].next()
                        dve(tt(Xb, v4(pr["px"]), pr["X"], ALU.add), pr["pxk"] + [pr["Xk"]], [Xbk])
                        pr["Xb"], pr["Xbk"] = Xb, Xbk
            if gstage < 4:
                return
            for pr in prs:
                ex, exk = pr["ex"], pr["exk"]
                pkt, pktk = ps_alloc(4)
                pktb = pkt.bitcast(BF16)
                pe(seq(*[tr(pktb[:, h * 128:(h + 1) * 128], pr["kT"][:, h, :], identb) for h in range(4)]),
                   [pr["kk"], "identb"], pktk)
                pvt, pvtk = ps_alloc(4)
                pvtb = pvt.bitcast(BF16)
                pe(seq(*[tr(pvtb[:, h * 128:(h + 1) * 128], pr["vT"][:, h, :], identb) for h in range(4)]),
                   [pr["vk"], "identb"], pvtk)
                Kg, Kgk = r_b["Kg"].next()
                dve(tt(Kg, v4(pktb[:, 0:512]), bc4(ex[:, 0:4]), ALU.mult), pktk + [exk], [Kgk])
                Kd, Kdk = r_b["Kd"].next()
                dve(tt(Kd, v4(pktb[:, 0:512]), bc4(ex[:, 4:8]), ALU.mult), pktk + [exk], [Kdk])
                vt, vtk = r_b["vt"].next()
                act(cp(vt, v4(pvtb[:, 0:512])), pvtk, [vtk])
                pU, pUk = ps_alloc(4)
                pe(seq(*[mm(pU[:, h * 128:(h + 1) * 128], pr["Xb"][:, h, :], vt[:, h, :]) for h in range(4)]),
                   [pr["Xbk"], vtk], pUk)
                Ub, Ubk = r_f["Ub"].next()
                dve(tt(Ub, v4(pU), bc4(pr["b4"]), ALU.mult), pUk + GBk, [Ubk])
                pW, pWk = ps_alloc(4)
                pe(seq(*[mm(pW[:, h * 128:(h + 1) * 128], Kg[:, h, :], pr["Xb"][:, h, :]) for h in range(4)]),
                   [Kgk, pr["Xbk"]], pWk)
                WT, WTk = r_b["WT"].next()
                act(cp(WT, v4(pW)), pWk, [WTk])
                pr.update(WT=WT, WTk=WTk, Ub=Ub, Ubk=Ubk, Kd=Kd, Kdk=Kdk)
            yield

        def gdn_chain(preps, second):
            for pr in preps:
                if pr["ofdefer"]:
                    dma(pr["ofi"], OFa[pr["T"] - 2], [f"OF{pr['T'] - 2}"], [pr["ofik"]])
            for pr in preps:
                pr["vn"], pr["vnk"] = r_b["vn"].next()
                if pr["islat"]:
                    pr["pO1"], pr["pO1k"] = PS[6 + pr["d"]][:, :], [f"psb{6 + pr['d']}"]
            for sub in range(2):
                def rs(pr):
                    j = sub if pr["d"] == 0 else 1 - sub
                    return j, slice(64 * j, 64 * j + 64)
                for pr in preps:
                    d = pr["d"]
                    j, r = rs(pr)
                    pP, pPk = ps_alloc(4)
                    pe(seq(*[mm(pP[r, h * 128:(h + 1) * 128], pr["WT"][:, h, r], Sb[:, d * 4 + h, :]) for h in range(4)]),
                       [pr["WTk"], f"Sb{d}"], pPk)
                    tmp, tmpk = r_f["tmp"].next()
                    dve(tt(tmp[r, :, :], v4(pP)[r, :, :], pr["nb"][r, :].unsqueeze(2).to_broadcast([64, 4, 128]), ALU.mult),
                        pPk + [pr["nbk"]], [tmpk])
                    pool(tt(pr["vn"][r, :, :], tmp[r, :, :], pr["Ub"][r, :, :], ALU.add), [tmpk, pr["Ubk"]], [pr["vnk"]])
                yield
                for pr in preps:
                    d = pr["d"]
                    j, r = rs(pr)
                    if pr["islat"]:
                        pe(seq(*[mm(pr["pO1"][r, h * 128:(h + 1) * 128], pr["qT"][:, h, r], Sb[:, d * 4 + h, :]) for h in range(4)]),
                           [pr["qk"], f"Sb{d}"], pr["pO1k"])
                    pS, pSk = ps_alloc(4)
                    pe(seq(*[mm(pS[:, h * 128:(h + 1) * 128], pr["Kd"][r, h, :], pr["vn"][r, h, :]) for h in range(4)]),
                       [pr["Kdk"], pr["vnk"]], pSk)
                    pr["pS"], pr["pSk"] = pS, pSk
                    S4 = Sst[:, d * 4:(d + 1) * 4, :]
                    pool(tt(S4, S4, bc4(pr["ex"][:, 8 + 4 * j:12 + 4 * j]), ALU.mult), [pr["exk"], f"S{d}"], [f"S{d}"])
                yield
                for pr in preps:
                    d = pr["d"]
                    S4 = Sst[:, d * 4:(d + 1) * 4, :]
                    dve(tt(S4, S4, v4(pr["pS"]), ALU.add), pr["pSk"] + [f"S{d}"], [f"S{d}"])
                    act(cp(Sb[:, d * 4:(d + 1) * 4, :], S4), [f"S{d}"], [f"Sb{d}"])
                yield
            for pr in preps:
                if not pr["islat"]:
                    continue
                lt = pr["T"] - 2
                pO2, pO2k = ps_alloc(4)
                pe(seq(*[mm(pO2[:, h * 128:(h + 1) * 128], pr["at"][:, h, :], pr["vn"][:, h, :]) for h in range(4)]),
                   [pr["atk"], pr["vnk"]], pO2k)
                tmp, tmpk = r_f["tmp"].next()
                dve(tt(tmp, v4(pr["pO1"]), bc4(pr["ex"][:, 0:4]), ALU.mult), pr["pO1k"] + [pr["exk"]], [tmpk])
                if not second:
                    ofo, ofok = r_ofo.next()
                    dve(tt(ofo, pO2, tmp.rearrange("p h v -> p (h v)"), ALU.add), pO2k + [tmpk], [ofok])
                    dstore(OFa[lt], ofo, [ofok], [f"OF{lt}"])
                else:
                    ot, otk = r_f["ot"].next()
                    dve(tt(ot, v4(pO2), tmp, ALU.add), pO2k + [tmpk], [otk])
                    pool(tt(ot, ot, v4(pr["ofi"]), ALU.add), [otk, pr["ofik"]], [otk])
                    finalize4(ot, otk, "OTa", OTa[:, :, lt * 128:(lt + 1) * 128], QSCALE)

            yield

        bwd_order = [1, 0] + list(range(33, 1, -1))
        nsteps = dbg.get("gdn_steps", NT) if dbg else NT
        def run_gen(g):
            for _ in g:
                pass

        def interleave(ga, gb, ratio=3):
            alive_a, alive_b = ga is not None, gb is not None
            while alive_a or alive_b:
                for _ in range(ratio):
                    if alive_a:
                        psst["grp"] = 'p'
                        try:
                            next(ga)
                        except StopIteration:
                            alive_a = False
                if alive_b:
                    psst["grp"] = 'c'
                    try:
                        next(gb)
                    except StopIteration:
                        alive_b = False
            psst["grp"] = None

        if not (dbg and dbg.get("skip_gdn")):
            cur = []
            psst["grp"] = 'p'
            run_gen(gdn_prep([(0, 0), (bwd_order[0], 1)], False, cur))
            psst["grp"] = None
            for i in range(nsteps):
                nxt = []
                gp = gdn_prep([(i + 1, 0), (bwd_order[i + 1], 1)], (i + 1) >= 18, nxt) if i + 1 < nsteps else None
                gc = gdn_chain(cur, second=(i >= 18)) if not (dbg and dbg.get('gdn_stage', 99) < 5) else None
                interleave(gp, gc)
                cur = nxt
        if dbg and dbg.get("dump_ota"):
            dma(dbg_ota, OTa, ["OTa"], ["dbg_ota"])
            dma(DBG[:, 4096:4096 + 1024], Sst.rearrange("p a b -> p (a b)"), ["S0", "S1"], ["DBG"])
        P.barrier()
        st["off"] = ph3

        h_q = Ring("hq", 4, [128, 4, 128], BF16)
        h_k = Ring("hk", 4, [128, 4, 128], BF16)
        h_i = Ring("hi", 4, [128, 4, 128], BF16)
        h_g = Ring("hg", 4, [128, 512], F32)
        h_w = {n: Ring("h" + n, 3, [128, 512], F32) for n in ("G", "eq", "ek", "ed", "Gx", "enx")}
        h_qg = Ring("hqg", 4, [128, 4, 128], BF16)
        h_kg = Ring("hkg", 3, [128, 4, 128], BF16)
        h_kd = Ring("hkd", 3, [128, 4, 128], BF16)
        h_gl = Ring("hgl", 4, [128, 16], F32)
        h_at = Ring("hat", 16, [128, 128], BF16)
        h_kt = Ring("hkt", 16, [128, 128], BF16)
        h_vt = Ring("hvt", 16, [128, 128], BF16)
        r_s = Ring("hss", 8, [128, 4], F32)
        r_f = {"ot": Ring("hot", 2, [128, 4, 128], F32), "tmp": Ring("htmp", 2, [128, 4, 128], F32)}
        r_b = {"on4": Ring("hon4", 2, [128, 4, 128], BF16)}
        r_s4 = Ring("hs4", 4, [128, 8], F32)
        r_ofo = Ring("hofo", 2, [128, 512], F32)
        r_ofi = Ring("hofi", 6, [128, 512], F32)
        pool(lambda e: e.memset(Sst, 0.0), ["S0", "S1"], ["S0", "S1"])
        pool(lambda e: e.memset(Sb, 0.0), ["Sb0", "Sb1"], ["Sb0", "Sb1"])

        def bc8(ap8):
            return ap8.unsqueeze(2).to_broadcast([128, 8, 64])

        def v864(ap):
            return ap.rearrange("p (a b) -> p a b", b=64)

        def hg_prep(pairs, second, prs):
            for (T, d) in pairs:
                islat = T >= 2
                tsl = slice(T * 128, (T + 1) * 128)
                qT, qk = h_q.next()
                kT, kk = h_k.next()
                iT, ik = h_i.next()
                gT, gk = h_g.next()
                dma(qT, ZBQ[:, :, tsl].rearrange("h p t -> p h t"), ["ZBQ"], [qk])
                dma(kT, ZBK[d * 4:(d + 1) * 4, :, tsl].rearrange("h p t -> p h t"), ["ZBK"], [kk])
                dma(iT, ZBI[:, :, tsl].rearrange("h p t -> p h t"), ["ZBI"], [ik])
                dma(gT.rearrange("p (h t) -> p h t", h=4), ZBL[d * 4:(d + 1) * 4, :, tsl].rearrange("h p t -> p h t"), ["ZBL"], [gk])
                ofi = ofik = None
                ofdefer = False
                if islat and second:
                    ofi, ofik = r_ofi.next()
                    ofdefer = f"OFb{T - 2}" not in P.lastw
                    if not ofdefer:
                        dma(ofi, OFb[T - 2], [f"OFb{T - 2}"], [ofik])
                G, Gk = h_w["G"].next()
                dve(lambda e, a=G, b=gT: e.tensor_tensor_scan(out=a, data0=segm, data1=b, initial=0.0, op0=ALU.mult, op1=ALU.add),
                    [gk, "segm"], [Gk])
                gl, glk = h_gl.next()
                Glast = v864(G)[:, :, 63]
                eq, eqk = h_w["eq"].next()
                ek, ekk = h_w["ek"].next()
                ed, edk = h_w["ed"].next()
                act(actf(gl[:, 0:8], Glast, AF.Exp), [Gk], [glk])
                if d == 0:
                    act(actf(eq, G, AF.Exp), [Gk], [eqk])
                    act(actf(ek, G, AF.Exp, scale=-1.0), [Gk], [ekk])
                    dve(tt(v864(ed), v864(ek), bc8(gl[:, 0:8]), ALU.mult), [ekk, glk], [edk])
                else:
                    Gx, Gxk = h_w["Gx"].next()
                    enx, enxk = h_w["enx"].next()
                    act(actf(gl[:, 8:16], Glast, AF.Exp, scale=-1.0), [Gk], [glk])
                    pool(tt(Gx, G, gT, ALU.subtract), [Gk, gk], [Gxk])
                    act(actf(ed, Gx, AF.Exp), [Gxk], [edk])
                    act(actf(enx, Gx, AF.Exp, scale=-1.0), [Gxk], [enxk])
                    dve(tt(v864(eq), v864(enx), bc8(gl[:, 0:8]), ALU.mult), [enxk, glk], [eqk])
                    pool(tt(v864(ek), v864(ed), bc8(gl[:, 8:16]), ALU.mult), [edk, glk], [ekk])
                yield
                qg, qgk = h_qg.next()
                kg, kgk = h_kg.next()
                kd, kdk = h_kd.next()
                f2 = lambda a: a.rearrange("p h t -> p (h t)")
                dve(tt(f2(qg), f2(qT), eq, ALU.mult), [qk, eqk], [qgk])
                pool(tt(f2(kg), f2(kT), ek, ALU.mult), [kk, ekk], [kgk])
                pool(tt(f2(kd), f2(kT), ed, ALU.mult), [kk, edk], [kdk])
                yield
                MI = BD_le if d == 0 else BD_ge
                heads = []
                for h in range(4):
                    c = d * 4 + h
                    at = atk = None
                    if islat:
                        pA, pAk = ps_alloc(1)
                        pe(mm(pA, kg[:, h, :], qg[:, h, :]), [kgk, qgk], pAk)
                        at, atk = h_at.next()
                        dve(tt(at, pA, MI, ALU.mult), pAk + ["masks"], [atk])
                    pk_, pkk = ps_alloc(1)
                    pe(tr(bfv(pk_), kd[:, h, :], identb), [kdk, "identb"], pkk)
                    kt, ktk = h_kt.next()
                    act(cp(kt, bfv(pk_)), pkk, [ktk])
                    pv_, pvk = ps_alloc(1)
                    pe(tr(bfv(pv_), iT[:, h, :], identb), [ik, "identb"], pvk)
                    vt, vtk = h_vt.next()
                    act(cp(vt, bfv(pv_)), pvk, [vtk])
                    heads.append(dict(h=h, c=c, at=at, atk=atk, kt=kt, ktk=ktk, vt=vt, vtk=vtk))
                prs.append(dict(T=T, d=d, islat=islat, qg=qg, qgk=qgk, gl=gl, glk=glk, ofi=ofi, ofik=ofik,
                                ofdefer=ofdefer, heads=heads))
            yield

        def otb_view(j):
            def colap(h):
                return OTb[:, h, :].rearrange("p (r w) -> p r w", w=64)[:, :, 2 * j:2 * j + 2]
            return colap

        def hg_chain(preps, second):
            for pr in preps:
                if pr["islat"]:
                    pr["pO"], pr["pOk"] = PS[6 + pr["d"]][:, :], [f"psb{6 + pr['d']}"]
            for sub in range(2):
                def rs(pr):
                    j = sub if pr["d"] == 0 else 1 - sub
                    return j, slice(64 * j, 64 * j + 64)
                for pr in preps:
                    d = pr["d"]
                    j, r = rs(pr)
                    hs = pr["heads"]
                    if pr["islat"]:
                        ops = []
                        for hd in hs:
                            h = hd["h"]
                            ops.append(mm(pr["pO"][r, h * 128:(h + 1) * 128], pr["qg"][:, h, r], Sb[:, d * 4 + h, :], start=True, stop=False))
                            ops.append(mm(pr["pO"][r, h * 128:(h + 1) * 128], hd["at"][r, r], hd["vt"][r, :], start=False, stop=True))
                        pe(seq(*ops), [pr["qgk"], f"Sb{d}"] + [hd["atk"] for hd in hs] + [hd["vtk"] for hd in hs], pr["pOk"])
                    pS, pSk = ps_alloc(4)
                    pe(seq(*[mm(pS[:, hd["h"] * 128:(hd["h"] + 1) * 128], hd["kt"][r, :], hd["vt"][r, :]) for hd in hs]),
                       [hd["ktk"] for hd in hs] + [hd["vtk"] for hd in hs], pSk)
                    pr["pS"], pr["pSk"] = pS, pSk
                    S4 = Sst[:, d * 4:(d + 1) * 4, :]
                    glj = pr["gl"][:, 0:8].rearrange("p (h j) -> p h j", j=2)[:, :, j]
                    pool(tt(S4, S4, bc4(glj), ALU.mult), [pr["glk"], f"S{d}"], [f"S{d}"])
                yield
                for pr in preps:
                    d = pr["d"]
                    S4 = Sst[:, d * 4:(d + 1) * 4, :]
                    dve(tt(S4, S4, v4(pr["pS"]), ALU.add), pr["pSk"] + [f"S{d}"], [f"S{d}"])
                    act(cp(Sb[:, d * 4:(d + 1) * 4, :], S4), [f"S{d}"], [f"Sb{d}"])
                yield
            for pr in preps:
                if pr["ofdefer"]:
                    assert f"OFb{pr['T'] - 2}" in P.lastw
                    dma(pr["ofi"], OFb[pr["T"] - 2], [f"OFb{pr['T'] - 2}"], [pr["ofik"]])
            for pr in preps:
                if not pr["islat"]:
                    continue
                j = pr["T"] - 2
                if not second:
                    ofo, ofok = r_ofo.next()
                    act(cp(ofo, pr["pO"]), pr["pOk"], [ofok])
                    dstore(OFb[j], ofo, [ofok], [f"OFb{j}"])
                else:
                    ot, otk = r_f["ot"].next()
                    dve(tt(ot, v4(pr["pO"]), v4(pr["ofi"]), ALU.add), pr["pOk"] + [pr["ofik"]], [otk])
                    finalize4(ot, otk, "OTb", None, QSCALE, hgj=j)
            yield

        hsteps = dbg.get("hg_steps", NT) if dbg else NT
        if not (dbg and dbg.get("skip_hg")):
            cur = []
            psst["grp"] = 'p'
            run_gen(hg_prep([(0, 0), (bwd_order[0], 1)], False, cur))
            psst["grp"] = None
            for i in range(hsteps):
                nxt = []
                gp = hg_prep([(i + 1, 0), (bwd_order[i + 1], 1)], (i + 1) >= 18, nxt) if i + 1 < hsteps else None
                gc = hg_chain(cur, second=(i >= 18))
                interleave(gp, gc, ratio=1)
                cur = nxt
        if dbg and dbg.get("dump_otb"):
            dma(dbg_otb, OTb, ["OTb"], ["dbg_otb"])
        P.barrier()
        st["off"] = ph3

        waoB = alloc([128, 4, 1024], BF16)
        wboB = alloc([128, 4, 1024], BF16)
        woB = alloc([128, 8, 1024], BF16)
        lng_row = alloc([128, 1024], F32)
        lnb_row = alloc([128, 1024], F32)
        Hr = Ring("oH", 3, [128, 1024], F32)
        xr4 = Ring("ox", 2, [128, 1024], F32)
        agr = Ring("oag", 1, [128, 4, 512], BF16)
        bgr = Ring("obg", 1, [128, 4, 512], BF16)
        mr = Ring("om", 4, [128, 2, 512], BF16)
        mar = Ring("oma", 1, [128, 4, 512], BF16)
        mbr = Ring("omb", 1, [128, 4, 512], BF16)
        t1r = Ring("ot1", 2, [128, 512], F32)
        t2r = Ring("ot2", 2, [128, 512], F32)
        mixr = Ring("omix", 2, [128, 8, 512], BF16)
        st4 = Ring("ost", 2, [128, 2, 6], F32)
        mv4 = Ring("omv", 2, [128, 4], F32)
        dma(lng_row, lng.partition_broadcast(128), [], ["lng_row"])
        dma(lnb_row, lnb.partition_broadcast(128), [], ["lnb_row"])
        for (wsrc, wdst, n, key) in ((wao, waoB, 4, "waoB"), (wbo, wboB, 4, "wboB"), (wo, woB, 8, "woB")):
            for i in range(n):
                hbuf, hk = Hr.next()
                dma(hbuf, wsrc[:, i, :], [], [hk])
                pool(cp(wdst[:, i, :], hbuf), [hk], [key])
        nblk = dbg.get("out_blocks", 8) if dbg else 8
        for bb in range(nblk):
            bsl = slice(bb * 512, (bb + 1) * 512)
            ag, agk = agr.next()
            bg, bgk = bgr.next()
            dma(ag, ZAG[:, :, bsl].rearrange("h p t -> p h t"), ["ZAG"], [agk])
            dma(bg, ZBGT[:, :, bsl].rearrange("h p t -> p h t"), ["ZBGT"], [bgk])
            ma, mak = mar.next()
            mb, mbk = mbr.next()
            dve(stt(ma, OTa[:, :, bsl], gains[:, 0:1], ag, ALU.mult, ALU.mult), ["OTa", "gains", agk], [mak])
            dve(stt(mb, OTb[:, :, bsl], gains[:, 1:2], bg, ALU.mult, ALU.mult), ["OTb", "gains", bgk], [mbk])
            mix, mixk = mixr.next()
            for cc in range(8):
                mt, mtk = mr.next()
                dma(mt[:, 0, :], ZM[cc, :, bsl], ["ZM"], [mtk])
                dma(mt[:, 1, :], ZM[8 + cc, :, bsl], ["ZM"], [mtk])
                pYa, pYak = ps_alloc(4)
                pe(seq(*[mm(pYa, waoB[:, h, cc * 128:(cc + 1) * 128], ma[:, h, :], start=(h == 0), stop=(h == 3))
                         for h in range(4)]), ["waoB", mak], pYak)
                pYb, pYbk = ps_alloc(4)
                pe(seq(*[mm(pYb, wboB[:, h, cc * 128:(cc + 1) * 128], mb[:, h, :], start=(h == 0), stop=(h == 3))
                         for h in range(4)]), ["wboB", mbk], pYbk)
                t1, t1k = t1r.next()
                t2, t2k = t2r.next()
                dve(tt(t1, pYa, mt[:, 0, :], ALU.mult), pYak + [mtk], [t1k])
                dve(tt(t2, pYb, mt[:, 1, :], ALU.mult), pYbk + [mtk], [t2k])
                pool(tt(mix[:, cc, :], t1, t2, ALU.add), [t1k, t2k], [mixk])
            for ti in range(4):
                lt = bb * 4 + ti
                xt, xk = xr4.next()
                dma(xt, x[lt * 128:(lt + 1) * 128, :], [], [xk])
                H, Hk = Hr.next()
                for half in range(2):
                    pSu, pSuk = ps_alloc(4)
                    pe(seq(*[mm(pSu, mix[:, cc, ti * 128:(ti + 1) * 128], woB[:, cc, half * 512:(half + 1) * 512],
                                start=(cc == 0), stop=(cc == 7)) for cc in range(8)]), [mixk, "woB"], pSuk)
                    dve(tt(H[:, half * 512:(half + 1) * 512], pSu, gate_row[:, half * 512:(half + 1) * 512], ALU.mult),
                        pSuk + ["gate_row"], [Hk])
                act(actf(xt, xt, AF.Copy, scale=ALPHA), [xk], [xk])
                pool(tt(H, H, xt, ALU.add), [Hk, xk], [Hk])
                stt_, stk = st4.next()
                mv, mvk = mv4.next()
                dve(seq(lambda e, a=stt_, b=H: e.bn_stats(out=a[:, 0, :], in_=b[:, 0:512]),
                        lambda e, a=stt_, b=H: e.bn_stats(out=a[:, 1, :], in_=b[:, 512:1024])), [Hk], [stk])
                dve(lambda e, a=mv, b=stt_: e.bn_aggr(out=a[:, 0:2], in_=b.rearrange("p a b -> p (a b)")), [stk], [mvk])
                act(actf(mv[:, 2:3], mv[:, 1:2], AF.Sqrt, bias=EPS), [mvk], [mvk + "s"])
                dve(lambda e, a=mv: e.reciprocal(out=a[:, 2:3], in_=a[:, 2:3]), [mvk + "s"], [mvk + "s"])
                dve(stt(mv[:, 3:4], mv[:, 0:1], -1.0, mv[:, 2:3], ALU.mult, ALU.mult), [mvk, mvk + "s"], [mvk + "n"])
                act(actf(H, H, AF.Identity, bias=mv[:, 3:4], scale=mv[:, 2:3]), [Hk, mvk + "s", mvk + "n"], [Hk])
                dve(tt(H, H, lng_row, ALU.mult), [Hk, "lng_row"], [Hk])
                pool(tt(H, H, lnb_row, ALU.add), [Hk, "lnb_row"], [Hk])
                dstore(y[lt * 128:(lt + 1) * 128, :], H, [Hk], ["y"])

        final_cnt = dict(P.cnt)

        @block.sync
        def _(e):
            P.replay('sp', e, sems)
            for l, n in final_cnt.items():
                if l[0] == 'd' and l != 'dve':
                    e.wait_ge(sems[l], 16 * n)

        @block.tensor
        def _(e):
            P.replay('pe', e, sems)

        @block.scalar
        def _(e):
            P.replay('act', e, sems)

        @block.vector
        def _(e):
            P.replay('dve', e, sems)

        @block.gpsimd
        def _(e):
            P.replay('pool', e, sems)
    return nc, P


def _consts():
    p = np.arange(128)[:, None]
    f = np.arange(128)[None, :]
    same = (p // 64) == (f // 64)
    m = np.stack([p <= f, p < f, p >= f, p > f, (p <= f) & same, (p < f) & same, (p >= f) & same, (p > f) & same], axis=1).astype(np.float32)
    segm = np.ones((128, 512), np.float32)
    segm[:, ::64] = 0.0
    return np.eye(128, dtype=np.float32), np.ascontiguousarray(m), segm


def make_in_maps(inp):
    f = np.float32
    A = lambda a: np.ascontiguousarray(a, dtype=f)
    w_in = inp['w_in'][0]
    cols = np.r_[0:1536, 1552:6672]
    win = A(w_in[:, cols].reshape(8, 128, 52, 128).transpose(2, 1, 0, 3))
    wab = A(w_in[:, 1536:1552].reshape(8, 128, 16).transpose(1, 0, 2))
    w_mod = inp['w_mod'][0]
    wmod = A(w_mod.reshape(8, 128, 6, 512).transpose(2, 1, 0, 3))
    b_mod = inp['b_mod'][0]
    bmodc = A(b_mod[:2048].reshape(16, 128).T)
    bmodg = A(b_mod[2048:].reshape(1, 1024))
    convw = A(inp['conv_w'][0].reshape(5, 12, 128).transpose(2, 1, 0))
    alog = A(inp['a_log'][0].reshape(1, 8))
    dtb = A(inp['dt_bias'][0].reshape(1, 8))
    lbp = A(inp['lb_param'].reshape(2, 2, 4, 128).transpose(3, 0, 1, 2).reshape(128, 2, 8))
    ang = A(inp['a_norm_g'][0].reshape(128, 1))
    bng = A(inp['b_norm_g'][0].reshape(128, 1))
    wao = A(inp['w_a_out'][0].reshape(4, 128, 1024).transpose(1, 0, 2))
    wbo = A(inp['w_b_out'][0].reshape(4, 128, 1024).transpose(1, 0, 2))
    wo = A(inp['w_out'][0].reshape(8, 128, 1024).transpose(1, 0, 2))
    lng = A(inp['ln_g'][0].reshape(1, 1024))
    lnb = A(inp['ln_b'][0].reshape(1, 1024))
    cidf, cmask, csegm = _consts()
    maps = []
    for b in range(8):
        ccv = np.stack([inp['c'][b].reshape(8, 128).T, inp['c_ctx'].reshape(8, 128).T], axis=2)
        maps.append(dict(x=A(inp['x'][b]), ctx=A(inp['ctx'][b]), cc=A(ccv), wmod=wmod, bmodc=bmodc, bmodg=bmodg,
                         win=win, wab=wab, convw=convw, alog=alog, dtb=dtb, lbp=lbp, ang=ang, bng=bng,
                         wao=wao, wbo=wbo, wo=wo, lng=lng, lnb=lnb, cidf=cidf, cmask=cmask, csegm=csegm))
    return maps


def kernel(**inputs):
    nc, _ = build()
    maps = make_in_maps(inputs)
    res = run_bass_kernel_spmd(nc, maps, core_ids=list(range(8)))
    return np.stack([np.asarray(r["y"], dtype=np.float32) for r in res.results], axis=0)
```

```python
import numpy as np
import ml_dtypes
from contextlib import ExitStack
import concourse.bass as bass
import concourse.mybir as mybir
from concourse.bass_utils import run_bass_kernel_spmd

F32 = mybir.dt.float32
BF16 = mybir.dt.bfloat16
U8 = mybir.dt.uint8
AF = mybir.ActivationFunctionType
ALU = mybir.AluOpType

NT = 34
NTOK = 4352
QSCALE = 128 ** -0.5
ALPHA = 2.0 ** 0.25
EPS = 1e-6


class Prog:
    ENG = ('pe', 'act', 'dve', 'pool', 'sp')

    def __init__(self):
        self.ops = {e: [] for e in self.ENG}
        self.cnt = {}
        self.know = {e: {} for e in self.ENG}
        self.opclock = {}
        self.lastw = {}
        self.readers = {}
        self.pending = {e: {} for e in self.ENG}
        self.nlanes = {'sp': 8, 'pool': 6, 'act': 4}
        self.rr = {e: 0 for e in self.ENG}

    def lanes(self):
        out = list(self.ENG[:4])
        for q, n in self.nlanes.items():
            out += [f'd{q}{i}' for i in range(n)]
        return out

    def barrier(self):
        snap = dict(self.cnt)
        for e in self.ENG:
            p = self.pending[e]
            for l, n in snap.items():
                if p.get(l, 0) < n:
                    p[l] = n

    def emit(self, eng, fn, reads=(), writes=(), dma=False):
        if dma:
            i = self.rr[eng]
            self.rr[eng] = (i + 1) % self.nlanes[eng]
            lane = f'd{eng}{i}'
        else:
            lane = eng
        psr = [r for r in reads if r.startswith('psb')]
        if psr:
            reads = [r for r in reads if not r.startswith('psb')]
            writes = list(writes) + psr
        deps = dict(self.pending[eng])
        self.pending[eng] = {}

        def add(l, n):
            if deps.get(l, 0) < n:
                deps[l] = n
        for r in reads:
            w = self.lastw.get(r)
            if w:
                add(*w)
        for r in writes:
            w = self.lastw.get(r)
            if w and not (w[0] == lane and not dma):
                add(*w)
            for l, n in self.readers.get(r, {}).items():
                if not (l == lane and not dma):
                    add(l, n)
        if dma and self.cnt.get(lane, 0) > 0:
            add(lane, self.cnt[lane])
        know = self.know[eng]
        waits = [(l, n) for l, n in deps.items() if know.get(l, 0) < n]
        for l, n in deps.items():
            for l2, n2 in self.opclock.get((l, n), {}).items():
                if know.get(l2, 0) < n2:
                    know[l2] = n2
            if know.get(l, 0) < n:
                know[l] = n
        n = self.cnt.get(lane, 0) + 1
        self.cnt[lane] = n
        self.opclock[(lane, n)] = dict(know)
        self.ops[eng].append((waits, fn, lane))
        for r in reads:
            self.readers.setdefault(r, {})[lane] = n
        for r in writes:
            self.lastw[r] = (lane, n)
            self.readers[r] = {}

    def replay(self, name, eng, sems):
        for waits, fn, lane in self.ops[name]:
            for l, n in waits:
                eng.wait_ge(sems[l], n * (16 if l[0] == 'd' and l != 'dve' else 1))
            inst = fn(eng)
            inst.then_inc(sems[lane], 16 if (lane[0] == 'd' and lane != 'dve') else 1)


def seq(*fns):
    def f(e):
        r = None
        for g in fns:
            r = g(e)
        return r
    return f


def build(dbg=None):
    nc = bass.Bass("TRN2", target_bir_lowering=False)
    P = Prog()

    def din(name, shape, dt=F32):
        return nc.dram_tensor(name, list(shape), dt, kind="ExternalInput").ap()

    x = din("x", [4096, 1024])
    ctx = din("ctx", [256, 1024])
    cc = din("cc", [128, 8, 2])
    wmod = din("wmod", [6, 128, 8, 512])
    bmodc = din("bmodc", [128, 16])
    bmodg = din("bmodg", [1, 1024])
    win = din("win", [52, 128, 8, 128])
    wab = din("wab", [128, 8, 16])
    convw = din("convw", [128, 12, 5])
    alog = din("alog", [1, 8])
    dtb = din("dtb", [1, 8])
    lbp = din("lbp", [128, 2, 8])
    ang = din("ang", [128, 1])
    bng = din("bng", [128, 1])
    wao = din("wao", [128, 4, 1024])
    wbo = din("wbo", [128, 4, 1024])
    wo = din("wo", [128, 8, 1024])
    lng = din("lng", [1, 1024])
    lnb = din("lnb", [1, 1024])
    cidf = din("cidf", [128, 128])
    cmask = din("cmask", [128, 8, 128])
    csegm = din("csegm", [128, 512])
    y = nc.dram_tensor("y", [4096, 1024], F32, kind="ExternalOutput").ap()

    def dscr(name, shape, dt):
        kind = {"kind": "ExternalOutput"} if (dbg and name in dbg) else {}
        return nc.dram_tensor(name, list(shape), dt, **kind).ap()

    ZQ = dscr("ZQ", [4, 128, NTOK], BF16)
    ZK = dscr("ZK", [4, 128, NTOK], BF16)
    ZV = dscr("ZV", [4, 128, NTOK], BF16)
    ZAG = dscr("ZAG", [4, 128, 4096], BF16)
    ZBGT = dscr("ZBGT", [4, 128, 4096], BF16)
    ZM = dscr("ZM", [16, 128, 4096], BF16)
    ZBQ = dscr("ZBQ", [4, 128, NTOK], BF16)
    ZBK = dscr("ZBK", [8, 128, NTOK], BF16)
    ZBL = dscr("ZBL", [8, 128, NTOK], F32)
    ZBI = dscr("ZBI", [4, 128, NTOK], BF16)
    OFa = dscr("OFa", [32, 128, 512], F32)
    OFb = dscr("OFb", [32, 128, 512], F32)
    DBG = dscr("DBG", [128, 8192], F32) if dbg else None
    dbg_ota = dscr("dbg_ota", [128, 4, 4096], BF16) if dbg else None
    dbg_otb = dscr("dbg_otb", [128, 4, 4096], BF16) if dbg else None

    es = ExitStack()
    with es:
        ARENA = 204 * 1024
        arena = es.enter_context(nc.sbuf_tensor("arena", [128, ARENA], U8))
        PS = [es.enter_context(nc.psum_tensor(f"ps{i}", [128, 512], F32)) for i in range(8)]
        sems = {l: es.enter_context(nc.semaphore(f"s_{l}")) for l in P.lanes()}
        block = es.enter_context(nc.Block())

        st = {"off": 0, "n": 0}

        def alloc(shape, dt, name=None):
            nb = int(np.prod(shape[1:])) * (4 if dt == F32 else 2)
            off = (st["off"] + 63) // 64 * 64
            assert off + nb <= ARENA, (name, off, nb)
            st["off"] = off + nb
            ap = arena[:, off:off + nb].bitcast(dt)
            if len(shape) == 3:
                ap = ap.rearrange("p (a b) -> p a b", a=shape[1])
            elif len(shape) == 4:
                ap = ap.rearrange("p (a b c) -> p a b c", a=shape[1], b=shape[2])
            st["n"] += 1
            return ap

        class Ring:
            def __init__(self, name, n, shape, dt):
                self.name = name
                self.aps = [alloc(shape, dt, name) for _ in range(n)]
                self.i = 0

            def next(self):
                i = self.i
                self.i = (i + 1) % len(self.aps)
                return self.aps[i], f"{self.name}#{i}"

        psst = {"i": 0, "q": [0] * 8, "ip": 0, "ic": 0, "grp": None}

        def ps_alloc(nq=1):
            grp = psst["grp"]
            if grp == 'p':
                b = psst["ip"] % 4
                psst["ip"] += 1
            elif grp == 'c':
                b = 4 + psst["ic"] % 2
                psst["ic"] += 1
            else:
                b = psst["i"] % 6
                psst["i"] += 1
            if nq == 4:
                s_ = 0
            else:
                s_ = psst["q"][b]
                psst["q"][b] = (s_ + nq) % 4
                assert s_ + nq <= 4
            ap = PS[b][:, s_ * 128:(s_ + nq) * 128]
            return ap, [f"psb{b}"]

        def bfv(ap):
            return ap.bitcast(BF16)[:, 0:128]

        STQ = 'pool'
        pe = lambda fn, r, w: P.emit('pe', fn, r, w)
        act = lambda fn, r, w: P.emit('act', fn, r, w)
        dve = lambda fn, r, w: P.emit('dve', fn, r, w)
        pool = lambda fn, r, w: P.emit('pool', fn, r, w)

        def dma(out, in_, r, w, q='sp'):
            P.emit(q, lambda e: e.dma_start(out=out, in_=in_), r, w, dma=True)

        def dstore(out, in_, r, w):
            dma(out, in_, r, w, q=STQ)

        def mm(out, lhsT, rhs, start=True, stop=True):
            return lambda e: e.matmul(out, lhsT=lhsT, rhs=rhs, start=start, stop=stop)

        def tr(out, in_, ident):
            return lambda e: e.transpose(out=out, in_=in_, identity=ident)

        def actf(out, in_, func, bias=None, scale=None, accum_out=None):
            kw = {}
            if bias is not None:
                kw["bias"] = bias
            if scale is not None:
                kw["scale"] = scale
            if accum_out is not None:
                kw["accum_out"] = accum_out
            return lambda e: e.activation(out=out, in_=in_, func=func, **kw)

        def tt(out, in0, in1, op):
            return lambda e: e.tensor_tensor(out=out, in0=in0, in1=in1, op=op)

        def ts(out, in0, s1, s2, op0, op1=None):
            if op1 is None:
                return lambda e: e.tensor_scalar(out=out, in0=in0, scalar1=s1, scalar2=None, op0=op0)
            return lambda e: e.tensor_scalar(out=out, in0=in0, scalar1=s1, scalar2=s2, op0=op0, op1=op1)

        def stt(out, in0, scalar, in1, op0, op1):
            return lambda e: e.scalar_tensor_tensor(out=out, in0=in0, scalar=scalar, in1=in1, op0=op0, op1=op1)

        def cp(out, in_):
            return lambda e: (e.tensor_copy(out=out, in_=in_) if hasattr(e, 'tensor_copy') else e.activation(out=out, in_=in_, func=AF.Copy))

        identf = alloc([128, 128], F32)
        identb = alloc([128, 128], BF16)
        masks = alloc([128, 8, 128], F32)
        onesf = alloc([128, 128], F32)
        segm = alloc([128, 512], F32)
        M_le, M_lt, M_ge, M_gt, BD_le, BD_lt, BD_ge, BD_gt = [masks[:, i, :] for i in range(8)]
        gate_row = alloc([128, 1024], F32)
        GB = alloc([128, NT, 16], F32)
        modc = alloc([128, 16, 2], F32)
        lbc = alloc([128, 8], F32)
        omlb = alloc([128, 8], F32)
        nomlb = alloc([128, 8], F32)
        gains = alloc([128, 2], F32)
        cw = alloc([128, 12, 5], F32)
        rowc = alloc([128, 16], F32)
        persist_off = st["off"]

        dma(identf, cidf, [], ["identf"])
        dma(masks, cmask, [], ["masks"])
        dma(segm, csegm, [], ["segm"])
        dma(cw, convw, [], ["cw"])
        dma(gains[:, 0:1], ang, [], ["gains"])
        dma(gains[:, 1:2], bng, [], ["gains"])
        dma(rowc[:, 0:8], alog.partition_broadcast(128), [], ["rowc"])
        dma(rowc[:, 8:16], dtb.partition_broadcast(128), [], ["rowc"])
        dve(cp(identb, identf), ["identf"], ["identb"])
        pool(lambda e: e.memset(onesf, 1.0), [], ["onesf"])
        act(actf(rowc[:, 0:8], rowc[:, 0:8], AF.Exp), ["rowc"], ["rowc"])
        dve(ts(rowc[:, 0:8], rowc[:, 0:8], -1.0, None, ALU.mult), ["rowc"], ["rowc"])

        ph0 = st["off"]
        cct = alloc([128, 8, 2], F32)
        sil = alloc([128, 8, 2], F32)
        srep = alloc([128, 8, 128], F32)
        bmc = alloc([128, 16], F32)
        lbt = alloc([128, 2, 8], F32)
        wmr = Ring("wm", 2, [128, 8, 512], F32)
        dma(cct, cc, [], ["cct"])
        dma(bmc, bmodc, [], ["bmc"])
        dma(lbt, lbp, [], ["lbt"])
        dma(gate_row, bmodg.partition_broadcast(128), [], ["gate_row"])
        act(actf(sil, cct, AF.Silu), ["cct"], ["sil"])
        dve(cp(srep, sil[:, :, 0:1].to_broadcast([128, 8, 128])), ["sil"], ["srep"])
        dve(tt(lbc, lbt[:, 0, :], lbt[:, 1, :], ALU.subtract), ["lbt"], ["lbc"])
        act(actf(lbc, lbc, AF.Sigmoid), ["lbc"], ["lbc"])
        dve(ts(omlb, lbc, -1.0, 1.0, ALU.mult, ALU.add), ["lbc"], ["omlb"])
        dve(ts(nomlb, lbc, -1.0, None, ALU.add), ["lbc"], ["nomlb"])
        for blk in range(6):
            wt, wk = wmr.next()
            dma(wt, wmod[blk], [], [wk])
            if blk < 4:
                for jj in range(4):
                    j = blk * 4 + jj
                    pt, pk = ps_alloc(1)
                    pe(seq(*[mm(pt[:, 0:2], wt[:, kc, jj * 128:(jj + 1) * 128], sil[:, kc, :],
                                start=(kc == 0), stop=(kc == 7)) for kc in range(8)]),
                       [wk, "sil"], pk)
                    dve(ts(modc[:, j, :], pt[:, 0:2], bmc[:, j:j + 1], 1.0 if j >= 8 else 0.0, ALU.add, ALU.add),
                        pk + ["bmc"], ["modc"])
            else:
                pt, pk = ps_alloc(4)
                pe(seq(*[mm(pt, srep[:, kc, :], wt[:, kc, :], start=(kc == 0), stop=(kc == 7))
                         for kc in range(8)]), [wk, "srep"], pk)
                gs = gate_row[:, (blk - 4) * 512:(blk - 3) * 512]
                dve(tt(gs, gs, pt, ALU.add), pk + ["gate_row"], ["gate_row"])
        P.barrier()
        st["off"] = ph0

        uT = alloc([128, 8, NTOK], BF16)
        ph2 = st["off"]
        xr = Ring("xt", 3, [128, 1024], F32)
        xnr = Ring("xn", 3, [128, 1024], BF16)
        str_ = Ring("st", 3, [128, 2, 6], F32)
        mvr = Ring("mv", 4, [128, 4], F32)

        def p1_A(T):
            src = ctx[T * 128:(T + 1) * 128, :] if T < 2 else x[(T - 2) * 128:(T - 1) * 128, :]
            xt, xk = xr.next()
            stt_, stk = str_.next()
            mv, mvk = mvr.next()
            dma(xt, src, [], [xk])
            dve(seq(lambda e, a=stt_, b=xt: e.bn_stats(out=a[:, 0, :], in_=b[:, 0:512]),
                    lambda e, a=stt_, b=xt: e.bn_stats(out=a[:, 1, :], in_=b[:, 512:1024])), [xk], [stk])
            dve(lambda e, a=mv, b=stt_: e.bn_aggr(out=a[:, 0:2], in_=b.rearrange("p a b -> p (a b)")), [stk], [mvk])
            act(actf(mv[:, 2:3], mv[:, 1:2], AF.Sqrt, bias=EPS), [mvk], [mvk + "s"])
            return dict(T=T, xt=xt, xk=xk, mv=mv, mvk=mvk)

        def p1_B(c_):
            mv, mvk, xt, xk = c_["mv"], c_["mvk"], c_["xt"], c_["xk"]
            xn, xnk = xnr.next()
            dve(lambda e, a=mv: e.reciprocal(out=a[:, 2:3], in_=a[:, 2:3]), [mvk + "s"], [mvk + "s"])
            dve(stt(mv[:, 3:4], mv[:, 0:1], -1.0, mv[:, 2:3], ALU.mult, ALU.mult), [mvk, mvk + "s"], [mvk + "n"])
            act(actf(xn, xt, AF.Identity, bias=mv[:, 3:4], scale=mv[:, 2:3]), [xk, mvk + "s", mvk + "n"], [xnk])
            pt, pk = ps_alloc(4)
            ptb = pt.bitcast(BF16)
            pe(seq(*[tr(ptb[:, kc * 128:(kc + 1) * 128], xn[:, kc * 128:(kc + 1) * 128], identb) for kc in range(8)]),
               [xnk, "identb"], pk)
            c_["ptb"], c_["pk"] = ptb, pk

        def p1_C(c_):
            T, ptb, pk = c_["T"], c_["ptb"], c_["pk"]
            w = 1 if T < 2 else 0
            for kc in range(8):
                o = uT[:, kc, T * 128:(T + 1) * 128]
                i = ptb[:, kc * 128:(kc + 1) * 128]
                if kc % 2 == 0:
                    act(actf(o, i, AF.Identity, bias=modc[:, kc, w:w + 1], scale=modc[:, 8 + kc, w:w + 1]),
                        pk + ["modc"], [f"uT{T}"])
                else:
                    dve(ts(o, i, modc[:, 8 + kc, w:w + 1], modc[:, kc, w:w + 1], ALU.mult, ALU.add),
                        pk + ["modc"], [f"uT{T}"])

        p1q = []
        for T in range(NT + 2):
            if T < NT:
                p1q.append(p1_A(T))
            if 1 <= T <= NT:
                p1_B(p1q[T - 1])
            if T >= 2:
                p1_C(p1q[T - 2])
        uT_all = [f"uT{T}" for T in range(NT)]

        wfr = Ring("wf", 2, [128, 8, 128], F32)
        wbr = Ring("wb", 4, [128, 8, 128], BF16)
        ZL = 4360
        bufA = alloc([128, ZL], F32)
        bufB = alloc([128, ZL], F32)
        bufC = alloc([128, ZL], F32)
        stg = Ring("stg", 3, [128, NTOK], BF16)
        wabf = alloc([128, 8, 16], F32)
        wabb = alloc([128, 8, 16], BF16)
        tmpab = alloc([128, NT, 8], F32)
        tmpab2 = alloc([128, NT, 8], F32)
        ssr = Ring("ss", 2, [128, 512], F32)

        dma(wabf, wab, [], ["wabf"])
        pool(cp(wabb, wabf), ["wabf"], ["wabb"])
        for T in range(NT):
            pt, pk = ps_alloc(1)
            pe(seq(*[mm(pt[:, 0:16], uT[:, kc, T * 128:(T + 1) * 128], wabb[:, kc, :], start=(kc == 0), stop=(kc == 7))
                     for kc in range(8)]), [f"uT{T}", "wabb"], pk)
            dve(cp(GB[:, T, :], pt[:, 0:16]), pk, ["GBraw"])
        a_ = tmpab
        b_ = tmpab2
        dve(tt(a_, GB[:, :, 0:8], rowc[:, 8:16].unsqueeze(1).to_broadcast([128, NT, 8]), ALU.add), ["GBraw", "rowc"], ["tmpab"])
        dve(stt(b_, a_, -1.0, a_, ALU.mult, ALU.max), ["tmpab"], ["tmpab2"])
        act(actf(b_, b_, AF.Exp, scale=-1.0), ["tmpab2"], ["tmpab2"])
        act(actf(b_, b_, AF.Ln, bias=1.0), ["tmpab2"], ["tmpab2"])
        dve(stt(a_, a_, 0.0, b_, ALU.max, ALU.add), ["tmpab", "tmpab2"], ["tmpab"])
        dve(tt(GB[:, :, 0:8], a_, rowc[:, 0:8].unsqueeze(1).to_broadcast([128, NT, 8]), ALU.mult), ["tmpab", "rowc", "GBraw"], ["GBg"])
        act(actf(GB[:, :, 8:16], GB[:, :, 8:16], AF.Sigmoid), ["GBraw"], ["GBb"])
        GBk = ["GBg", "GBb"]

        blocks = [(0, 256)] + [(256 + i * 512, 512) for i in range(8)]

        def cm_view(buf, r0, nr=8):
            v = buf[:, 256:256 + 4096].rearrange("p (w r) -> p w r", r=64)[:, :, r0:r0 + nr]
            return v.rearrange("p w r -> p r w")

        def ps_rw(pt):
            return pt.rearrange("p (r w) -> p r w", w=64)

        wpre = {}

        def prefetch_w(j):
            if j in wpre:
                return
            wf, wfk = wfr.next()
            wb, wbk = wbr.next()
            dma(wf, win[j], [], [wfk])
            pool(cp(wb, wf), [wfk], [wbk])
            wpre[j] = (wb, wbk)

        def project(j, lat_only, epi):
            prefetch_w(j)
            wb, wbk = wpre[j]
            for bi, (c0, n) in enumerate(blocks):
                if lat_only and bi == 0:
                    continue
                pt, pk = ps_alloc(4)
                T0 = c0 // 128
                pe(seq(*[mm(pt[:, 0:n], wb[:, kc, :], uT[:, kc, c0:c0 + n], start=(kc == 0), stop=(kc == 7))
                         for kc in range(8)]), [wbk] + [f"uT{T0 + i}" for i in range(n // 128)], pk)
                epi(bi, c0, n, pt, pk)

        def zpos(c0):
            return c0 + 2 if c0 < 256 else c0 + 6

        pool(lambda e: e.memset(bufA, 0.0), [], ["bufA"])

        def conv_chunk(j, kind, h):
            def epi(bi, c0, n, pt, pk):
                z0 = zpos(c0)
                act(cp(bufA[:, z0:z0 + n], pt[:, 0:n]) if False else actf(bufA[:, z0:z0 + n], pt[:, 0:n], AF.Identity), pk, ["bufA"])
            project(j, False, epi)
            L = 4356
            dve(ts(bufB[:, 0:L], bufA[:, 0:L], cw[:, j, 0:1], None, ALU.mult), ["bufA", "cw"], ["bufB"])
            for k in range(1, 5):
                dve(stt(bufB[:, 0:L], bufA[:, k:k + L], cw[:, j, k:k + 1], bufB[:, 0:L], ALU.mult, ALU.add),
                    ["bufA", "bufB", "cw"], ["bufB"])
            return lambda: conv_part2(j, kind, h)

        def conv_part2(j, kind, h):
            L = 4356
            sg, sgk = stg.next()
            if kind == 'v':
                act(actf(sg[:, 0:256], bufB[:, 0:256], AF.Silu), ["bufB"], [sgk])
                act(actf(sg[:, 256:NTOK], bufB[:, 260:4356], AF.Silu), ["bufB"], [sgk])
                dstore(ZV[h], sg, [sgk], ["ZV"])
                return
            act(actf(bufC[:, 0:L], bufB[:, 0:L], AF.Silu), ["bufB"], ["bufC"])
            pool(tt(bufB[:, 0:L], bufC[:, 0:L], bufC[:, 0:L], ALU.mult), ["bufC", "bufB"], ["bufB"])
            for (c0, n) in blocks:
                a0 = c0 if c0 < 256 else c0 + 4
                pt, pk = ps_alloc(4)
                pe(mm(pt[:, 0:n], onesf, bufB[:, a0:a0 + n]), ["onesf", "bufB"], pk)
                ss, ssk = ssr.next()
                act(actf(ss[:, 0:n], pt[:, 0:n], AF.Sqrt, bias=EPS), pk, [ssk])
                dve(lambda e, a=ss, n=n: e.reciprocal(out=a[:, 0:n], in_=a[:, 0:n]), [ssk], [ssk])
                dve(tt(sg[:, c0:c0 + n], bufC[:, a0:a0 + n], ss[:, 0:n], ALU.mult), [ssk, "bufC"], [sgk])
            dstore((ZQ if kind == 'q' else ZK)[h], sg, [sgk], ["ZQ" if kind == 'q' else "ZK"])

        def simple_chunk(j, func, dst, dkey, lat_only, cmo):
            sg, sgk = stg.next()

            def epi(bi, c0, n, pt, pk):
                if lat_only:
                    o = sg[:, c0 - 256:c0 - 256 + n]
                    i = pt[:, 0:n]
                elif bi == 0 or not cmo:
                    o = sg[:, c0:c0 + n]
                    i = pt[:, 0:n]
                else:
                    o = cm_view(sg, (c0 - 256) // 64)
                    i = ps_rw(pt)
                act(actf(o, i, func), pk, [sgk])
            project(j, lat_only, epi)
            if lat_only:
                dstore(dst, sg[:, 0:4096], [sgk], [dkey])
            else:
                dstore(dst, sg, [sgk], [dkey])

        def f_chunk(j, d, h):
            def epi(bi, c0, n, pt, pk):
                if bi == 0:
                    o = bufA[:, c0:c0 + n]
                    i = pt[:, 0:n]
                else:
                    o = cm_view(bufA, (c0 - 256) // 64)
                    i = ps_rw(pt)
                act(actf(o, i, AF.Sigmoid), pk, ["bufA"])
            project(j, False, epi)
            return lambda: f_part2(j, d, h)

        def f_part2(j, d, h):
            c = d * 4 + h
            dve(ts(bufB[:, 0:NTOK], bufA[:, 0:NTOK], omlb[:, c:c + 1], lbc[:, c:c + 1], ALU.mult, ALU.add),
                ["bufA", "omlb", "lbc"], ["bufB"])
            act(actf(bufC[:, 0:NTOK], bufB[:, 0:NTOK], AF.Ln), ["bufB"], ["bufC"])
            dstore(ZBL[c], bufC[:, 0:NTOK], ["bufC"], ["ZBL"])
            sg, sgk = stg.next()
            pool(ts(sg, bufA[:, 0:NTOK], nomlb[:, c:c + 1], omlb[:, c:c + 1], ALU.mult, ALU.add),
                 ["bufA", "nomlb", "omlb"], [sgk])
            dstore(ZBK[c], sg, [sgk], ["ZBK"])

        def do_chunk(j):
            if j < 4:
                return conv_chunk(j, 'q', j)
            elif j < 8:
                return conv_chunk(j, 'k', j - 4)
            elif j < 12:
                return conv_chunk(j, 'v', j - 8)
            elif j < 16:
                return simple_chunk(j, AF.Silu, ZAG[j - 12], "ZAG", True, False)
            elif j < 20:
                return simple_chunk(j, AF.Silu, ZBQ[j - 16], "ZBQ", False, True)
            elif j < 24:
                return f_chunk(j, 0, j - 20)
            elif j < 28:
                return f_chunk(j, 1, j - 24)
            elif j < 32:
                return simple_chunk(j, AF.Identity, ZBI[j - 28], "ZBI", False, True)
            elif j < 36:
                return simple_chunk(j, AF.Silu, ZBGT[j - 32], "ZBGT", True, False)
            else:
                return simple_chunk(j, AF.Sigmoid, ZM[j - 36], "ZM", True, False)

        if dbg and "chunks" in dbg:
            for j in dbg["chunks"]:
                p2_ = do_chunk(j)
                if p2_:
                    p2_()
        else:
            heavy = list(range(0, 12)) + list(range(20, 28))
            simple = list(range(12, 20)) + list(range(28, 52))
            order = []
            si = 0
            for j in heavy:
                order.append(("a", j))
                for _ in range(2 if j < 12 else 1):
                    if si < len(simple):
                        order.append(("s", simple[si]))
                        si += 1
                order.append(("b", j))
            while si < len(simple):
                order.append(("s", simple[si]))
                si += 1
            projs = [j for (k, j) in order if k != "b"]
            pending2 = {}
            pi = 0
            prefetch_w(projs[0])
            prefetch_w(projs[1])
            for (k, j) in order:
                if k == "b":
                    pending2.pop(j)()
                    continue
                if pi + 2 < len(projs):
                    prefetch_w(projs[pi + 2])
                pi += 1
                r_ = do_chunk(j)
                if k == "a":
                    pending2[j] = r_
        if dbg and dbg.get("dump_gb"):
            dma(DBG[:, 0:NT * 16], GB.rearrange("p a b -> p (a b)"), GBk, ["DBG"])
            dma(DBG[:, 1024:1024 + 32], modc.rearrange("p a b -> p (a b)"), ["modc"], ["DBG"])
            dma(DBG[:, 2048:3072], gate_row, ["gate_row"], ["DBG"])
        P.barrier()
        st["off"] = persist_off


        OTa = alloc([128, 4, 4096], BF16)
        OTb = alloc([128, 4, 4096], BF16)
        Sst = alloc([128, 8, 128], F32)
        Sb = alloc([128, 8, 128], BF16)
        ph3 = st["off"]
        r_q = Ring("gq", 4, [128, 4, 128], BF16)
        r_k = Ring("gk", 4, [128, 4, 128], BF16)
        r_v = Ring("gv", 4, [128, 4, 128], BF16)
        r_ex = Ring("gex", 6, [128, 16], F32)
        r_nb = Ring("gnb", 6, [128, 4], F32)
        r_gm = Ring("ggm", 4, [128, 8], F32)
        r_f = {n: Ring("g" + n, k, [128, 4, 128], F32) for n, k in
               (("M1", 2), ("E", 2), ("Es", 2), ("B", 2), ("BT", 2), ("X", 4), ("P", 4), ("PT", 4),
                ("Ub", 4), ("tmp", 2), ("ot", 2))}
        r_f["Ei"] = r_f["M1"]
        r_b = {n: Ring("g" + n, k, [128, 4, 128], BF16) for n, k in
               (("at", 4), ("Xb", 2), ("Kg", 2), ("Kd", 4), ("vt", 2), ("WT", 4), ("vn", 2))}
        r_s = Ring("gss", 8, [128, 4], F32)
        r_s4 = Ring("gs4", 4, [128, 8], F32)
        r_b["on4"] = Ring("gon4", 2, [128, 4, 128], BF16)
        r_ofo = Ring("ofo", 2, [128, 512], F32)
        r_ofi = Ring("ofi", 4, [128, 512], F32)

        def finalize4(ot, otk, OTkey, dest, scale, hgj=None):
            ss, ssk = r_s4.next()
            jk, jkk = r_f["tmp"].next()
            for h in range(4):
                act(actf(jk[:, h, :], ot[:, h, :], AF.Square, accum_out=ss[:, h:h + 1]), [otk], [jkk, ssk])
            dve(ts(ss[:, 4:8], ss[:, 0:4], 1.0 / 128.0, EPS / (scale * scale), ALU.mult, ALU.add), [ssk], [ssk + "b"])
            act(actf(ss[:, 4:8], ss[:, 4:8], AF.Sqrt), [ssk + "b"], [ssk + "b"])
            dve(lambda e, a=ss: e.reciprocal(out=a[:, 4:8], in_=a[:, 4:8]), [ssk + "b"], [ssk + "b"])
            on, onk = r_b["on4"].next()
            dve(tt(on, ot, bc4(ss[:, 4:8]), ALU.mult), [otk, ssk + "b"], [onk])
            pt, pk = ps_alloc(4)
            ptb = pt.bitcast(BF16)
            pe(seq(*[tr(ptb[:, h * 128:(h + 1) * 128], on[:, h, :], identb) for h in range(4)]), [onk, "identb"], pk)
            if hgj is None:
                act(cp(dest, v4(ptb[:, 0:512])), pk, [OTkey])
            else:
                srcv = ptb[:, 0:512].rearrange("p (h a b) -> p h a b", h=4, a=2)
                dstv = OTb.rearrange("p h (r w) -> p h r w", w=64)
                for w2 in range(2):
                    act(cp(dstv[:, :, :, 2 * hgj + w2], srcv[:, :, w2, :]), pk, [OTkey])

        def finalize(ot, otk, OTkey, colap, scale, tview=None):
            ss, ssk = r_s.next()
            jk, jkk = r_f["o1"].next()
            act(actf(jk, ot, AF.Square, accum_out=ss[:, 0:1]), [otk], [jkk, ssk])
            dve(ts(ss[:, 1:2], ss[:, 0:1], scale * scale / 128.0, EPS, ALU.mult, ALU.add), [ssk], [ssk + "b"])
            act(actf(ss[:, 1:2], ss[:, 1:2], AF.Sqrt), [ssk + "b"], [ssk + "b"])
            dve(lambda e, a=ss: e.reciprocal(out=a[:, 1:2], in_=a[:, 1:2]), [ssk + "b"], [ssk + "b"])
            on, onk = r_b["on"].next()
            dve(ts(on, ot, ss[:, 1:2], scale, ALU.mult, ALU.mult), [otk, ssk + "b"], [onk])
            pt, pk = ps_alloc(1)
            pe(tr(bfv(pt), on, identb), [onk, "identb"], pk)
            if tview is None:
                act(cp(colap, bfv(pt)), pk, [OTkey])
            else:
                for w2 in range(2):
                    act(cp(colap[:, :, w2], bfv(pt)[:, w2 * 64:(w2 + 1) * 64]), pk, [OTkey])

        pool(lambda e: e.memset(Sst, 0.0), [], ["S0", "S1"])
        pool(lambda e: e.memset(Sb, 0.0), [], ["Sb0", "Sb1"])

        def bc4(col4):
            return col4.unsqueeze(2).to_broadcast([128, 4, 128])

        def mb4(m):
            return m.unsqueeze(1).to_broadcast([128, 4, 128])

        def v4(ap):
            return ap.rearrange("p (h v) -> p h v", h=4)

        def gdn_prep(pairs, second, prs):
            for (T, d) in pairs:
                islat = T >= 2
                kT, kk = r_k.next()
                vT, vk = r_v.next()
                tsl = slice(T * 128, (T + 1) * 128)
                dma(kT, ZK[:, :, tsl].rearrange("h p t -> p h t"), ["ZK"], [kk])
                dma(vT, ZV[:, :, tsl].rearrange("h p t -> p h t"), ["ZV"], [vk])
                qT = qk = None
                if islat:
                    qT, qk = r_q.next()
                    dma(qT, ZQ[:, :, tsl].rearrange("h p t -> p h t"), ["ZQ"], [qk])
                ofi = ofik = None
                ofdefer = False
                if islat and second:
                    ofi, ofik = r_ofi.next()
                    ofdefer = f"OF{T - 2}" not in P.lastw
                    if not ofdefer:
                        dma(ofi, OFa[T - 2], [f"OF{T - 2}"], [ofik])
                ML, MR, MS = (BD_le, BD_gt, BD_lt) if d == 0 else (BD_ge, BD_lt, BD_gt)
                g4 = GB[:, T, d * 4:(d + 1) * 4]
                b4 = GB[:, T, 8 + d * 4:8 + (d + 1) * 4]
                pc, pck = ps_alloc(1)
                gm, gmk = r_gm.next()
                pool(ts(gm[:, 0:4], g4, BD_le[:, 63:64], None, ALU.mult), GBk + ["masks"], [gmk])
                pool(ts(gm[:, 4:8], g4, BD_ge[:, 64:65], None, ALU.mult), GBk + ["masks"], [gmk])
                pe(seq(mm(pc[:, 0:4], ML, g4), mm(pc[:, 4:8], MR, g4),
                       mm(pc[:, 8:12], onesf, gm[:, 0:4]), mm(pc[:, 12:16], onesf, gm[:, 4:8])),
                   ["masks", "onesf", gmk] + GBk, pck)
                ex, exk = r_ex.next()
                act(actf(ex, pc[:, 0:16], AF.Exp), pck, [exk])
                nb, nbk = r_nb.next()
                pool(ts(nb, b4, -1.0, None, ALU.mult), GBk, [nbk])
                prs.append(dict(T=T, d=d, islat=islat, qT=qT, qk=qk, kT=kT, kk=kk, vT=vT, vk=vk, ex=ex, exk=exk,
                                nb=nb, nbk=nbk, ML=ML, MR=MR, MS=MS, ofi=ofi, ofik=ofik, ofdefer=ofdefer,
                                g4=g4, b4=b4))
            gstage = dbg.get('gdn_stage', 99) if dbg else 99
            yield
            if gstage < 1:
                return
            for pr in prs:
                kT, kk = pr["kT"], pr["kk"]
                pKK, pKKk = ps_alloc(4)
                pe(seq(*[mm(pKK[:, h * 128:(h + 1) * 128], kT[:, h, :], kT[:, h, :]) for h in range(4)]), [kk], pKKk)
                M1, M1k = r_f["M1"].next()
                pool(tt(M1, mb4(pr["MR"]), bc4(pr["g4"]), ALU.mult), ["masks"] + GBk, [M1k])
                pD, pDk = ps_alloc(4)
                pe(seq(*[mm(pD[:, h * 128:(h + 1) * 128], M1[:, h, :], pr["ML"]) for h in range(4)]), [M1k, "masks"], pDk)
                E, Ek = r_f["E"].next()
                act(actf(E, v4(pD), AF.Exp), pDk, [Ek])
                Es, Esk = r_f["Es"].next()
                pool(tt(Es, E, mb4(pr["MS"]), ALU.mult), [Ek, "masks"], [Esk])
                pool(tt(Es, Es, bc4(pr["b4"]), ALU.mult), [Esk] + GBk, [Esk])
                B, Bk = r_f["B"].next()
                dve(tt(B, v4(pKK), Es, ALU.mult), pKKk + [Esk], [Bk])
                pr["B"], pr["Bk"] = B, Bk
                pr["at"] = pr["atk"] = None
                if pr["islat"]:
                    Ei, Eik = r_f["Ei"].next()
                    pool(tt(Ei, E, mb4(pr["ML"]), ALU.mult), [Ek, "masks"], [Eik])
                    pQK, pQKk = ps_alloc(4)
                    pe(seq(*[mm(pQK[:, h * 128:(h + 1) * 128], kT[:, h, :], pr["qT"][:, h, :]) for h in range(4)]),
                       [kk, pr["qk"]], pQKk)
                    at, atk = r_b["at"].next()
                    dve(tt(at, v4(pQK), Ei, ALU.mult), pQKk + [Eik], [atk])
                    pr["at"], pr["atk"] = at, atk
            yield
            if gstage < 2:
                return
            for pr in prs:
                B, Bk = pr["B"], pr["Bk"]
                pBT, pBTk = ps_alloc(4)
                pe(seq(*[mm(pBT[:, h * 128:(h + 1) * 128], B[:, h, :], identf) for h in range(4)]), [Bk, "identf"], pBTk)
                BT, BTk = r_f["BT"].next()
                act(cp(BT, v4(pBT)), pBTk, [BTk])
                X, Xk = r_f["X"].next()
                pool(tt(X, mb4(identf), B, ALU.subtract), [Bk, "identf"], [Xk])
                pr.update(P=B, Pk=Bk, PT=BT, PTk=BTk, X=X, Xk=Xk)
            yield
            if gstage < 3:
                return
            NL = dbg.get('gdn_levels', 5) if dbg else 5

            def sq(pr, only_t):
                Pm, Pk, PTm, PTk = pr["P"], pr["Pk"], pr["PT"], pr["PTk"]
                pr["p2t"], pr["p2tk"] = ps_alloc(4)
                pe(seq(*[mm(pr["p2t"][:, h * 128:(h + 1) * 128], Pm[:, h, :], PTm[:, h, :]) for h in range(4)]),
                   [Pk, PTk], pr["p2tk"])
                pr["p2"] = None
                if not only_t:
                    pr["p2"], pr["p2k"] = ps_alloc(4)
                    pe(seq(*[mm(pr["p2"][:, h * 128:(h + 1) * 128], PTm[:, h, :], Pm[:, h, :]) for h in range(4)]),
                       [Pk, PTk], pr["p2k"])

            def cps(pr):
                nPT, nPTk = r_f["PT"].next()
                act(cp(nPT, v4(pr["p2t"])), pr["p2tk"], [nPTk])
                pr["PT"], pr["PTk"] = nPT, nPTk
                if pr["p2"] is not None:
                    nP, nPk = r_f["P"].next()
                    dve(cp(nP, v4(pr["p2"])), pr["p2k"], [nPk])
                    pr["P"], pr["Pk"] = nP, nPk

            for pr in prs:
                sq(pr, NL == 1)
            yield
            for pr in prs:
                cps(pr)
            yield
            for lvl in range(NL):
                last = lvl == NL - 1
                for pr in prs:
                    px, pxk = ps_alloc(4)
                    pe(seq(*[mm(px[:, h * 128:(h + 1) * 128], pr["PT"][:, h, :], pr["X"][:, h, :]) for h in range(4)]),
                       [pr["PTk"], pr["Xk"]], pxk)
                    if not last:
                        sq(pr, lvl + 1 == NL - 1)
                        nX, nXk = r_f["X"].next()
                        dve(tt(nX, v4(px), pr["X"], ALU.add), pxk + [pr["Xk"]], [nXk])
                        pr["X"], pr["Xk"] = nX, nXk
                        cps(pr)
                    else:
                        Xb, Xbk = r_b["Xb"].next()
                        dve(tt(Xb, v4(px), pr["X"], ALU.add), pxk + [pr["Xk"]], [Xbk])
                        pr["Xb"], pr["Xbk"] = Xb, Xbk
                    yield
            if gstage < 4:
                return
            for pr in prs:
                ex, exk = pr["ex"], pr["exk"]
                pkt, pktk = ps_alloc(4)
                pktb = pkt.bitcast(BF16)
                pe(seq(*[tr(pktb[:, h * 128:(h + 1) * 128], pr["kT"][:, h, :], identb) for h in range(4)]),
                   [pr["kk"], "identb"], pktk)
                pvt, pvtk = ps_alloc(4)
                pvtb = pvt.bitcast(BF16)
                pe(seq(*[tr(pvtb[:, h * 128:(h + 1) * 128], pr["vT"][:, h, :], identb) for h in range(4)]),
                   [pr["vk"], "identb"], pvtk)
                Kg, Kgk = r_b["Kg"].next()
                dve(tt(Kg, v4(pktb[:, 0:512]), bc4(ex[:, 0:4]), ALU.mult), pktk + [exk], [Kgk])
                Kd, Kdk = r_b["Kd"].next()
                dve(tt(Kd, v4(pktb[:, 0:512]), bc4(ex[:, 4:8]), ALU.mult), pktk + [exk], [Kdk])
                vt, vtk = r_b["vt"].next()
                act(cp(vt, v4(pvtb[:, 0:512])), pvtk, [vtk])
                pU, pUk = ps_alloc(4)
                pe(seq(*[mm(pU[:, h * 128:(h + 1) * 128], pr["Xb"][:, h, :], vt[:, h, :]) for h in range(4)]),
                   [pr["Xbk"], vtk], pUk)
                Ub, Ubk = r_f["Ub"].next()
                dve(tt(Ub, v4(pU), bc4(pr["b4"]), ALU.mult), pUk + GBk, [Ubk])
                pW, pWk = ps_alloc(4)
                pe(seq(*[mm(pW[:, h * 128:(h + 1) * 128], Kg[:, h, :], pr["Xb"][:, h, :]) for h in range(4)]),
                   [Kgk, pr["Xbk"]], pWk)
                WT, WTk = r_b["WT"].next()
                act(cp(WT, v4(pW)), pWk, [WTk])
                pr.update(WT=WT, WTk=WTk, Ub=Ub, Ubk=Ubk, Kd=Kd, Kdk=Kdk)
            yield

        def gdn_chain(preps, second):
            for pr in preps:
                if pr["ofdefer"]:
                    dma(pr["ofi"], OFa[pr["T"] - 2], [f"OF{pr['T'] - 2}"], [pr["ofik"]])
            for pr in preps:
                pr["vn"], pr["vnk"] = r_b["vn"].next()
                if pr["islat"]:
                    pr["pO1"], pr["pO1k"] = PS[6 + pr["d"]][:, :], [f"psb{6 + pr['d']}"]
            for sub in range(2):
                def rs(pr):
                    j = sub if pr["d"] == 0 else 1 - sub
                    return j, slice(64 * j, 64 * j + 64)
                for pr in preps:
                    d = pr["d"]
                    j, r = rs(pr)
                    pP, pPk = ps_alloc(4)
                    pe(seq(*[mm(pP[r, h * 128:(h + 1) * 128], pr["WT"][:, h, r], Sb[:, d * 4 + h, :]) for h in range(4)]),
                       [pr["WTk"], f"Sb{d}"], pPk)
                    tmp, tmpk = r_f["tmp"].next()
                    dve(tt(tmp[r, :, :], v4(pP)[r, :, :], pr["nb"][r, :].unsqueeze(2).to_broadcast([64, 4, 128]), ALU.mult),
                        pPk + [pr["nbk"]], [tmpk])
                    pool(tt(pr["vn"][r, :, :], tmp[r, :, :], pr["Ub"][r, :, :], ALU.add), [tmpk, pr["Ubk"]], [pr["vnk"]])
                yield
                for pr in preps:
                    d = pr["d"]
                    j, r = rs(pr)
                    if pr["islat"]:
                        pe(seq(*[mm(pr["pO1"][r, h * 128:(h + 1) * 128], pr["qT"][:, h, r], Sb[:, d * 4 + h, :]) for h in range(4)]),
                           [pr["qk"], f"Sb{d}"], pr["pO1k"])
                    pS, pSk = ps_alloc(4)
                    pe(seq(*[mm(pS[:, h * 128:(h + 1) * 128], pr["Kd"][r, h, :], pr["vn"][r, h, :]) for h in range(4)]),
                       [pr["Kdk"], pr["vnk"]], pSk)
                    pr["pS"], pr["pSk"] = pS, pSk
                    S4 = Sst[:, d * 4:(d + 1) * 4, :]
                    pool(tt(S4, S4, bc4(pr["ex"][:, 8 + 4 * j:12 + 4 * j]), ALU.mult), [pr["exk"], f"S{d}"], [f"S{d}"])
                yield
                for pr in preps:
                    d = pr["d"]
                    S4 = Sst[:, d * 4:(d + 1) * 4, :]
                    dve(tt(S4, S4, v4(pr["pS"]), ALU.add), pr["pSk"] + [f"S{d}"], [f"S{d}"])
                    act(cp(Sb[:, d * 4:(d + 1) * 4, :], S4), [f"S{d}"], [f"Sb{d}"])
                yield
            for pr in preps:
                if not pr["islat"]:
                    continue
                lt = pr["T"] - 2
                pO2, pO2k = ps_alloc(4)
                pe(seq(*[mm(pO2[:, h * 128:(h + 1) * 128], pr["at"][:, h, :], pr["vn"][:, h, :]) for h in range(4)]),
                   [pr["atk"], pr["vnk"]], pO2k)
                tmp, tmpk = r_f["tmp"].next()
                dve(tt(tmp, v4(pr["pO1"]), bc4(pr["ex"][:, 0:4]), ALU.mult), pr["pO1k"] + [pr["exk"]], [tmpk])
                if not second:
                    ofo, ofok = r_ofo.next()
                    dve(tt(ofo, pO2, tmp.rearrange("p h v -> p (h v)"), ALU.add), pO2k + [tmpk], [ofok])
                    dstore(OFa[lt], ofo, [ofok], [f"OF{lt}"])
                else:
                    ot, otk = r_f["ot"].next()
                    dve(tt(ot, v4(pO2), tmp, ALU.add), pO2k + [tmpk], [otk])
                    pool(tt(ot, ot, v4(pr["ofi"]), ALU.add), [otk, pr["ofik"]], [otk])
                    finalize4(ot, otk, "OTa", OTa[:, :, lt * 128:(lt + 1) * 128], QSCALE)

            yield

        bwd_order = [1, 0] + list(range(33, 1, -1))
        nsteps = dbg.get("gdn_steps", NT) if dbg else NT
        def run_gen(g):
            for _ in g:
                pass

        def interleave(ga, gb, ratio=3):
            alive_a, alive_b = ga is not None, gb is not None
            while alive_a or alive_b:
                for _ in range(ratio):
                    if alive_a:
                        psst["grp"] = 'p'
                        try:
                            next(ga)
                        except StopIteration:
                            alive_a = False
                if alive_b:
                    psst["grp"] = 'c'
                    try:
                        next(gb)
                    except StopIteration:
                        alive_b = False
            psst["grp"] = None

        if not (dbg and dbg.get("skip_gdn")):
            cur = []
            psst["grp"] = 'p'
            run_gen(gdn_prep([(0, 0), (bwd_order[0], 1)], False, cur))
            psst["grp"] = None
            for i in range(nsteps):
                nxt = []
                gp = gdn_prep([(i + 1, 0), (bwd_order[i + 1], 1)], (i + 1) >= 18, nxt) if i + 1 < nsteps else None
                gc = gdn_chain(cur, second=(i >= 18)) if not (dbg and dbg.get('gdn_stage', 99) < 5) else None
                interleave(gp, gc)
                cur = nxt
        if dbg and dbg.get("dump_ota"):
            dma(dbg_ota, OTa, ["OTa"], ["dbg_ota"])
            dma(DBG[:, 4096:4096 + 1024], Sst.rearrange("p a b -> p (a b)"), ["S0", "S1"], ["DBG"])
        P.barrier()
        st["off"] = ph3

        h_q = Ring("hq", 4, [128, 4, 128], BF16)
        h_k = Ring("hk", 4, [128, 4, 128], BF16)
        h_i = Ring("hi", 4, [128, 4, 128], BF16)
        h_g = Ring("hg", 4, [128, 512], F32)
        h_w = {n: Ring("h" + n, 3, [128, 512], F32) for n in ("G", "eq", "ek", "ed", "Gx", "enx")}
        h_qg = Ring("hqg", 4, [128, 4, 128], BF16)
        h_kg = Ring("hkg", 3, [128, 4, 128], BF16)
        h_kd = Ring("hkd", 3, [128, 4, 128], BF16)
        h_gl = Ring("hgl", 4, [128, 16], F32)
        h_at = Ring("hat", 4, [128, 4, 128], BF16)
        h_kt = Ring("hkt", 4, [128, 4, 128], BF16)
        h_vt = Ring("hvt", 4, [128, 4, 128], BF16)
        r_s = Ring("hss", 8, [128, 4], F32)
        r_f = {"ot": Ring("hot", 2, [128, 4, 128], F32), "tmp": Ring("htmp", 2, [128, 4, 128], F32)}
        r_b = {"on4": Ring("hon4", 2, [128, 4, 128], BF16)}
        r_s4 = Ring("hs4", 4, [128, 8], F32)
        r_ofo = Ring("hofo", 2, [128, 512], F32)
        r_ofi = Ring("hofi", 6, [128, 512], F32)
        pool(lambda e: e.memset(Sst, 0.0), ["S0", "S1"], ["S0", "S1"])
        pool(lambda e: e.memset(Sb, 0.0), ["Sb0", "Sb1"], ["Sb0", "Sb1"])

        def bc8(ap8):
            return ap8.unsqueeze(2).to_broadcast([128, 8, 64])

        def v864(ap):
            return ap.rearrange("p (a b) -> p a b", b=64)

        def hg_prep(pairs, second, prs):
            for (T, d) in pairs:
                islat = T >= 2
                tsl = slice(T * 128, (T + 1) * 128)
                qT, qk = h_q.next()
                kT, kk = h_k.next()
                iT, ik = h_i.next()
                gT, gk = h_g.next()
                dma(qT, ZBQ[:, :, tsl].rearrange("h p t -> p h t"), ["ZBQ"], [qk])
                dma(kT, ZBK[d * 4:(d + 1) * 4, :, tsl].rearrange("h p t -> p h t"), ["ZBK"], [kk])
                dma(iT, ZBI[:, :, tsl].rearrange("h p t -> p h t"), ["ZBI"], [ik])
                dma(gT.rearrange("p (h t) -> p h t", h=4), ZBL[d * 4:(d + 1) * 4, :, tsl].rearrange("h p t -> p h t"), ["ZBL"], [gk])
                ofi = ofik = None
                ofdefer = False
                if islat and second:
                    ofi, ofik = r_ofi.next()
                    ofdefer = f"OFb{T - 2}" not in P.lastw
                    if not ofdefer:
                        dma(ofi, OFb[T - 2], [f"OFb{T - 2}"], [ofik])
                G, Gk = h_w["G"].next()
                dve(lambda e, a=G, b=gT: e.tensor_tensor_scan(out=a, data0=segm, data1=b, initial=0.0, op0=ALU.mult, op1=ALU.add),
                    [gk, "segm"], [Gk])
                gl, glk = h_gl.next()
                Glast = v864(G)[:, :, 63]
                eq, eqk = h_w["eq"].next()
                ek, ekk = h_w["ek"].next()
                ed, edk = h_w["ed"].next()
                act(actf(gl[:, 0:8], Glast, AF.Exp), [Gk], [glk])
                if d == 0:
                    act(actf(eq, G, AF.Exp), [Gk], [eqk])
                    act(actf(ek, G, AF.Exp, scale=-1.0), [Gk], [ekk])
                    dve(tt(v864(ed), v864(ek), bc8(gl[:, 0:8]), ALU.mult), [ekk, glk], [edk])
                else:
                    Gx, Gxk = h_w["Gx"].next()
                    enx, enxk = h_w["enx"].next()
                    act(actf(gl[:, 8:16], Glast, AF.Exp, scale=-1.0), [Gk], [glk])
                    pool(tt(Gx, G, gT, ALU.subtract), [Gk, gk], [Gxk])
                    act(actf(ed, Gx, AF.Exp), [Gxk], [edk])
                    act(actf(enx, Gx, AF.Exp, scale=-1.0), [Gxk], [enxk])
                    dve(tt(v864(eq), v864(enx), bc8(gl[:, 0:8]), ALU.mult), [enxk, glk], [eqk])
                    pool(tt(v864(ek), v864(ed), bc8(gl[:, 8:16]), ALU.mult), [edk, glk], [ekk])
                yield
                qg, qgk = h_qg.next()
                kg, kgk = h_kg.next()
                kd, kdk = h_kd.next()
                f2 = lambda a: a.rearrange("p h t -> p (h t)")
                dve(tt(f2(qg), f2(qT), eq, ALU.mult), [qk, eqk], [qgk])
                pool(tt(f2(kg), f2(kT), ek, ALU.mult), [kk, ekk], [kgk])
                pool(tt(f2(kd), f2(kT), ed, ALU.mult), [kk, edk], [kdk])
                yield
                MI = BD_le if d == 0 else BD_ge
                heads = []
                at4 = at4k = None
                if islat:
                    pA, pAk = ps_alloc(4)
                    pe(seq(*[mm(pA[:, h * 128:(h + 1) * 128], kg[:, h, :], qg[:, h, :]) for h in range(4)]), [kgk, qgk], pAk)
                    at4, at4k = h_at.next()
                    dve(tt(at4, v4(pA), mb4(MI), ALU.mult), pAk + ["masks"], [at4k])
                pk_, pkk = ps_alloc(4)
                pkb = pk_.bitcast(BF16)
                pe(seq(*[tr(pkb[:, h * 128:(h + 1) * 128], kd[:, h, :], identb) for h in range(4)]), [kdk, "identb"], pkk)
                kt4, kt4k = h_kt.next()
                act(cp(kt4, v4(pkb[:, 0:512])), pkk, [kt4k])
                pv_, pvk = ps_alloc(4)
                pvb = pv_.bitcast(BF16)
                pe(seq(*[tr(pvb[:, h * 128:(h + 1) * 128], iT[:, h, :], identb) for h in range(4)]), [ik, "identb"], pvk)
                vt4, vt4k = h_vt.next()
                act(cp(vt4, v4(pvb[:, 0:512])), pvk, [vt4k])
                for h in range(4):
                    heads.append(dict(h=h, c=d * 4 + h, at=(at4[:, h, :] if islat else None), atk=at4k,
                                      kt=kt4[:, h, :], ktk=kt4k, vt=vt4[:, h, :], vtk=vt4k))
                prs.append(dict(T=T, d=d, islat=islat, qg=qg, qgk=qgk, gl=gl, glk=glk, ofi=ofi, ofik=ofik,
                                ofdefer=ofdefer, heads=heads))
            yield

        def otb_view(j):
            def colap(h):
                return OTb[:, h, :].rearrange("p (r w) -> p r w", w=64)[:, :, 2 * j:2 * j + 2]
            return colap

        def hg_chain(preps, second):
            for pr in preps:
                if pr["islat"]:
                    pr["pO"], pr["pOk"] = PS[6 + pr["d"]][:, :], [f"psb{6 + pr['d']}"]
            for sub in range(2):
                def rs(pr):
                    j = sub if pr["d"] == 0 else 1 - sub
                    return j, slice(64 * j, 64 * j + 64)
                for pr in preps:
                    d = pr["d"]
                    j, r = rs(pr)
                    hs = pr["heads"]
                    if pr["islat"]:
                        ops = []
                        for hd in hs:
                            h = hd["h"]
                            ops.append(mm(pr["pO"][r, h * 128:(h + 1) * 128], pr["qg"][:, h, r], Sb[:, d * 4 + h, :], start=True, stop=False))
                            ops.append(mm(pr["pO"][r, h * 128:(h + 1) * 128], hd["at"][r, r], hd["vt"][r, :], start=False, stop=True))
                        pe(seq(*ops), [pr["qgk"], f"Sb{d}"] + [hd["atk"] for hd in hs] + [hd["vtk"] for hd in hs], pr["pOk"])
                    pS, pSk = ps_alloc(4)
                    pe(seq(*[mm(pS[:, hd["h"] * 128:(hd["h"] + 1) * 128], hd["kt"][r, :], hd["vt"][r, :]) for hd in hs]),
                       [hd["ktk"] for hd in hs] + [hd["vtk"] for hd in hs], pSk)
                    pr["pS"], pr["pSk"] = pS, pSk
                    S4 = Sst[:, d * 4:(d + 1) * 4, :]
                    glj = pr["gl"][:, 0:8].rearrange("p (h j) -> p h j", j=2)[:, :, j]
                    pool(tt(S4, S4, bc4(glj), ALU.mult), [pr["glk"], f"S{d}"], [f"S{d}"])
                yield
                for pr in preps:
                    d = pr["d"]
                    S4 = Sst[:, d * 4:(d + 1) * 4, :]
                    dve(tt(S4, S4, v4(pr["pS"]), ALU.add), pr["pSk"] + [f"S{d}"], [f"S{d}"])
                    act(cp(Sb[:, d * 4:(d + 1) * 4, :], S4), [f"S{d}"], [f"Sb{d}"])
                yield
            for pr in preps:
                if pr["ofdefer"]:
                    assert f"OFb{pr['T'] - 2}" in P.lastw
                    dma(pr["ofi"], OFb[pr["T"] - 2], [f"OFb{pr['T'] - 2}"], [pr["ofik"]])
            for pr in preps:
                if not pr["islat"]:
                    continue
                j = pr["T"] - 2
                if not second:
                    ofo, ofok = r_ofo.next()
                    act(cp(ofo, pr["pO"]), pr["pOk"], [ofok])
                    dstore(OFb[j], ofo, [ofok], [f"OFb{j}"])
                else:
                    ot, otk = r_f["ot"].next()
                    dve(tt(ot, v4(pr["pO"]), v4(pr["ofi"]), ALU.add), pr["pOk"] + [pr["ofik"]], [otk])
                    finalize4(ot, otk, "OTb", None, QSCALE, hgj=j)
            yield

        hsteps = dbg.get("hg_steps", NT) if dbg else NT
        if not (dbg and dbg.get("skip_hg")):
            cur = []
            psst["grp"] = 'p'
            run_gen(hg_prep([(0, 0), (bwd_order[0], 1)], False, cur))
            psst["grp"] = None
            for i in range(hsteps):
                nxt = []
                gp = hg_prep([(i + 1, 0), (bwd_order[i + 1], 1)], (i + 1) >= 18, nxt) if i + 1 < hsteps else None
                gc = hg_chain(cur, second=(i >= 18))
                interleave(gp, gc, ratio=1)
                cur = nxt
        if dbg and dbg.get("dump_otb"):
            dma(dbg_otb, OTb, ["OTb"], ["dbg_otb"])
        P.barrier()
        st["off"] = ph3

        waoB = alloc([128, 4, 1024], BF16)
        wboB = alloc([128, 4, 1024], BF16)
        woB = alloc([128, 8, 1024], BF16)
        lng_row = alloc([128, 1024], F32)
        lnb_row = alloc([128, 1024], F32)
        Hr = Ring("oH", 4, [128, 1024], F32)
        xr4 = Ring("ox", 2, [128, 1024], F32)
        agr = Ring("oag", 1, [128, 4, 512], BF16)
        bgr = Ring("obg", 1, [128, 4, 512], BF16)
        mr = Ring("om", 4, [128, 2, 512], BF16)
        mar = Ring("oma", 1, [128, 4, 512], BF16)
        mbr = Ring("omb", 1, [128, 4, 512], BF16)
        t1r = Ring("ot1", 2, [128, 512], F32)
        t2r = Ring("ot2", 2, [128, 512], F32)
        mixr = Ring("omix", 1, [128, 8, 512], BF16)
        st4 = Ring("ost", 3, [128, 2, 6], F32)
        mv4 = Ring("omv", 4, [128, 4], F32)
        dma(lng_row, lng.partition_broadcast(128), [], ["lng_row"])
        dma(lnb_row, lnb.partition_broadcast(128), [], ["lnb_row"])
        for (wsrc, wdst, n, key) in ((wao, waoB, 4, "waoB"), (wbo, wboB, 4, "wboB"), (wo, woB, 8, "woB")):
            for i in range(n):
                hbuf, hk = Hr.next()
                dma(hbuf, wsrc[:, i, :], [], [hk])
                pool(cp(wdst[:, i, :], hbuf), [hk], [key])
        nblk = dbg.get("out_blocks", 8) if dbg else 8
        p4pend = []
        for bb in range(nblk):
            bsl = slice(bb * 512, (bb + 1) * 512)
            ag, agk = agr.next()
            bg, bgk = bgr.next()
            dma(ag, ZAG[:, :, bsl].rearrange("h p t -> p h t"), ["ZAG"], [agk])
            dma(bg, ZBGT[:, :, bsl].rearrange("h p t -> p h t"), ["ZBGT"], [bgk])
            ma, mak = mar.next()
            mb, mbk = mbr.next()
            dve(stt(ma, OTa[:, :, bsl], gains[:, 0:1], ag, ALU.mult, ALU.mult), ["OTa", "gains", agk], [mak])
            dve(stt(mb, OTb[:, :, bsl], gains[:, 1:2], bg, ALU.mult, ALU.mult), ["OTb", "gains", bgk], [mbk])
            mix, mixk = mixr.next()
            for cc in range(8):
                mt, mtk = mr.next()
                dma(mt[:, 0, :], ZM[cc, :, bsl], ["ZM"], [mtk])
                dma(mt[:, 1, :], ZM[8 + cc, :, bsl], ["ZM"], [mtk])
                pYa, pYak = ps_alloc(4)
                pe(seq(*[mm(pYa, waoB[:, h, cc * 128:(cc + 1) * 128], ma[:, h, :], start=(h == 0), stop=(h == 3))
                         for h in range(4)]), ["waoB", mak], pYak)
                pYb, pYbk = ps_alloc(4)
                pe(seq(*[mm(pYb, wboB[:, h, cc * 128:(cc + 1) * 128], mb[:, h, :], start=(h == 0), stop=(h == 3))
                         for h in range(4)]), ["wboB", mbk], pYbk)
                t1, t1k = t1r.next()
                t2, t2k = t2r.next()
                dve(tt(t1, pYa, mt[:, 0, :], ALU.mult), pYak + [mtk], [t1k])
                dve(tt(t2, pYb, mt[:, 1, :], ALU.mult), pYbk + [mtk], [t2k])
                pool(tt(mix[:, cc, :], t1, t2, ALU.add), [t1k, t2k], [mixk])
            for ti in range(4):
                lt = bb * 4 + ti
                xt, xk = xr4.next()
                dma(xt, x[lt * 128:(lt + 1) * 128, :], [], [xk])
                H, Hk = Hr.next()
                for half in range(2):
                    pSu, pSuk = ps_alloc(4)
                    pe(seq(*[mm(pSu, mix[:, cc, ti * 128:(ti + 1) * 128], woB[:, cc, half * 512:(half + 1) * 512],
                                start=(cc == 0), stop=(cc == 7)) for cc in range(8)]), [mixk, "woB"], pSuk)
                    dve(tt(H[:, half * 512:(half + 1) * 512], pSu, gate_row[:, half * 512:(half + 1) * 512], ALU.mult),
                        pSuk + ["gate_row"], [Hk])
                dve(stt(H, xt, ALPHA, H, ALU.mult, ALU.add), [Hk, xk], [Hk])
                stt_, stk = st4.next()
                mv, mvk = mv4.next()
                dve(seq(lambda e, a=stt_, b=H: e.bn_stats(out=a[:, 0, :], in_=b[:, 0:512]),
                        lambda e, a=stt_, b=H: e.bn_stats(out=a[:, 1, :], in_=b[:, 512:1024])), [Hk], [stk])
                dve(lambda e, a=mv, b=stt_: e.bn_aggr(out=a[:, 0:2], in_=b.rearrange("p a b -> p (a b)")), [stk], [mvk])
                act(actf(mv[:, 2:3], mv[:, 1:2], AF.Sqrt, bias=EPS), [mvk], [mvk + "s"])
                if p4pend:
                    p4pend.pop()()

                def late(H=H, Hk=Hk, mv=mv, mvk=mvk, lt=lt):
                    dve(lambda e, a=mv: e.reciprocal(out=a[:, 2:3], in_=a[:, 2:3]), [mvk + "s"], [mvk + "s"])
                    dve(stt(mv[:, 3:4], mv[:, 0:1], -1.0, mv[:, 2:3], ALU.mult, ALU.mult), [mvk, mvk + "s"], [mvk + "n"])
                    act(actf(H, H, AF.Identity, bias=mv[:, 3:4], scale=mv[:, 2:3]), [Hk, mvk + "s", mvk + "n"], [Hk])
                    pool(tt(H, H, lng_row, ALU.mult), [Hk, "lng_row"], [Hk])
                    pool(tt(H, H, lnb_row, ALU.add), [Hk, "lnb_row"], [Hk])
                    dstore(y[lt * 128:(lt + 1) * 128, :], H, [Hk], ["y"])
                p4pend.append(late)
        while p4pend:
            p4pend.pop()()

        final_cnt = dict(P.cnt)

        @block.sync
        def _(e):
            P.replay('sp', e, sems)
            for l, n in final_cnt.items():
                if l[0] == 'd' and l != 'dve':
                    e.wait_ge(sems[l], 16 * n)

        @block.tensor
        def _(e):
            P.replay('pe', e, sems)

        @block.scalar
        def _(e):
            P.replay('act', e, sems)

        @block.vector
        def _(e):
            P.replay('dve', e, sems)

        @block.gpsimd
        def _(e):
            P.replay('pool', e, sems)
    return nc, P


def _consts():
    p = np.arange(128)[:, None]
    f = np.arange(128)[None, :]
    same = (p // 64) == (f // 64)
    m = np.stack([p <= f, p < f, p >= f, p > f, (p <= f) & same, (p < f) & same, (p >= f) & same, (p > f) & same], axis=1).astype(np.float32)
    segm = np.ones((128, 512), np.float32)
    segm[:, ::64] = 0.0
    return np.eye(128, dtype=np.float32), np.ascontiguousarray(m), segm


def make_in_maps(inp):
    f = np.float32
    A = lambda a: np.ascontiguousarray(a, dtype=f)
    w_in = inp['w_in'][0]
    cols = np.r_[0:1536, 1552:6672]
    win = A(w_in[:, cols].reshape(8, 128, 52, 128).transpose(2, 1, 0, 3))
    wab = A(w_in[:, 1536:1552].reshape(8, 128, 16).transpose(1, 0, 2))
    w_mod = inp['w_mod'][0]
    wmod = A(w_mod.reshape(8, 128, 6, 512).transpose(2, 1, 0, 3))
    b_mod = inp['b_mod'][0]
    bmodc = A(b_mod[:2048].reshape(16, 128).T)
    bmodg = A(b_mod[2048:].reshape(1, 1024))
    convw = A(inp['conv_w'][0].reshape(5, 12, 128).transpose(2, 1, 0))
    alog = A(inp['a_log'][0].reshape(1, 8))
    dtb = A(inp['dt_bias'][0].reshape(1, 8))
    lbp = A(inp['lb_param'].reshape(2, 2, 4, 128).transpose(3, 0, 1, 2).reshape(128, 2, 8))
    ang = A(inp['a_norm_g'][0].reshape(128, 1))
    bng = A(inp['b_norm_g'][0].reshape(128, 1))
    wao = A(inp['w_a_out'][0].reshape(4, 128, 1024).transpose(1, 0, 2))
    wbo = A(inp['w_b_out'][0].reshape(4, 128, 1024).transpose(1, 0, 2))
    wo = A(inp['w_out'][0].reshape(8, 128, 1024).transpose(1, 0, 2))
    lng = A(inp['ln_g'][0].reshape(1, 1024))
    lnb = A(inp['ln_b'][0].reshape(1, 1024))
    cidf, cmask, csegm = _consts()
    maps = []
    for b in range(8):
        ccv = np.stack([inp['c'][b].reshape(8, 128).T, inp['c_ctx'].reshape(8, 128).T], axis=2)
        maps.append(dict(x=A(inp['x'][b]), ctx=A(inp['ctx'][b]), cc=A(ccv), wmod=wmod, bmodc=bmodc, bmodg=bmodg,
                         win=win, wab=wab, convw=convw, alog=alog, dtb=dtb, lbp=lbp, ang=ang, bng=bng,
                         wao=wao, wbo=wbo, wo=wo, lng=lng, lnb=lnb, cidf=cidf, cmask=cmask, csegm=csegm))
    return maps


def kernel(**inputs):
    nc, _ = build()
    maps = make_in_maps(inputs)
    res = run_bass_kernel_spmd(nc, maps, core_ids=list(range(8)))
    return np.stack([np.asarray(r["y"], dtype=np.float32) for r in res.results], axis=0)
```

```python
import numpy as np
import ml_dtypes
from contextlib import ExitStack
import concourse.bass as bass
import concourse.mybir as mybir
from concourse.bass_utils import run_bass_kernel_spmd

F32 = mybir.dt.float32
BF16 = mybir.dt.bfloat16
U8 = mybir.dt.uint8
AF = mybir.ActivationFunctionType
ALU = mybir.AluOpType

NT = 34
NTOK = 4352
QSCALE = 128 ** -0.5
ALPHA = 2.0 ** 0.25
EPS = 1e-6


class Prog:
    ENG = ('pe', 'act', 'dve', 'pool', 'sp')

    def __init__(self):
        self.ops = {e: [] for e in self.ENG}
        self.cnt = {}
        self.know = {e: {} for e in self.ENG}
        self.opclock = {}
        self.lastw = {}
        self.readers = {}
        self.pending = {e: {} for e in self.ENG}
        self.nlanes = {'sp': 8, 'pool': 6, 'act': 4}
        self.rr = {e: 0 for e in self.ENG}

    def lanes(self):
        out = list(self.ENG[:4])
        for q, n in self.nlanes.items():
            out += [f'd{q}{i}' for i in range(n)]
        return out

    def barrier(self):
        snap = dict(self.cnt)
        for e in self.ENG:
            p = self.pending[e]
            for l, n in snap.items():
                if p.get(l, 0) < n:
                    p[l] = n

    def emit(self, eng, fn, reads=(), writes=(), dma=False):
        if dma:
            i = self.rr[eng]
            self.rr[eng] = (i + 1) % self.nlanes[eng]
            lane = f'd{eng}{i}'
        else:
            lane = eng
        psr = [r for r in reads if r.startswith('psb')]
        if psr:
            reads = [r for r in reads if not r.startswith('psb')]
            writes = list(writes) + psr
        deps = dict(self.pending[eng])
        self.pending[eng] = {}

        def add(l, n):
            if deps.get(l, 0) < n:
                deps[l] = n
        for r in reads:
            w = self.lastw.get(r)
            if w:
                add(*w)
        for r in writes:
            w = self.lastw.get(r)
            if w and not (w[0] == lane and not dma):
                add(*w)
            for l, n in self.readers.get(r, {}).items():
                if not (l == lane and not dma):
                    add(l, n)
        if dma and self.cnt.get(lane, 0) > 0:
            add(lane, self.cnt[lane])
        know = self.know[eng]
        waits = [(l, n) for l, n in deps.items() if know.get(l, 0) < n]
        for l, n in deps.items():
            for l2, n2 in self.opclock.get((l, n), {}).items():
                if know.get(l2, 0) < n2:
                    know[l2] = n2
            if know.get(l, 0) < n:
                know[l] = n
        n = self.cnt.get(lane, 0) + 1
        self.cnt[lane] = n
        self.opclock[(lane, n)] = dict(know)
        self.ops[eng].append((waits, fn, lane))
        for r in reads:
            self.readers.setdefault(r, {})[lane] = n
        for r in writes:
            self.lastw[r] = (lane, n)
            self.readers[r] = {}

    def replay(self, name, eng, sems):
        for waits, fn, lane in self.ops[name]:
            for l, n in waits:
                eng.wait_ge(sems[l], n * (16 if l[0] == 'd' and l != 'dve' else 1))
            inst = fn(eng)
            inst.then_inc(sems[lane], 16 if (lane[0] == 'd' and lane != 'dve') else 1)


def seq(*fns):
    def f(e):
        r = None
        for g in fns:
            r = g(e)
        return r
    return f


def build(dbg=None):
    nc = bass.Bass("TRN2", target_bir_lowering=False)
    P = Prog()

    def din(name, shape, dt=F32):
        return nc.dram_tensor(name, list(shape), dt, kind="ExternalInput").ap()

    x = din("x", [4096, 1024])
    ctx = din("ctx", [256, 1024])
    cc = din("cc", [128, 8, 2])
    wmod = din("wmod", [6, 128, 8, 512])
    bmodc = din("bmodc", [128, 16])
    bmodg = din("bmodg", [1, 1024])
    win = din("win", [52, 128, 8, 128])
    wab = din("wab", [128, 8, 16])
    convw = din("convw", [128, 12, 5])
    alog = din("alog", [1, 8])
    dtb = din("dtb", [1, 8])
    lbp = din("lbp", [128, 2, 8])
    ang = din("ang", [128, 1])
    bng = din("bng", [128, 1])
    wao = din("wao", [128, 4, 1024])
    wbo = din("wbo", [128, 4, 1024])
    wo = din("wo", [128, 8, 1024])
    lng = din("lng", [1, 1024])
    lnb = din("lnb", [1, 1024])
    cidf = din("cidf", [128, 128])
    cmask = din("cmask", [128, 8, 128])
    csegm = din("csegm", [128, 512])
    y = nc.dram_tensor("y", [4096, 1024], F32, kind="ExternalOutput").ap()

    def dscr(name, shape, dt):
        kind = {"kind": "ExternalOutput"} if (dbg and name in dbg) else {}
        return nc.dram_tensor(name, list(shape), dt, **kind).ap()

    ZQ = dscr("ZQ", [4, 128, NTOK], BF16)
    ZK = dscr("ZK", [4, 128, NTOK], BF16)
    ZV = dscr("ZV", [4, 128, NTOK], BF16)
    ZAG = dscr("ZAG", [4, 128, 4096], BF16)
    ZBGT = dscr("ZBGT", [4, 128, 4096], BF16)
    ZM = dscr("ZM", [16, 128, 4096], BF16)
    ZBQ = dscr("ZBQ", [4, 128, NTOK], BF16)
    ZBK = dscr("ZBK", [8, 128, NTOK], BF16)
    ZBL = dscr("ZBL", [8, 128, NTOK], F32)
    ZBI = dscr("ZBI", [4, 128, NTOK], BF16)
    OFa = dscr("OFa", [32, 128, 512], F32)
    OFb = dscr("OFb", [32, 128, 512], F32)
    DBG = dscr("DBG", [128, 8192], F32) if dbg else None
    dbg_ota = dscr("dbg_ota", [128, 4, 4096], BF16) if dbg else None
    dbg_otb = dscr("dbg_otb", [128, 4, 4096], BF16) if dbg else None

    es = ExitStack()
    with es:
        ARENA = 204 * 1024
        arena = es.enter_context(nc.sbuf_tensor("arena", [128, ARENA], U8))
        PS = [es.enter_context(nc.psum_tensor(f"ps{i}", [128, 512], F32)) for i in range(8)]
        sems = {l: es.enter_context(nc.semaphore(f"s_{l}")) for l in P.lanes()}
        block = es.enter_context(nc.Block())

        st = {"off": 0, "n": 0}

        def alloc(shape, dt, name=None):
            nb = int(np.prod(shape[1:])) * (4 if dt == F32 else 2)
            off = (st["off"] + 63) // 64 * 64
            assert off + nb <= ARENA, (name, off, nb)
            st["off"] = off + nb
            ap = arena[:, off:off + nb].bitcast(dt)
            if len(shape) == 3:
                ap = ap.rearrange("p (a b) -> p a b", a=shape[1])
            elif len(shape) == 4:
                ap = ap.rearrange("p (a b c) -> p a b c", a=shape[1], b=shape[2])
            st["n"] += 1
            return ap

        class Ring:
            def __init__(self, name, n, shape, dt):
                self.name = name
                self.aps = [alloc(shape, dt, name) for _ in range(n)]
                self.i = 0

            def next(self):
                i = self.i
                self.i = (i + 1) % len(self.aps)
                return self.aps[i], f"{self.name}#{i}"

        psst = {"i": 0, "q": [0] * 8, "ip": 0, "ic": 0, "grp": None}

        def ps_alloc(nq=1):
            grp = psst["grp"]
            if grp == 'p':
                b = psst["ip"] % 4
                psst["ip"] += 1
            elif grp == 'c':
                b = 4 + psst["ic"] % 2
                psst["ic"] += 1
            else:
                b = psst["i"] % 6
                psst["i"] += 1
            if nq == 4:
                s_ = 0
            else:
                s_ = psst["q"][b]
                psst["q"][b] = (s_ + nq) % 4
                assert s_ + nq <= 4
            ap = PS[b][:, s_ * 128:(s_ + nq) * 128]
            return ap, [f"psb{b}"]

        def bfv(ap):
            return ap.bitcast(BF16)[:, 0:128]

        STQ = 'pool'
        pe = lambda fn, r, w: P.emit('pe', fn, r, w)
        act = lambda fn, r, w: P.emit('act', fn, r, w)
        dve = lambda fn, r, w: P.emit('dve', fn, r, w)
        pool = lambda fn, r, w: P.emit('pool', fn, r, w)

        def dma(out, in_, r, w, q='sp'):
            P.emit(q, lambda e: e.dma_start(out=out, in_=in_), r, w, dma=True)

        def dstore(out, in_, r, w):
            dma(out, in_, r, w, q=STQ)

        def mm(out, lhsT, rhs, start=True, stop=True):
            return lambda e: e.matmul(out, lhsT=lhsT, rhs=rhs, start=start, stop=stop)

        def tr(out, in_, ident):
            return lambda e: e.transpose(out=out, in_=in_, identity=ident)

        def actf(out, in_, func, bias=None, scale=None, accum_out=None):
            kw = {}
            if bias is not None:
                kw["bias"] = bias
            if scale is not None:
                kw["scale"] = scale
            if accum_out is not None:
                kw["accum_out"] = accum_out
            return lambda e: e.activation(out=out, in_=in_, func=func, **kw)

        def tt(out, in0, in1, op):
            return lambda e: e.tensor_tensor(out=out, in0=in0, in1=in1, op=op)

        def ts(out, in0, s1, s2, op0, op1=None):
            if op1 is None:
                return lambda e: e.tensor_scalar(out=out, in0=in0, scalar1=s1, scalar2=None, op0=op0)
            return lambda e: e.tensor_scalar(out=out, in0=in0, scalar1=s1, scalar2=s2, op0=op0, op1=op1)

        def stt(out, in0, scalar, in1, op0, op1):
            return lambda e: e.scalar_tensor_tensor(out=out, in0=in0, scalar=scalar, in1=in1, op0=op0, op1=op1)

        def cp(out, in_):
            return lambda e: (e.tensor_copy(out=out, in_=in_) if hasattr(e, 'tensor_copy') else e.activation(out=out, in_=in_, func=AF.Copy))

        identf = alloc([128, 128], F32)
        identb = alloc([128, 128], BF16)
        masks = alloc([128, 8, 128], F32)
        onesf = alloc([128, 128], F32)
        segm = alloc([128, 512], F32)
        M_le, M_lt, M_ge, M_gt, BD_le, BD_lt, BD_ge, BD_gt = [masks[:, i, :] for i in range(8)]
        gate_row = alloc([128, 1024], F32)
        GB = alloc([128, NT, 16], F32)
        modc = alloc([128, 16, 2], F32)
        lbc = alloc([128, 8], F32)
        omlb = alloc([128, 8], F32)
        nomlb = alloc([128, 8], F32)
        gains = alloc([128, 2], F32)
        cw = alloc([128, 12, 5], F32)
        rowc = alloc([128, 16], F32)
        persist_off = st["off"]

        dma(identf, cidf, [], ["identf"])
        dma(masks, cmask, [], ["masks"])
        dma(segm, csegm, [], ["segm"])
        dma(cw, convw, [], ["cw"])
        dma(gains[:, 0:1], ang, [], ["gains"])
        dma(gains[:, 1:2], bng, [], ["gains"])
        dma(rowc[:, 0:8], alog.partition_broadcast(128), [], ["rowc"])
        dma(rowc[:, 8:16], dtb.partition_broadcast(128), [], ["rowc"])
        dve(cp(identb, identf), ["identf"], ["identb"])
        pool(lambda e: e.memset(onesf, 1.0), [], ["onesf"])
        act(actf(rowc[:, 0:8], rowc[:, 0:8], AF.Exp), ["rowc"], ["rowc"])
        dve(ts(rowc[:, 0:8], rowc[:, 0:8], -1.0, None, ALU.mult), ["rowc"], ["rowc"])

        ph0 = st["off"]
        cct = alloc([128, 8, 2], F32)
        sil = alloc([128, 8, 2], F32)
        srep = alloc([128, 8, 128], F32)
        bmc = alloc([128, 16], F32)
        lbt = alloc([128, 2, 8], F32)
        wmr = Ring("wm", 2, [128, 8, 512], F32)
        dma(cct, cc, [], ["cct"])
        dma(bmc, bmodc, [], ["bmc"])
        dma(lbt, lbp, [], ["lbt"])
        dma(gate_row, bmodg.partition_broadcast(128), [], ["gate_row"])
        act(actf(sil, cct, AF.Silu), ["cct"], ["sil"])
        dve(cp(srep, sil[:, :, 0:1].to_broadcast([128, 8, 128])), ["sil"], ["srep"])
        dve(tt(lbc, lbt[:, 0, :], lbt[:, 1, :], ALU.subtract), ["lbt"], ["lbc"])
        act(actf(lbc, lbc, AF.Sigmoid), ["lbc"], ["lbc"])
        dve(ts(omlb, lbc, -1.0, 1.0, ALU.mult, ALU.add), ["lbc"], ["omlb"])
        dve(ts(nomlb, lbc, -1.0, None, ALU.add), ["lbc"], ["nomlb"])
        for blk in range(6):
            wt, wk = wmr.next()
            dma(wt, wmod[blk], [], [wk])
            if blk < 4:
                for jj in range(4):
                    j = blk * 4 + jj
                    pt, pk = ps_alloc(1)
                    pe(seq(*[mm(pt[:, 0:2], wt[:, kc, jj * 128:(jj + 1) * 128], sil[:, kc, :],
                                start=(kc == 0), stop=(kc == 7)) for kc in range(8)]),
                       [wk, "sil"], pk)
                    dve(ts(modc[:, j, :], pt[:, 0:2], bmc[:, j:j + 1], 1.0 if j >= 8 else 0.0, ALU.add, ALU.add),
                        pk + ["bmc"], ["modc"])
            else:
                pt, pk = ps_alloc(4)
                pe(seq(*[mm(pt, srep[:, kc, :], wt[:, kc, :], start=(kc == 0), stop=(kc == 7))
                         for kc in range(8)]), [wk, "srep"], pk)
                gs = gate_row[:, (blk - 4) * 512:(blk - 3) * 512]
                dve(tt(gs, gs, pt, ALU.add), pk + ["gate_row"], ["gate_row"])
        P.barrier()
        st["off"] = ph0

        uT = alloc([128, 8, NTOK], BF16)
        ph2 = st["off"]
        xr = Ring("xt", 3, [128, 1024], F32)
        xnr = Ring("xn", 3, [128, 1024], BF16)
        str_ = Ring("st", 3, [128, 2, 6], F32)
        mvr = Ring("mv", 4, [128, 4], F32)

        def p1_A(T):
            src = ctx[T * 128:(T + 1) * 128, :] if T < 2 else x[(T - 2) * 128:(T - 1) * 128, :]
            xt, xk = xr.next()
            stt_, stk = str_.next()
            mv, mvk = mvr.next()
            dma(xt, src, [], [xk])
            dve(seq(lambda e, a=stt_, b=xt: e.bn_stats(out=a[:, 0, :], in_=b[:, 0:512]),
                    lambda e, a=stt_, b=xt: e.bn_stats(out=a[:, 1, :], in_=b[:, 512:1024])), [xk], [stk])
            dve(lambda e, a=mv, b=stt_: e.bn_aggr(out=a[:, 0:2], in_=b.rearrange("p a b -> p (a b)")), [stk], [mvk])
            act(actf(mv[:, 2:3], mv[:, 1:2], AF.Sqrt, bias=EPS), [mvk], [mvk + "s"])
            return dict(T=T, xt=xt, xk=xk, mv=mv, mvk=mvk)

        def p1_B(c_):
            mv, mvk, xt, xk = c_["mv"], c_["mvk"], c_["xt"], c_["xk"]
            xn, xnk = xnr.next()
            dve(lambda e, a=mv: e.reciprocal(out=a[:, 2:3], in_=a[:, 2:3]), [mvk + "s"], [mvk + "s"])
            dve(stt(mv[:, 3:4], mv[:, 0:1], -1.0, mv[:, 2:3], ALU.mult, ALU.mult), [mvk, mvk + "s"], [mvk + "n"])
            act(actf(xn, xt, AF.Identity, bias=mv[:, 3:4], scale=mv[:, 2:3]), [xk, mvk + "s", mvk + "n"], [xnk])
            pt, pk = ps_alloc(4)
            ptb = pt.bitcast(BF16)
            pe(seq(*[tr(ptb[:, kc * 128:(kc + 1) * 128], xn[:, kc * 128:(kc + 1) * 128], identb) for kc in range(8)]),
               [xnk, "identb"], pk)
            c_["ptb"], c_["pk"] = ptb, pk

        def p1_C(c_):
            T, ptb, pk = c_["T"], c_["ptb"], c_["pk"]
            w = 1 if T < 2 else 0
            for kc in range(8):
                o = uT[:, kc, T * 128:(T + 1) * 128]
                i = ptb[:, kc * 128:(kc + 1) * 128]
                if kc % 2 == 0:
                    act(actf(o, i, AF.Identity, bias=modc[:, kc, w:w + 1], scale=modc[:, 8 + kc, w:w + 1]),
                        pk + ["modc"], [f"uT{T}"])
                else:
                    dve(ts(o, i, modc[:, 8 + kc, w:w + 1], modc[:, kc, w:w + 1], ALU.mult, ALU.add),
                        pk + ["modc"], [f"uT{T}"])

        p1q = []
        for T in range(NT + 2):
            if T < NT:
                p1q.append(p1_A(T))
            if 1 <= T <= NT:
                p1_B(p1q[T - 1])
            if T >= 2:
                p1_C(p1q[T - 2])
        uT_all = [f"uT{T}" for T in range(NT)]

        wfr = Ring("wf", 2, [128, 8, 128], F32)
        wbr = Ring("wb", 4, [128, 8, 128], BF16)
        ZL = 4360
        bufA = alloc([128, ZL], F32)
        bufB = alloc([128, ZL], F32)
        bufC = alloc([128, ZL], F32)
        stg = Ring("stg", 3, [128, NTOK], BF16)
        wabf = alloc([128, 8, 16], F32)
        wabb = alloc([128, 8, 16], BF16)
        tmpab = alloc([128, NT, 8], F32)
        tmpab2 = alloc([128, NT, 8], F32)
        ssr = Ring("ss", 2, [128, 512], F32)

        dma(wabf, wab, [], ["wabf"])
        pool(cp(wabb, wabf), ["wabf"], ["wabb"])
        for T in range(NT):
            pt, pk = ps_alloc(1)
            pe(seq(*[mm(pt[:, 0:16], uT[:, kc, T * 128:(T + 1) * 128], wabb[:, kc, :], start=(kc == 0), stop=(kc == 7))
                     for kc in range(8)]), [f"uT{T}", "wabb"], pk)
            dve(cp(GB[:, T, :], pt[:, 0:16]), pk, ["GBraw"])
        a_ = tmpab
        b_ = tmpab2
        dve(tt(a_, GB[:, :, 0:8], rowc[:, 8:16].unsqueeze(1).to_broadcast([128, NT, 8]), ALU.add), ["GBraw", "rowc"], ["tmpab"])
        dve(stt(b_, a_, -1.0, a_, ALU.mult, ALU.max), ["tmpab"], ["tmpab2"])
        act(actf(b_, b_, AF.Exp, scale=-1.0), ["tmpab2"], ["tmpab2"])
        act(actf(b_, b_, AF.Ln, bias=1.0), ["tmpab2"], ["tmpab2"])
        dve(stt(a_, a_, 0.0, b_, ALU.max, ALU.add), ["tmpab", "tmpab2"], ["tmpab"])
        dve(tt(GB[:, :, 0:8], a_, rowc[:, 0:8].unsqueeze(1).to_broadcast([128, NT, 8]), ALU.mult), ["tmpab", "rowc", "GBraw"], ["GBg"])
        act(actf(GB[:, :, 8:16], GB[:, :, 8:16], AF.Sigmoid), ["GBraw"], ["GBb"])
        GBk = ["GBg", "GBb"]

        blocks = [(0, 256)] + [(256 + i * 512, 512) for i in range(8)]

        def cm_view(buf, r0, nr=8):
            v = buf[:, 256:256 + 4096].rearrange("p (w r) -> p w r", r=64)[:, :, r0:r0 + nr]
            return v.rearrange("p w r -> p r w")

        def ps_rw(pt):
            return pt.rearrange("p (r w) -> p r w", w=64)

        wpre = {}

        def prefetch_w(j):
            if j in wpre:
                return
            wf, wfk = wfr.next()
            wb, wbk = wbr.next()
            dma(wf, win[j], [], [wfk])
            pool(cp(wb, wf), [wfk], [wbk])
            wpre[j] = (wb, wbk)

        def project(j, lat_only, epi):
            prefetch_w(j)
            wb, wbk = wpre[j]
            for bi, (c0, n) in enumerate(blocks):
                if lat_only and bi == 0:
                    continue
                pt, pk = ps_alloc(4)
                T0 = c0 // 128
                pe(seq(*[mm(pt[:, 0:n], wb[:, kc, :], uT[:, kc, c0:c0 + n], start=(kc == 0), stop=(kc == 7))
                         for kc in range(8)]), [wbk] + [f"uT{T0 + i}" for i in range(n // 128)], pk)
                epi(bi, c0, n, pt, pk)

        def zpos(c0):
            return c0 + 2 if c0 < 256 else c0 + 6

        pool(lambda e: e.memset(bufA, 0.0), [], ["bufA"])

        def conv_chunk(j, kind, h):
            def epi(bi, c0, n, pt, pk):
                z0 = zpos(c0)
                act(cp(bufA[:, z0:z0 + n], pt[:, 0:n]) if False else actf(bufA[:, z0:z0 + n], pt[:, 0:n], AF.Identity), pk, ["bufA"])
            project(j, False, epi)
            L = 4356
            dve(ts(bufB[:, 0:L], bufA[:, 0:L], cw[:, j, 0:1], None, ALU.mult), ["bufA", "cw"], ["bufB"])
            for k in range(1, 5):
                dve(stt(bufB[:, 0:L], bufA[:, k:k + L], cw[:, j, k:k + 1], bufB[:, 0:L], ALU.mult, ALU.add),
                    ["bufA", "bufB", "cw"], ["bufB"])
            return lambda: conv_part2(j, kind, h)

        def conv_part2(j, kind, h):
            L = 4356
            sg, sgk = stg.next()
            if kind == 'v':
                act(actf(sg[:, 0:256], bufB[:, 0:256], AF.Silu), ["bufB"], [sgk])
                act(actf(sg[:, 256:NTOK], bufB[:, 260:4356], AF.Silu), ["bufB"], [sgk])
                dstore(ZV[h], sg, [sgk], ["ZV"])
                return
            act(actf(bufC[:, 0:L], bufB[:, 0:L], AF.Silu), ["bufB"], ["bufC"])
            pool(tt(bufB[:, 0:L], bufC[:, 0:L], bufC[:, 0:L], ALU.mult), ["bufC", "bufB"], ["bufB"])
            for (c0, n) in blocks:
                a0 = c0 if c0 < 256 else c0 + 4
                pt, pk = ps_alloc(4)
                pe(mm(pt[:, 0:n], onesf, bufB[:, a0:a0 + n]), ["onesf", "bufB"], pk)
                ss, ssk = ssr.next()
                act(actf(ss[:, 0:n], pt[:, 0:n], AF.Sqrt, bias=EPS), pk, [ssk])
                dve(lambda e, a=ss, n=n: e.reciprocal(out=a[:, 0:n], in_=a[:, 0:n]), [ssk], [ssk])
                dve(tt(sg[:, c0:c0 + n], bufC[:, a0:a0 + n], ss[:, 0:n], ALU.mult), [ssk, "bufC"], [sgk])
            dstore((ZQ if kind == 'q' else ZK)[h], sg, [sgk], ["ZQ" if kind == 'q' else "ZK"])

        def simple_chunk(j, func, dst, dkey, lat_only, cmo):
            sg, sgk = stg.next()

            def epi(bi, c0, n, pt, pk):
                if lat_only:
                    o = sg[:, c0 - 256:c0 - 256 + n]
                    i = pt[:, 0:n]
                elif bi == 0 or not cmo:
                    o = sg[:, c0:c0 + n]
                    i = pt[:, 0:n]
                else:
                    o = cm_view(sg, (c0 - 256) // 64)
                    i = ps_rw(pt)
                act(actf(o, i, func), pk, [sgk])
            project(j, lat_only, epi)
            if lat_only:
                dstore(dst, sg[:, 0:4096], [sgk], [dkey])
            else:
                dstore(dst, sg, [sgk], [dkey])

        def f_chunk(j, d, h):
            def epi(bi, c0, n, pt, pk):
                if bi == 0:
                    o = bufA[:, c0:c0 + n]
                    i = pt[:, 0:n]
                else:
                    o = cm_view(bufA, (c0 - 256) // 64)
                    i = ps_rw(pt)
                act(actf(o, i, AF.Sigmoid), pk, ["bufA"])
            project(j, False, epi)
            return lambda: f_part2(j, d, h)

        def f_part2(j, d, h):
            c = d * 4 + h
            dve(ts(bufB[:, 0:NTOK], bufA[:, 0:NTOK], omlb[:, c:c + 1], lbc[:, c:c + 1], ALU.mult, ALU.add),
                ["bufA", "omlb", "lbc"], ["bufB"])
            act(actf(bufC[:, 0:NTOK], bufB[:, 0:NTOK], AF.Ln), ["bufB"], ["bufC"])
            dstore(ZBL[c], bufC[:, 0:NTOK], ["bufC"], ["ZBL"])
            sg, sgk = stg.next()
            pool(ts(sg, bufA[:, 0:NTOK], nomlb[:, c:c + 1], omlb[:, c:c + 1], ALU.mult, ALU.add),
                 ["bufA", "nomlb", "omlb"], [sgk])
            dstore(ZBK[c], sg, [sgk], ["ZBK"])

        def do_chunk(j):
            if j < 4:
                return conv_chunk(j, 'q', j)
            elif j < 8:
                return conv_chunk(j, 'k', j - 4)
            elif j < 12:
                return conv_chunk(j, 'v', j - 8)
            elif j < 16:
                return simple_chunk(j, AF.Silu, ZAG[j - 12], "ZAG", True, False)
            elif j < 20:
                return simple_chunk(j, AF.Silu, ZBQ[j - 16], "ZBQ", False, True)
            elif j < 24:
                return f_chunk(j, 0, j - 20)
            elif j < 28:
                return f_chunk(j, 1, j - 24)
            elif j < 32:
                return simple_chunk(j, AF.Identity, ZBI[j - 28], "ZBI", False, True)
            elif j < 36:
                return simple_chunk(j, AF.Silu, ZBGT[j - 32], "ZBGT", True, False)
            else:
                return simple_chunk(j, AF.Sigmoid, ZM[j - 36], "ZM", True, False)

        if dbg and "chunks" in dbg:
            for j in dbg["chunks"]:
                p2_ = do_chunk(j)
                if p2_:
                    p2_()
        else:
            heavy = list(range(0, 12)) + list(range(20, 28))
            simple = list(range(12, 20)) + list(range(28, 52))
            order = []
            si = 0
            for j in heavy:
                order.append(("a", j))
                for _ in range(2 if j < 12 else 1):
                    if si < len(simple):
                        order.append(("s", simple[si]))
                        si += 1
                order.append(("b", j))
            while si < len(simple):
                order.append(("s", simple[si]))
                si += 1
            projs = [j for (k, j) in order if k != "b"]
            pending2 = {}
            pi = 0
            prefetch_w(projs[0])
            prefetch_w(projs[1])
            for (k, j) in order:
                if k == "b":
                    pending2.pop(j)()
                    continue
                if pi + 2 < len(projs):
                    prefetch_w(projs[pi + 2])
                pi += 1
                r_ = do_chunk(j)
                if k == "a":
                    pending2[j] = r_
        if dbg and dbg.get("dump_gb"):
            dma(DBG[:, 0:NT * 16], GB.rearrange("p a b -> p (a b)"), GBk, ["DBG"])
            dma(DBG[:, 1024:1024 + 32], modc.rearrange("p a b -> p (a b)"), ["modc"], ["DBG"])
            dma(DBG[:, 2048:3072], gate_row, ["gate_row"], ["DBG"])
        P.barrier()
        st["off"] = persist_off


        OTa = alloc([128, 4, 4096], BF16)
        Sst = alloc([128, 8, 128], F32)
        Sb = alloc([128, 8, 128], BF16)
        ph3 = st["off"]
        r_q = Ring("gq", 4, [128, 4, 128], BF16)
        r_k = Ring("gk", 4, [128, 4, 128], BF16)
        r_v = Ring("gv", 4, [128, 4, 128], BF16)
        r_ex = Ring("gex", 6, [128, 16], F32)
        r_nb = Ring("gnb", 6, [128, 4], F32)
        r_gm = Ring("ggm", 6, [128, 8], F32)
        r_f = {n: Ring("g" + n, k, [128, 4, 128], F32) for n, k in
               (("M1", 2), ("E", 2), ("Es", 4), ("Ei", 4), ("B", 2), ("BT", 2), ("X", 4), ("P", 4), ("PT", 4),
                ("Ub", 4), ("tmp", 2), ("ot", 2))}
        r_b = {n: Ring("g" + n, k, [128, 4, 128], BF16) for n, k in
               (("at", 4), ("Xb", 2), ("Kg", 2), ("Kd", 4), ("vt", 2), ("WT", 4), ("vn", 2))}
        r_s = Ring("gss", 8, [128, 4], F32)
        r_s4 = Ring("gs4", 4, [128, 8], F32)
        r_b["on4"] = Ring("gon4", 2, [128, 4, 128], BF16)
        r_ofo = Ring("ofo", 2, [128, 512], F32)
        r_ofi = Ring("ofi", 4, [128, 512], F32)

        def finalize4(ot, otk, OTkey, dest, scale, hgj=None):
            ss, ssk = r_s4.next()
            jk, jkk = r_f["tmp"].next()
            for h in range(4):
                act(actf(jk[:, h, :], ot[:, h, :], AF.Square, accum_out=ss[:, h:h + 1]), [otk], [jkk, ssk])
            dve(ts(ss[:, 4:8], ss[:, 0:4], 1.0 / 128.0, EPS / (scale * scale), ALU.mult, ALU.add), [ssk], [ssk + "b"])
            act(actf(ss[:, 4:8], ss[:, 4:8], AF.Sqrt), [ssk + "b"], [ssk + "b"])
            dve(lambda e, a=ss: e.reciprocal(out=a[:, 4:8], in_=a[:, 4:8]), [ssk + "b"], [ssk + "b"])
            on, onk = r_b["on4"].next()
            dve(tt(on, ot, bc4(ss[:, 4:8]), ALU.mult), [otk, ssk + "b"], [onk])
            pt, pk = ps_alloc(4)
            ptb = pt.bitcast(BF16)
            pe(seq(*[tr(ptb[:, h * 128:(h + 1) * 128], on[:, h, :], identb) for h in range(4)]), [onk, "identb"], pk)
            if hgj is None:
                act(cp(dest, v4(ptb[:, 0:512])), pk, [OTkey])
            else:
                srcv = ptb[:, 0:512].rearrange("p (h a b) -> p h a b", h=4, a=2)
                dstv = OTb.rearrange("p h (r w) -> p h r w", w=64)
                for w2 in range(2):
                    act(cp(dstv[:, :, :, 2 * hgj + w2], srcv[:, :, w2, :]), pk, [OTkey])

        def finalize(ot, otk, OTkey, colap, scale, tview=None):
            ss, ssk = r_s.next()
            jk, jkk = r_f["o1"].next()
            act(actf(jk, ot, AF.Square, accum_out=ss[:, 0:1]), [otk], [jkk, ssk])
            dve(ts(ss[:, 1:2], ss[:, 0:1], scale * scale / 128.0, EPS, ALU.mult, ALU.add), [ssk], [ssk + "b"])
            act(actf(ss[:, 1:2], ss[:, 1:2], AF.Sqrt), [ssk + "b"], [ssk + "b"])
            dve(lambda e, a=ss: e.reciprocal(out=a[:, 1:2], in_=a[:, 1:2]), [ssk + "b"], [ssk + "b"])
            on, onk = r_b["on"].next()
            dve(ts(on, ot, ss[:, 1:2], scale, ALU.mult, ALU.mult), [otk, ssk + "b"], [onk])
            pt, pk = ps_alloc(1)
            pe(tr(bfv(pt), on, identb), [onk, "identb"], pk)
            if tview is None:
                act(cp(colap, bfv(pt)), pk, [OTkey])
            else:
                for w2 in range(2):
                    act(cp(colap[:, :, w2], bfv(pt)[:, w2 * 64:(w2 + 1) * 64]), pk, [OTkey])

        pool(lambda e: e.memset(Sst, 0.0), [], ["S0", "S1"])
        pool(lambda e: e.memset(Sb, 0.0), [], ["Sb0", "Sb1"])

        def bc4(col4):
            return col4.unsqueeze(2).to_broadcast([128, 4, 128])

        def mb4(m):
            return m.unsqueeze(1).to_broadcast([128, 4, 128])

        def v4(ap):
            return ap.rearrange("p (h v) -> p h v", h=4)

        predec = {}

        def gdn_decay(T, d):
            islat = T >= 2
            ML, MR, MS = (BD_le, BD_gt, BD_lt) if d == 0 else (BD_ge, BD_lt, BD_gt)
            g4 = GB[:, T, d * 4:(d + 1) * 4]
            b4 = GB[:, T, 8 + d * 4:8 + (d + 1) * 4]
            pc, pck = ps_alloc(1)
            gm, gmk = r_gm.next()
            pool(ts(gm[:, 0:4], g4, BD_le[:, 63:64], None, ALU.mult), GBk + ["masks"], [gmk])
            pool(ts(gm[:, 4:8], g4, BD_ge[:, 64:65], None, ALU.mult), GBk + ["masks"], [gmk])
            pe(seq(mm(pc[:, 0:4], ML, g4), mm(pc[:, 4:8], MR, g4),
                   mm(pc[:, 8:16], onesf, gm[:, 0:8])),
               ["masks", "onesf", gmk] + GBk, pck)
            ex, exk = r_ex.next()
            act(actf(ex, pc[:, 0:16], AF.Exp), pck, [exk])
            nb, nbk = r_nb.next()
            pool(ts(nb, b4, -1.0, None, ALU.mult), GBk, [nbk])
            M1, M1k = r_f["M1"].next()
            pool(tt(M1, mb4(MR), bc4(g4), ALU.mult), ["masks"] + GBk, [M1k])
            pD, pDk = ps_alloc(4)
            pe(seq(*[mm(pD[:, h * 128:(h + 1) * 128], M1[:, h, :], ML) for h in range(4)]), [M1k, "masks"], pDk)
            E, Ek = r_f["E"].next()
            act(actf(E, v4(pD), AF.Exp), pDk, [Ek])
            Es, Esk = r_f["Es"].next()
            pool(tt(Es, E, mb4(MS), ALU.mult), [Ek, "masks"], [Esk])
            pool(tt(Es, Es, bc4(b4), ALU.mult), [Esk] + GBk, [Esk])
            Ei = Eik = None
            if islat:
                Ei, Eik = r_f["Ei"].next()
                pool(tt(Ei, E, mb4(ML), ALU.mult), [Ek, "masks"], [Eik])
            predec[(T, d)] = dict(ML=ML, MR=MR, MS=MS, g4=g4, b4=b4, ex=ex, exk=exk, nb=nb, nbk=nbk,
                                  Es=Es, Esk=Esk, Ei=Ei, Eik=Eik)

        def gdn_prep(pairs, second, prs, nextpairs=()):
            for (T, d) in pairs:
                islat = T >= 2
                kT, kk = r_k.next()
                vT, vk = r_v.next()
                tsl = slice(T * 128, (T + 1) * 128)
                dma(kT, ZK[:, :, tsl].rearrange("h p t -> p h t"), ["ZK"], [kk])
                dma(vT, ZV[:, :, tsl].rearrange("h p t -> p h t"), ["ZV"], [vk])
                qT = qk = None
                if islat:
                    qT, qk = r_q.next()
                    dma(qT, ZQ[:, :, tsl].rearrange("h p t -> p h t"), ["ZQ"], [qk])
                ofi = ofik = None
                ofdefer = False
                if islat and second:
                    ofi, ofik = r_ofi.next()
                    ofdefer = f"OF{T - 2}" not in P.lastw
                    if not ofdefer:
                        dma(ofi, OFa[T - 2], [f"OF{T - 2}"], [ofik])
                if (T, d) not in predec:
                    gdn_decay(T, d)
                dc = predec.pop((T, d))
                ML, MR, MS, g4, b4 = dc["ML"], dc["MR"], dc["MS"], dc["g4"], dc["b4"]
                ex, exk, nb, nbk = dc["ex"], dc["exk"], dc["nb"], dc["nbk"]
                prs.append(dict(T=T, d=d, islat=islat, qT=qT, qk=qk, kT=kT, kk=kk, vT=vT, vk=vk, ex=ex, exk=exk,
                                nb=nb, nbk=nbk, ML=ML, MR=MR, MS=MS, ofi=ofi, ofik=ofik, ofdefer=ofdefer,
                                g4=g4, b4=b4, Es=dc["Es"], Esk=dc["Esk"], Ei=dc["Ei"], Eik=dc["Eik"]))
            gstage = dbg.get('gdn_stage', 99) if dbg else 99
            yield
            if gstage < 1:
                return
            for pr in prs:
                kT, kk = pr["kT"], pr["kk"]
                pKK, pKKk = ps_alloc(4)
                pe(seq(*[mm(pKK[:, h * 128:(h + 1) * 128], kT[:, h, :], kT[:, h, :]) for h in range(4)]), [kk], pKKk)
                Es, Esk = pr["Es"], pr["Esk"]
                B, Bk = r_f["B"].next()
                dve(tt(B, v4(pKK), Es, ALU.mult), pKKk + [Esk], [Bk])
                pr["B"], pr["Bk"] = B, Bk
                pr["at"] = pr["atk"] = None
                if pr["islat"]:
                    Ei, Eik = pr["Ei"], pr["Eik"]
                    pQK, pQKk = ps_alloc(4)
                    pe(seq(*[mm(pQK[:, h * 128:(h + 1) * 128], kT[:, h, :], pr["qT"][:, h, :]) for h in range(4)]),
                       [kk, pr["qk"]], pQKk)
                    at, atk = r_b["at"].next()
                    dve(tt(at, v4(pQK), Ei, ALU.mult), pQKk + [Eik], [atk])
                    pr["at"], pr["atk"] = at, atk
            yield
            if gstage < 2:
                return
            for pr in prs:
                B, Bk = pr["B"], pr["Bk"]
                pBT, pBTk = ps_alloc(4)
                pe(seq(*[tr(pBT[:, h * 128:(h + 1) * 128], B[:, h, :], identf) for h in range(4)]), [Bk, "identf"], pBTk)
                BT, BTk = r_f["BT"].next()
                act(cp(BT, v4(pBT)), pBTk, [BTk])
                X, Xk = r_f["X"].next()
                pool(tt(X, mb4(identf), B, ALU.subtract), [Bk, "identf"], [Xk])
                pr.update(P=B, Pk=Bk, PT=BT, PTk=BTk, X=X, Xk=Xk)
            yield
            if gstage < 3:
                return
            NL = dbg.get('gdn_levels', 5) if dbg else 5

            def sq(pr, only_t):
                Pm, Pk, PTm, PTk = pr["P"], pr["Pk"], pr["PT"], pr["PTk"]
                pr["p2t"], pr["p2tk"] = ps_alloc(4)
                pe(seq(*[mm(pr["p2t"][:, h * 128:(h + 1) * 128], Pm[:, h, :], PTm[:, h, :]) for h in range(4)]),
                   [Pk, PTk], pr["p2tk"])
                pr["p2"] = None
                if not only_t:
                    pr["p2"], pr["p2k"] = ps_alloc(4)
                    pe(seq(*[mm(pr["p2"][:, h * 128:(h + 1) * 128], PTm[:, h, :], Pm[:, h, :]) for h in range(4)]),
                       [Pk, PTk], pr["p2k"])

            def cps(pr):
                nPT, nPTk = r_f["PT"].next()
                act(cp(nPT, v4(pr["p2t"])), pr["p2tk"], [nPTk])
                pr["PT"], pr["PTk"] = nPT, nPTk
                if pr["p2"] is not None:
                    nP, nPk = r_f["P"].next()
                    dve(cp(nP, v4(pr["p2"])), pr["p2k"], [nPk])
                    pr["P"], pr["Pk"] = nP, nPk

            for pr in prs:
                sq(pr, NL == 1)
            yield
            for pr in prs:
                cps(pr)
            yield
            for lvl in range(NL):
                last = lvl == NL - 1
                for pr in prs:
                    px, pxk = ps_alloc(4)
                    pe(seq(*[mm(px[:, h * 128:(h + 1) * 128], pr["PT"][:, h, :], pr["X"][:, h, :]) for h in range(4)]),
                       [pr["PTk"], pr["Xk"]], pxk)
                    if not last:
                        sq(pr, lvl + 1 == NL - 1)
                        nX, nXk = r_f["X"].next()
                        dve(tt(nX, v4(px), pr["X"], ALU.add), pxk + [pr["Xk"]], [nXk])
                        pr["X"], pr["Xk"] = nX, nXk
                        cps(pr)
                    else:
                        Xb, Xbk = r_b["Xb"].next()
                        dve(tt(Xb, v4(px), pr["X"], ALU.add), pxk + [pr["Xk"]], [Xbk])
                        pr["Xb"], pr["Xbk"] = Xb, Xbk
                    yield
            if gstage < 4:
                return
            for pr in prs:
                ex, exk = pr["ex"], pr["exk"]
                pkt, pktk = ps_alloc(4)
                pktb = pkt.bitcast(BF16)
                pe(seq(*[tr(pktb[:, h * 128:(h + 1) * 128], pr["kT"][:, h, :], identb) for h in range(4)]),
                   [pr["kk"], "identb"], pktk)
                pvt, pvtk = ps_alloc(4)
                pvtb = pvt.bitcast(BF16)
                pe(seq(*[tr(pvtb[:, h * 128:(h + 1) * 128], pr["vT"][:, h, :], identb) for h in range(4)]),
                   [pr["vk"], "identb"], pvtk)
                Kg, Kgk = r_b["Kg"].next()
                dve(tt(Kg, v4(pktb[:, 0:512]), bc4(ex[:, 0:4]), ALU.mult), pktk + [exk], [Kgk])
                Kd, Kdk = r_b["Kd"].next()
                dve(tt(Kd, v4(pktb[:, 0:512]), bc4(ex[:, 4:8]), ALU.mult), pktk + [exk], [Kdk])
                vt, vtk = r_b["vt"].next()
                act(cp(vt, v4(pvtb[:, 0:512])), pvtk, [vtk])
                pU, pUk = ps_alloc(4)
                pe(seq(*[mm(pU[:, h * 128:(h + 1) * 128], pr["Xb"][:, h, :], vt[:, h, :]) for h in range(4)]),
                   [pr["Xbk"], vtk], pUk)
                Ub, Ubk = r_f["Ub"].next()
                dve(tt(Ub, v4(pU), bc4(pr["b4"]), ALU.mult), pUk + GBk, [Ubk])
                pW, pWk = ps_alloc(4)
                pe(seq(*[mm(pW[:, h * 128:(h + 1) * 128], Kg[:, h, :], pr["Xb"][:, h, :]) for h in range(4)]),
                   [Kgk, pr["Xbk"]], pWk)
                WT, WTk = r_b["WT"].next()
                act(cp(WT, v4(pW)), pWk, [WTk])
                pr.update(WT=WT, WTk=WTk, Ub=Ub, Ubk=Ubk, Kd=Kd, Kdk=Kdk)
            yield
            for (T2, d2) in nextpairs:
                gdn_decay(T2, d2)
                yield

        def gdn_chain(preps, second):
            for pr in preps:
                if pr["ofdefer"]:
                    dma(pr["ofi"], OFa[pr["T"] - 2], [f"OF{pr['T'] - 2}"], [pr["ofik"]])
            for pr in preps:
                pr["vn"], pr["vnk"] = r_b["vn"].next()
                if pr["islat"]:
                    pr["pO1"], pr["pO1k"] = PS[6 + pr["d"]][:, :], [f"psb{6 + pr['d']}"]
            for sub in range(2):
                def rs(pr):
                    j = sub if pr["d"] == 0 else 1 - sub
                    return j, slice(64 * j, 64 * j + 64)
                for pr in preps:
                    d = pr["d"]
                    j, r = rs(pr)
                    pP, pPk = ps_alloc(4)
                    pe(seq(*[mm(pP[r, h * 128:(h + 1) * 128], pr["WT"][:, h, r], Sb[:, d * 4 + h, :]) for h in range(4)]),
                       [pr["WTk"], f"Sb{d}"], pPk)
                    tmp, tmpk = r_f["tmp"].next()
                    dve(tt(tmp[r, :, :], v4(pP)[r, :, :], pr["nb"][r, :].unsqueeze(2).to_broadcast([64, 4, 128]), ALU.mult),
                        pPk + [pr["nbk"]], [tmpk])
                    pool(tt(pr["vn"][r, :, :], tmp[r, :, :], pr["Ub"][r, :, :], ALU.add), [tmpk, pr["Ubk"]], [pr["vnk"]])
                yield
                for pr in preps:
                    d = pr["d"]
                    j, r = rs(pr)
                    if pr["islat"]:
                        pe(seq(*[mm(pr["pO1"][r, h * 128:(h + 1) * 128], pr["qT"][:, h, r], Sb[:, d * 4 + h, :]) for h in range(4)]),
                           [pr["qk"], f"Sb{d}"], pr["pO1k"])
                    pS, pSk = ps_alloc(4)
                    pe(seq(*[mm(pS[:, h * 128:(h + 1) * 128], pr["Kd"][r, h, :], pr["vn"][r, h, :]) for h in range(4)]),
                       [pr["Kdk"], pr["vnk"]], pSk)
                    pr["pS"], pr["pSk"] = pS, pSk
                    S4 = Sst[:, d * 4:(d + 1) * 4, :]
                    pool(tt(S4, S4, bc4(pr["ex"][:, 8 + 4 * j:12 + 4 * j]), ALU.mult), [pr["exk"], f"S{d}"], [f"S{d}"])
                yield
                for pr in preps:
                    d = pr["d"]
                    S4 = Sst[:, d * 4:(d + 1) * 4, :]
                    dve(tt(S4, S4, v4(pr["pS"]), ALU.add), pr["pSk"] + [f"S{d}"], [f"S{d}"])
                    act(cp(Sb[:, d * 4:(d + 1) * 4, :], S4), [f"S{d}"], [f"Sb{d}"])
                yield
            for pr in preps:
                if not pr["islat"]:
                    continue
                lt = pr["T"] - 2
                pO2, pO2k = ps_alloc(4)
                pe(seq(*[mm(pO2[:, h * 128:(h + 1) * 128], pr["at"][:, h, :], pr["vn"][:, h, :]) for h in range(4)]),
                   [pr["atk"], pr["vnk"]], pO2k)
                tmp, tmpk = r_f["tmp"].next()
                dve(tt(tmp, v4(pr["pO1"]), bc4(pr["ex"][:, 0:4]), ALU.mult), pr["pO1k"] + [pr["exk"]], [tmpk])
                if not second:
                    ofo, ofok = r_ofo.next()
                    dve(tt(ofo, pO2, tmp.rearrange("p h v -> p (h v)"), ALU.add), pO2k + [tmpk], [ofok])
                    dstore(OFa[lt], ofo, [ofok], [f"OF{lt}"])
                else:
                    ot, otk = r_f["ot"].next()
                    dve(tt(ot, v4(pO2), tmp, ALU.add), pO2k + [tmpk], [otk])
                    pool(tt(ot, ot, v4(pr["ofi"]), ALU.add), [otk, pr["ofik"]], [otk])
                    finalize4(ot, otk, "OTa", OTa[:, :, lt * 128:(lt + 1) * 128], QSCALE)

            yield

        bwd_order = [1, 0] + list(range(33, 1, -1))
        nsteps = dbg.get("gdn_steps", NT) if dbg else NT
        def run_gen(g):
            for _ in g:
                pass

        def interleave(ga, gb, ratio=3):
            alive_a, alive_b = ga is not None, gb is not None
            while alive_a or alive_b:
                for _ in range(ratio):
                    if alive_a:
                        psst["grp"] = 'p'
                        try:
                            next(ga)
                        except StopIteration:
                            alive_a = False
                if alive_b:
                    psst["grp"] = 'c'
                    try:
                        next(gb)
                    except StopIteration:
                        alive_b = False
            psst["grp"] = None

        if not (dbg and dbg.get("skip_gdn")):
            cur = []
            psst["grp"] = 'p'
            run_gen(gdn_prep([(0, 0), (bwd_order[0], 1)], False, cur, [(1, 0), (bwd_order[1], 1)] if nsteps > 1 else []))
            psst["grp"] = None
            for i in range(nsteps):
                nxt = []
                np_ = [(i + 2, 0), (bwd_order[i + 2], 1)] if i + 2 < nsteps else []
                gp = gdn_prep([(i + 1, 0), (bwd_order[i + 1], 1)], (i + 1) >= 18, nxt, np_) if i + 1 < nsteps else None
                gc = gdn_chain(cur, second=(i >= 18)) if not (dbg and dbg.get('gdn_stage', 99) < 5) else None
                interleave(gp, gc, ratio=(dbg.get('gratio', 3) if dbg else 3))
                cur = nxt
        if dbg and dbg.get("dump_ota"):
            dma(dbg_ota, OTa, ["OTa"], ["dbg_ota"])
            dma(DBG[:, 4096:4096 + 1024], Sst.rearrange("p a b -> p (a b)"), ["S0", "S1"], ["DBG"])
        P.barrier()
        st["off"] = ph3
        OTb = alloc([128, 4, 4096], BF16)
        ph3b = st["off"]

        h_q = Ring("hq", 4, [128, 4, 128], BF16)
        h_k = Ring("hk", 4, [128, 4, 128], BF16)
        h_i = Ring("hi", 4, [128, 4, 128], BF16)
        h_g = Ring("hg", 4, [128, 512], F32)
        h_w = {n: Ring("h" + n, 3, [128, 512], F32) for n in ("G", "eq", "ek", "ed", "Gx", "enx")}
        h_qg = Ring("hqg", 4, [128, 4, 128], BF16)
        h_kg = Ring("hkg", 3, [128, 4, 128], BF16)
        h_kd = Ring("hkd", 3, [128, 4, 128], BF16)
        h_gl = Ring("hgl", 4, [128, 16], F32)
        h_at = Ring("hat", 4, [128, 4, 128], BF16)
        h_kt = Ring("hkt", 4, [128, 4, 128], BF16)
        h_vt = Ring("hvt", 4, [128, 4, 128], BF16)
        r_s = Ring("hss", 8, [128, 4], F32)
        r_f = {"ot": Ring("hot", 2, [128, 4, 128], F32), "tmp": Ring("htmp", 2, [128, 4, 128], F32)}
        r_b = {"on4": Ring("hon4", 2, [128, 4, 128], BF16)}
        r_s4 = Ring("hs4", 4, [128, 8], F32)
        r_ofo = Ring("hofo", 2, [128, 512], F32)
        r_ofi = Ring("hofi", 6, [128, 512], F32)
        pool(lambda e: e.memset(Sst, 0.0), ["S0", "S1"], ["S0", "S1"])
        pool(lambda e: e.memset(Sb, 0.0), ["Sb0", "Sb1"], ["Sb0", "Sb1"])

        def bc8(ap8):
            return ap8.unsqueeze(2).to_broadcast([128, 8, 64])

        def v864(ap):
            return ap.rearrange("p (a b) -> p a b", b=64)

        def hg_prep(pairs, second, prs):
            for (T, d) in pairs:
                islat = T >= 2
                tsl = slice(T * 128, (T + 1) * 128)
                qT, qk = h_q.next()
                kT, kk = h_k.next()
                iT, ik = h_i.next()
                gT, gk = h_g.next()
                dma(qT, ZBQ[:, :, tsl].rearrange("h p t -> p h t"), ["ZBQ"], [qk])
                dma(kT, ZBK[d * 4:(d + 1) * 4, :, tsl].rearrange("h p t -> p h t"), ["ZBK"], [kk])
                dma(iT, ZBI[:, :, tsl].rearrange("h p t -> p h t"), ["ZBI"], [ik])
                dma(gT.rearrange("p (h t) -> p h t", h=4), ZBL[d * 4:(d + 1) * 4, :, tsl].rearrange("h p t -> p h t"), ["ZBL"], [gk])
                ofi = ofik = None
                ofdefer = False
                if islat and second:
                    ofi, ofik = r_ofi.next()
                    ofdefer = f"OFb{T - 2}" not in P.lastw
                    if not ofdefer:
                        dma(ofi, OFb[T - 2], [f"OFb{T - 2}"], [ofik])
                G, Gk = h_w["G"].next()
                dve(lambda e, a=G, b=gT: e.tensor_tensor_scan(out=a, data0=segm, data1=b, initial=0.0, op0=ALU.mult, op1=ALU.add),
                    [gk, "segm"], [Gk])
                gl, glk = h_gl.next()
                Glast = v864(G)[:, :, 63]
                eq, eqk = h_w["eq"].next()
                ek, ekk = h_w["ek"].next()
                ed, edk = h_w["ed"].next()
                act(actf(gl[:, 0:8], Glast, AF.Exp), [Gk], [glk])
                if d == 0:
                    act(actf(eq, G, AF.Exp), [Gk], [eqk])
                    act(actf(ek, G, AF.Exp, scale=-1.0), [Gk], [ekk])
                    dve(tt(v864(ed), v864(ek), bc8(gl[:, 0:8]), ALU.mult), [ekk, glk], [edk])
                else:
                    Gx, Gxk = h_w["Gx"].next()
                    enx, enxk = h_w["enx"].next()
                    act(actf(gl[:, 8:16], Glast, AF.Exp, scale=-1.0), [Gk], [glk])
                    pool(tt(Gx, G, gT, ALU.subtract), [Gk, gk], [Gxk])
                    act(actf(ed, Gx, AF.Exp), [Gxk], [edk])
                    act(actf(enx, Gx, AF.Exp, scale=-1.0), [Gxk], [enxk])
                    dve(tt(v864(eq), v864(enx), bc8(gl[:, 0:8]), ALU.mult), [enxk, glk], [eqk])
                    pool(tt(v864(ek), v864(ed), bc8(gl[:, 8:16]), ALU.mult), [edk, glk], [ekk])
                yield
                qg, qgk = h_qg.next()
                kg, kgk = h_kg.next()
                kd, kdk = h_kd.next()
                f2 = lambda a: a.rearrange("p h t -> p (h t)")
                dve(tt(f2(qg), f2(qT), eq, ALU.mult), [qk, eqk], [qgk])
                pool(tt(f2(kg), f2(kT), ek, ALU.mult), [kk, ekk], [kgk])
                pool(tt(f2(kd), f2(kT), ed, ALU.mult), [kk, edk], [kdk])
                yield
                MI = BD_le if d == 0 else BD_ge
                heads = []
                at4 = at4k = None
                if islat:
                    pA, pAk = ps_alloc(4)
                    pe(seq(*[mm(pA[:, h * 128:(h + 1) * 128], kg[:, h, :], qg[:, h, :]) for h in range(4)]), [kgk, qgk], pAk)
                    at4, at4k = h_at.next()
                    dve(tt(at4, v4(pA), mb4(MI), ALU.mult), pAk + ["masks"], [at4k])
                pk_, pkk = ps_alloc(4)
                pkb = pk_.bitcast(BF16)
                pe(seq(*[tr(pkb[:, h * 128:(h + 1) * 128], kd[:, h, :], identb) for h in range(4)]), [kdk, "identb"], pkk)
                kt4, kt4k = h_kt.next()
                act(cp(kt4, v4(pkb[:, 0:512])), pkk, [kt4k])
                pv_, pvk = ps_alloc(4)
                pvb = pv_.bitcast(BF16)
                pe(seq(*[tr(pvb[:, h * 128:(h + 1) * 128], iT[:, h, :], identb) for h in range(4)]), [ik, "identb"], pvk)
                vt4, vt4k = h_vt.next()
                act(cp(vt4, v4(pvb[:, 0:512])), pvk, [vt4k])
                for h in range(4):
                    heads.append(dict(h=h, c=d * 4 + h, at=(at4[:, h, :] if islat else None), atk=at4k,
                                      kt=kt4[:, h, :], ktk=kt4k, vt=vt4[:, h, :], vtk=vt4k))
                prs.append(dict(T=T, d=d, islat=islat, qg=qg, qgk=qgk, gl=gl, glk=glk, ofi=ofi, ofik=ofik,
                                ofdefer=ofdefer, heads=heads))
            yield

        def otb_view(j):
            def colap(h):
                return OTb[:, h, :].rearrange("p (r w) -> p r w", w=64)[:, :, 2 * j:2 * j + 2]
            return colap

        def hg_chain(preps, second):
            for pr in preps:
                if pr["islat"]:
                    pr["pO"], pr["pOk"] = PS[6 + pr["d"]][:, :], [f"psb{6 + pr['d']}"]
            for sub in range(2):
                def rs(pr):
                    j = sub if pr["d"] == 0 else 1 - sub
                    return j, slice(64 * j, 64 * j + 64)
                for pr in preps:
                    d = pr["d"]
                    j, r = rs(pr)
                    hs = pr["heads"]
                    if pr["islat"]:
                        ops = []
                        for hd in hs:
                            h = hd["h"]
                            ops.append(mm(pr["pO"][r, h * 128:(h + 1) * 128], pr["qg"][:, h, r], Sb[:, d * 4 + h, :], start=True, stop=False))
                            ops.append(mm(pr["pO"][r, h * 128:(h + 1) * 128], hd["at"][r, r], hd["vt"][r, :], start=False, stop=True))
                        pe(seq(*ops), [pr["qgk"], f"Sb{d}"] + [hd["atk"] for hd in hs] + [hd["vtk"] for hd in hs], pr["pOk"])
                    pS, pSk = ps_alloc(4)
                    pe(seq(*[mm(pS[:, hd["h"] * 128:(hd["h"] + 1) * 128], hd["kt"][r, :], hd["vt"][r, :]) for hd in hs]),
                       [hd["ktk"] for hd in hs] + [hd["vtk"] for hd in hs], pSk)
                    pr["pS"], pr["pSk"] = pS, pSk
                    S4 = Sst[:, d * 4:(d + 1) * 4, :]
                    glj = pr["gl"][:, 0:8].rearrange("p (h j) -> p h j", j=2)[:, :, j]
                    pool(tt(S4, S4, bc4(glj), ALU.mult), [pr["glk"], f"S{d}"], [f"S{d}"])
                yield
                for pr in preps:
                    d = pr["d"]
                    S4 = Sst[:, d * 4:(d + 1) * 4, :]
                    dve(tt(S4, S4, v4(pr["pS"]), ALU.add), pr["pSk"] + [f"S{d}"], [f"S{d}"])
                    act(cp(Sb[:, d * 4:(d + 1) * 4, :], S4), [f"S{d}"], [f"Sb{d}"])
                yield
            for pr in preps:
                if pr["ofdefer"]:
                    assert f"OFb{pr['T'] - 2}" in P.lastw
                    dma(pr["ofi"], OFb[pr["T"] - 2], [f"OFb{pr['T'] - 2}"], [pr["ofik"]])
            for pr in preps:
                if not pr["islat"]:
                    continue
                j = pr["T"] - 2
                if not second:
                    ofo, ofok = r_ofo.next()
                    act(cp(ofo, pr["pO"]), pr["pOk"], [ofok])
                    dstore(OFb[j], ofo, [ofok], [f"OFb{j}"])
                else:
                    ot, otk = r_f["ot"].next()
                    dve(tt(ot, v4(pr["pO"]), v4(pr["ofi"]), ALU.add), pr["pOk"] + [pr["ofik"]], [otk])
                    finalize4(ot, otk, "OTb", None, QSCALE, hgj=j)
            yield

        hsteps = dbg.get("hg_steps", NT) if dbg else NT
        if not (dbg and dbg.get("skip_hg")):
            cur = []
            psst["grp"] = 'p'
            run_gen(hg_prep([(0, 0), (bwd_order[0], 1)], False, cur))
            psst["grp"] = None
            for i in range(hsteps):
                nxt = []
                gp = hg_prep([(i + 1, 0), (bwd_order[i + 1], 1)], (i + 1) >= 18, nxt) if i + 1 < hsteps else None
                gc = hg_chain(cur, second=(i >= 18))
                interleave(gp, gc, ratio=(dbg.get('hratio', 1) if dbg else 1))
                cur = nxt
        if dbg and dbg.get("dump_otb"):
            dma(dbg_otb, OTb, ["OTb"], ["dbg_otb"])
        P.barrier()
        st["off"] = ph3b

        waoB = alloc([128, 4, 1024], BF16)
        wboB = alloc([128, 4, 1024], BF16)
        woB = alloc([128, 8, 1024], BF16)
        lng_row = alloc([128, 1024], F32)
        lnb_row = alloc([128, 1024], F32)
        Hr = Ring("oH", 4, [128, 1024], F32)
        xr4 = Ring("ox", 2, [128, 1024], F32)
        agr = Ring("oag", 1, [128, 4, 512], BF16)
        bgr = Ring("obg", 1, [128, 4, 512], BF16)
        mr = Ring("om", 4, [128, 2, 512], BF16)
        mar = Ring("oma", 1, [128, 4, 512], BF16)
        mbr = Ring("omb", 1, [128, 4, 512], BF16)
        t1r = Ring("ot1", 2, [128, 512], F32)
        t2r = Ring("ot2", 2, [128, 512], F32)
        mixr = Ring("omix", 1, [128, 8, 512], BF16)
        st4 = Ring("ost", 3, [128, 2, 6], F32)
        mv4 = Ring("omv", 4, [128, 4], F32)
        dma(lng_row, lng.partition_broadcast(128), [], ["lng_row"])
        dma(lnb_row, lnb.partition_broadcast(128), [], ["lnb_row"])
        for (wsrc, wdst, n, key) in ((wao, waoB, 4, "waoB"), (wbo, wboB, 4, "wboB"), (wo, woB, 8, "woB")):
            for i in range(n):
                hbuf, hk = Hr.next()
                dma(hbuf, wsrc[:, i, :], [], [hk])
                pool(cp(wdst[:, i, :], hbuf), [hk], [key])
        nblk = dbg.get("out_blocks", 8) if dbg else 8
        p4pend = []
        for bb in range(nblk):
            bsl = slice(bb * 512, (bb + 1) * 512)
            ag, agk = agr.next()
            bg, bgk = bgr.next()
            dma(ag, ZAG[:, :, bsl].rearrange("h p t -> p h t"), ["ZAG"], [agk])
            dma(bg, ZBGT[:, :, bsl].rearrange("h p t -> p h t"), ["ZBGT"], [bgk])
            ma, mak = mar.next()
            mb, mbk = mbr.next()
            dve(stt(ma, OTa[:, :, bsl], gains[:, 0:1], ag, ALU.mult, ALU.mult), ["OTa", "gains", agk], [mak])
            dve(stt(mb, OTb[:, :, bsl], gains[:, 1:2], bg, ALU.mult, ALU.mult), ["OTb", "gains", bgk], [mbk])
            mix, mixk = mixr.next()
            for cc in range(8):
                mt, mtk = mr.next()
                dma(mt[:, 0, :], ZM[cc, :, bsl], ["ZM"], [mtk])
                dma(mt[:, 1, :], ZM[8 + cc, :, bsl], ["ZM"], [mtk])
                pYa, pYak = ps_alloc(4)
                pe(seq(*[mm(pYa, waoB[:, h, cc * 128:(cc + 1) * 128], ma[:, h, :], start=(h == 0), stop=(h == 3))
                         for h in range(4)]), ["waoB", mak], pYak)
                pYb, pYbk = ps_alloc(4)
                pe(seq(*[mm(pYb, wboB[:, h, cc * 128:(cc + 1) * 128], mb[:, h, :], start=(h == 0), stop=(h == 3))
                         for h in range(4)]), ["wboB", mbk], pYbk)
                t1, t1k = t1r.next()
                t2, t2k = t2r.next()
                dve(tt(t1, pYa, mt[:, 0, :], ALU.mult), pYak + [mtk], [t1k])
                dve(tt(t2, pYb, mt[:, 1, :], ALU.mult), pYbk + [mtk], [t2k])
                pool(tt(mix[:, cc, :], t1, t2, ALU.add), [t1k, t2k], [mixk])
            for ti in range(4):
                lt = bb * 4 + ti
                xt, xk = xr4.next()
                dma(xt, x[lt * 128:(lt + 1) * 128, :], [], [xk])
                H, Hk = Hr.next()
                for half in range(2):
                    pSu, pSuk = ps_alloc(4)
                    pe(seq(*[mm(pSu, mix[:, cc, ti * 128:(ti + 1) * 128], woB[:, cc, half * 512:(half + 1) * 512],
                                start=(cc == 0), stop=(cc == 7)) for cc in range(8)]), [mixk, "woB"], pSuk)
                    dve(tt(H[:, half * 512:(half + 1) * 512], pSu, gate_row[:, half * 512:(half + 1) * 512], ALU.mult),
                        pSuk + ["gate_row"], [Hk])
                dve(stt(H, xt, ALPHA, H, ALU.mult, ALU.add), [Hk, xk], [Hk])
                stt_, stk = st4.next()
                mv, mvk = mv4.next()
                dve(seq(lambda e, a=stt_, b=H: e.bn_stats(out=a[:, 0, :], in_=b[:, 0:512]),
                        lambda e, a=stt_, b=H: e.bn_stats(out=a[:, 1, :], in_=b[:, 512:1024])), [Hk], [stk])
                dve(lambda e, a=mv, b=stt_: e.bn_aggr(out=a[:, 0:2], in_=b.rearrange("p a b -> p (a b)")), [stk], [mvk])
                act(actf(mv[:, 2:3], mv[:, 1:2], AF.Sqrt, bias=EPS), [mvk], [mvk + "s"])
                if p4pend:
                    p4pend.pop()()

                def late(H=H, Hk=Hk, mv=mv, mvk=mvk, lt=lt):
                    dve(lambda e, a=mv: e.reciprocal(out=a[:, 2:3], in_=a[:, 2:3]), [mvk + "s"], [mvk + "s"])
                    dve(stt(mv[:, 3:4], mv[:, 0:1], -1.0, mv[:, 2:3], ALU.mult, ALU.mult), [mvk, mvk + "s"], [mvk + "n"])
                    act(actf(H, H, AF.Identity, bias=mv[:, 3:4], scale=mv[:, 2:3]), [Hk, mvk + "s", mvk + "n"], [Hk])
                    pool(tt(H, H, lng_row, ALU.mult), [Hk, "lng_row"], [Hk])
                    pool(tt(H, H, lnb_row, ALU.add), [Hk, "lnb_row"], [Hk])
                    dstore(y[lt * 128:(lt + 1) * 128, :], H, [Hk], ["y"])
                p4pend.append(late)
        while p4pend:
            p4pend.pop()()

        final_cnt = dict(P.cnt)

        @block.sync
        def _(e):
            P.replay('sp', e, sems)
            for l, n in final_cnt.items():
                if l[0] == 'd' and l != 'dve':
                    e.wait_ge(sems[l], 16 * n)

        @block.tensor
        def _(e):
            P.replay('pe', e, sems)

        @block.scalar
        def _(e):
            P.replay('act', e, sems)

        @block.vector
        def _(e):
            P.replay('dve', e, sems)

        @block.gpsimd
        def _(e):
            P.replay('pool', e, sems)
    return nc, P


def _consts():
    p = np.arange(128)[:, None]
    f = np.arange(128)[None, :]
    same = (p // 64) == (f // 64)
    m = np.stack([p <= f, p < f, p >= f, p > f, (p <= f) & same, (p < f) & same, (p >= f) & same, (p > f) & same], axis=1).astype(np.float32)
    segm = np.ones((128, 512), np.float32)
    segm[:, ::64] = 0.0
    return np.eye(128, dtype=np.float32), np.ascontiguousarray(m), segm


def make_in_maps(inp):
    f = np.float32
    A = lambda a: np.ascontiguousarray(a, dtype=f)
    w_in = inp['w_in'][0]
    cols = np.r_[0:1536, 1552:6672]
    win = A(w_in[:, cols].reshape(8, 128, 52, 128).transpose(2, 1, 0, 3))
    wab = A(w_in[:, 1536:1552].reshape(8, 128, 16).transpose(1, 0, 2))
    w_mod = inp['w_mod'][0]
    wmod = A(w_mod.reshape(8, 128, 6, 512).transpose(2, 1, 0, 3))
    b_mod = inp['b_mod'][0]
    bmodc = A(b_mod[:2048].reshape(16, 128).T)
    bmodg = A(b_mod[2048:].reshape(1, 1024))
    convw = A(inp['conv_w'][0].reshape(5, 12, 128).transpose(2, 1, 0))
    alog = A(inp['a_log'][0].reshape(1, 8))
    dtb = A(inp['dt_bias'][0].reshape(1, 8))
    lbp = A(inp['lb_param'].reshape(2, 2, 4, 128).transpose(3, 0, 1, 2).reshape(128, 2, 8))
    ang = A(inp['a_norm_g'][0].reshape(128, 1))
    bng = A(inp['b_norm_g'][0].reshape(128, 1))
    wao = A(inp['w_a_out'][0].reshape(4, 128, 1024).transpose(1, 0, 2))
    wbo = A(inp['w_b_out'][0].reshape(4, 128, 1024).transpose(1, 0, 2))
    wo = A(inp['w_out'][0].reshape(8, 128, 1024).transpose(1, 0, 2))
    lng = A(inp['ln_g'][0].reshape(1, 1024))
    lnb = A(inp['ln_b'][0].reshape(1, 1024))
    cidf, cmask, csegm = _consts()
    maps = []
    for b in range(8):
        ccv = np.stack([inp['c'][b].reshape(8, 128).T, inp['c_ctx'].reshape(8, 128).T], axis=2)
        maps.append(dict(x=A(inp['x'][b]), ctx=A(inp['ctx'][b]), cc=A(ccv), wmod=wmod, bmodc=bmodc, bmodg=bmodg,
                         win=win, wab=wab, convw=convw, alog=alog, dtb=dtb, lbp=lbp, ang=ang, bng=bng,
                         wao=wao, wbo=wbo, wo=wo, lng=lng, lnb=lnb, cidf=cidf, cmask=cmask, csegm=csegm))
    return maps


def kernel(**inputs):
    nc, _ = build()
    maps = make_in_maps(inputs)
    res = run_bass_kernel_spmd(nc, maps, core_ids=list(range(8)))
    return np.stack([np.asarray(r["y"], dtype=np.float32) for r in res.results], axis=0)
```
